# Optimizing a Trainium2 kernel written in Bass

```python
import math
import jax, jax.numpy as jnp
from jax import lax
import numpy as np

D_MODEL = 2048
BATCH = 2
SEQ = 4096
DEPTH = 1

N_META = 16
GDN_HEADS = 16
GDN_DK = 128
GDN_DV = 128
CONV_K = 4
CHUNK = 64
FOX_HEADS = 16
FOX_DH = 128
Q_BLOCK = 128
D_FF = 4 * D_MODEL
NORM_EPS = 1e-6

GDN_QK = GDN_HEADS * GDN_DK
GDN_V = GDN_HEADS * GDN_DV
GDN_CONV_DIM = 2 * GDN_QK + GDN_V
FOX_W = FOX_HEADS * FOX_DH
IN_SPLITS = (GDN_CONV_DIM, GDN_V, GDN_HEADS, GDN_HEADS, 3 * FOX_W, FOX_HEADS, D_MODEL, D_MODEL)
N_IN = sum(IN_SPLITS)
IN_SPLIT_IDX = tuple(int(i) for i in np.cumsum(IN_SPLITS)[:-1])

kernel_name = "gdn_fox_gated_hybrid_block"


def rms_norm(x, g):
    xf = x.astype(jnp.float32)
    y = xf * lax.rsqrt(jnp.mean(jnp.square(xf), axis=-1, keepdims=True) + NORM_EPS)
    return (y * g.astype(jnp.float32)).astype(x.dtype)


def l2_normalize(x):
    return x * lax.rsqrt(jnp.sum(jnp.square(x), axis=-1, keepdims=True) + NORM_EPS)


def causal_depthwise_conv(x, w):
    L = x.shape[1]
    xp = jnp.pad(x, ((0, 0), (CONV_K - 1, 0), (0, 0)))
    return sum(xp[:, k:k + L] * w[k] for k in range(CONV_K))


def chunked_gated_delta_rule(q, k, v, g, beta):
    B, H, L, DK = q.shape
    DV = v.shape[-1]
    N = L // CHUNK
    rs = lambda t: t.reshape(B, H, N, CHUNK, *t.shape[3:])
    q, k, v, g, beta = rs(q) * DK ** -0.5, rs(k), rs(v), rs(g), rs(beta)
    g = jnp.cumsum(g, axis=-1)
    causal = jnp.tril(jnp.ones((CHUNK, CHUNK), bool))
    strict = jnp.tril(jnp.ones((CHUNK, CHUNK), bool), -1)
    decay = jnp.exp(jnp.where(causal, g[..., :, None] - g[..., None, :], -jnp.inf))
    k_beta = k * beta[..., None]
    a_mat = jnp.where(strict, jnp.einsum('bhncd,bhnsd->bhncs', k_beta, k) * decay, 0.0)
    eye = jnp.eye(CHUNK, dtype=q.dtype)
    rhs = jnp.concatenate([v * beta[..., None], k_beta * jnp.exp(g)[..., None]], axis=-1)
    sol = lax.linalg.triangular_solve(a_mat + eye, rhs, left_side=True, lower=True, unit_diagonal=True)
    w_val, w_key = sol[..., :DV], sol[..., DV:]
    qk_intra = jnp.where(causal, jnp.einsum('bhncd,bhnsd->bhncs', q, k) * decay, 0.0)
    q_dec = q * jnp.exp(g)[..., None]
    k_dec = k * jnp.exp(g[..., -1:] - g)[..., None]
    g_last = jnp.exp(g[..., -1])

    def step(S, inp):
        qk_c, q_dec_c, k_dec_c, w_val_c, w_key_c, g_last_c = inp
        v_new = w_val_c - jnp.einsum('bhck,bhkv->bhcv', w_key_c, S)
        o = jnp.einsum('bhck,bhkv->bhcv', q_dec_c, S) + jnp.einsum('bhcs,bhsv->bhcv', qk_c, v_new)
        S = S * g_last_c[..., None, None] + jnp.einsum('bhck,bhcv->bhkv', k_dec_c, v_new)
        return S, o

    xs = tuple(jnp.moveaxis(t, 2, 0) for t in (qk_intra, q_dec, k_dec, w_val, w_key, g_last))
    S0 = jnp.zeros((B, H, DK, DV), q.dtype)
    _, o = lax.scan(step, S0, xs)
    return jnp.moveaxis(o, 0, 2).reshape(B, H, L, DV)


def gated_deltanet_branch(qkv, z, b, a, conv_w, a_log, dt_bias, norm_g):
    B, L, _ = qkv.shape
    out_dtype = qkv.dtype
    f32 = jnp.float32
    pad = (-L) % CHUNK
    Lp = L + pad
    front = lambda t: jnp.pad(t, ((0, 0), (pad, 0)) + ((0, 0),) * (t.ndim - 2))
    qkv = jax.nn.silu(causal_depthwise_conv(front(qkv.astype(f32)), conv_w.astype(f32)))
    q, k, v = jnp.split(qkv, [GDN_QK, 2 * GDN_QK], axis=-1)
    q = l2_normalize(q.reshape(B, Lp, GDN_HEADS, GDN_DK))
    k = l2_normalize(k.reshape(B, Lp, GDN_HEADS, GDN_DK))
    v = v.reshape(B, Lp, GDN_HEADS, GDN_DV)
    beta = front(jax.nn.sigmoid(b.astype(f32)))
    g = front(-jnp.exp(a_log.astype(f32)) * jax.nn.softplus(a.astype(f32) + dt_bias.astype(f32)))
    bhl = lambda t: jnp.moveaxis(t, 2, 1)
    o = chunked_gated_delta_rule(bhl(q), bhl(k), bhl(v), bhl(g), bhl(beta))
    o = jnp.moveaxis(o, 1, 2)[:, pad:]
    o = rms_norm(o, norm_g) * jax.nn.silu(z.astype(f32).reshape(B, L, GDN_HEADS, GDN_DV))
    return o.reshape(B, L, GDN_V).astype(out_dtype)


def fox_branch(qkv, f_logit, q_norm_g, k_norm_g, f_bias):
    B, L, _ = qkv.shape
    pad = (-L) % Q_BLOCK
    Lp = L + pad
    nblk = Lp // Q_BLOCK
    q, k, v = jnp.split(qkv, 3, axis=-1)
    heads = lambda t: jnp.pad(t, ((0, 0), (0, pad), (0, 0), (0, 0))).transpose(0, 2, 1, 3)
    q = heads(rms_norm(q.reshape(B, L, FOX_HEADS, FOX_DH), q_norm_g))
    k = heads(rms_norm(k.reshape(B, L, FOX_HEADS, FOX_DH), k_norm_g))
    v = heads(v.reshape(B, L, FOX_HEADS, FOX_DH))
    log_f = jax.nn.log_sigmoid(f_logit.astype(jnp.float32) + f_bias.astype(jnp.float32))
    c = jnp.cumsum(log_f, axis=1)
    c = jnp.pad(c, ((0, 0), (0, pad), (0, 0)), mode='edge').transpose(0, 2, 1)
    kpos = jnp.arange(Lp)
    scale = FOX_DH ** -0.5

    def block(i):
        start = i * Q_BLOCK
        qs = lax.dynamic_slice_in_dim(q, start, Q_BLOCK, axis=2)
        cq = lax.dynamic_slice_in_dim(c, start, Q_BLOCK, axis=2)
        s = jnp.einsum('bhqd,bhkd->bhqk', qs, k).astype(jnp.float32) * scale
        s = s + cq[..., :, None] - c[..., None, :]
        qpos = start + jnp.arange(Q_BLOCK)
        s = jnp.where(kpos[None, :] <= qpos[:, None], s, -jnp.inf)
        p = jax.nn.softmax(s, axis=-1).astype(v.dtype)
        return jnp.einsum('bhqk,bhkd->bhqd', p, v)

    o = lax.map(block, jnp.arange(nblk))
    o = jnp.transpose(o, (1, 0, 3, 2, 4)).reshape(B, Lp, FOX_W)
    return o[:, :L]


def hybrid_layer(h, mix_norm_g, w_in, conv_w, a_log, dt_bias, gdn_norm_g, w_o_gdn,
                 fox_q_norm_g, fox_k_norm_g, fox_f_bias, w_o_fox, w_out, mlp_norm_g, w_up, w_down):
    u = rms_norm(h, mix_norm_g)
    p = u @ w_in
    qkv_a, z_a, b_a, a_a, qkv_b, f_b, gate_a, gate_b = jnp.split(p, IN_SPLIT_IDX, axis=-1)
    y_a = gated_deltanet_branch(qkv_a, z_a, b_a, a_a, conv_w, a_log, dt_bias, gdn_norm_g)
    y_b = fox_branch(qkv_b, f_b, fox_q_norm_g, fox_k_norm_g, fox_f_bias)
    mix = jax.nn.sigmoid(gate_a) * (y_a @ w_o_gdn) + jax.nn.sigmoid(gate_b) * (y_b @ w_o_fox)
    h = h + mix @ w_out
    u = rms_norm(h, mlp_norm_g)
    h = h + jnp.square(jax.nn.relu(u @ w_up)) @ w_down
    return h


def setup_inputs(seed: int = 0) -> dict:
    key = jax.random.key(seed)
    ks = jax.random.split(key, 20)
    f32 = jnp.float32
    nrm = lambda k, shape, scale: jax.random.normal(k, shape, f32) * scale
    gain = lambda k, shape: 1.0 + 0.02 * jax.random.normal(k, shape, f32)
    x = nrm(ks[0], (BATCH, SEQ, D_MODEL), 1.0)
    meta_tokens = nrm(ks[1], (N_META, D_MODEL), 1.0)
    mix_norm_g = gain(ks[2], (DEPTH, D_MODEL))
    w_in = nrm(ks[3], (DEPTH, D_MODEL, N_IN), D_MODEL ** -0.5)
    conv_w = nrm(ks[4], (DEPTH, CONV_K, GDN_CONV_DIM), CONV_K ** -0.5)
    a_log = jnp.log(jax.random.uniform(ks[5], (DEPTH, GDN_HEADS), f32, 1.0, 16.0))
    dt = jnp.exp(jax.random.uniform(ks[6], (DEPTH, GDN_HEADS), f32, math.log(1e-3), math.log(1e-1)))
    dt_bias = dt + jnp.log(-jnp.expm1(-dt))
    gdn_norm_g = gain(ks[7], (DEPTH, GDN_DV))
    w_o_gdn = nrm(ks[8], (DEPTH, GDN_V, D_MODEL), GDN_V ** -0.5)
    fox_q_norm_g = gain(ks[9], (DEPTH, FOX_DH))
    fox_k_norm_g = gain(ks[10], (DEPTH, FOX_DH))
    fox_f_bias = jax.random.uniform(ks[11], (DEPTH, FOX_HEADS), f32, 3.0, 6.0)
    w_o_fox = nrm(ks[12], (DEPTH, FOX_W, D_MODEL), FOX_W ** -0.5)
    w_out = nrm(ks[13], (DEPTH, D_MODEL, D_MODEL), D_MODEL ** -0.5)
    mlp_norm_g = gain(ks[14], (DEPTH, D_MODEL))
    w_up = nrm(ks[15], (DEPTH, D_MODEL, D_FF), D_MODEL ** -0.5)
    w_down = nrm(ks[16], (DEPTH, D_FF, D_MODEL), D_FF ** -0.5)
    final_norm_g = gain(ks[17], (D_MODEL,))
    return {"x": x, "meta_tokens": meta_tokens, "mix_norm_g": mix_norm_g, "w_in": w_in,
            "conv_w": conv_w, "a_log": a_log, "dt_bias": dt_bias, "gdn_norm_g": gdn_norm_g,
            "w_o_gdn": w_o_gdn, "fox_q_norm_g": fox_q_norm_g, "fox_k_norm_g": fox_k_norm_g,
            "fox_f_bias": fox_f_bias, "w_o_fox": w_o_fox, "w_out": w_out, "mlp_norm_g": mlp_norm_g,
            "w_up": w_up, "w_down": w_down, "final_norm_g": final_norm_g}


def reference(x, meta_tokens, mix_norm_g, w_in, conv_w, a_log, dt_bias, gdn_norm_g, w_o_gdn,
              fox_q_norm_g, fox_k_norm_g, fox_f_bias, w_o_fox, w_out, mlp_norm_g, w_up, w_down,
              final_norm_g):
    B = x.shape[0]
    meta = jnp.broadcast_to(meta_tokens[None].astype(x.dtype), (B, N_META, D_MODEL))
    h = jnp.concatenate([meta, x], axis=1)
    for l in range(DEPTH):
        h = hybrid_layer(h, mix_norm_g[l], w_in[l], conv_w[l], a_log[l], dt_bias[l], gdn_norm_g[l],
                         w_o_gdn[l], fox_q_norm_g[l], fox_k_norm_g[l], fox_f_bias[l], w_o_fox[l],
                         w_out[l], mlp_norm_g[l], w_up[l], w_down[l])
    return rms_norm(h, final_norm_g)[:, N_META:]
```

```python
import numpy as np
from contextlib import ExitStack
import concourse.bass as bass
import concourse.mybir as mybir
from concourse.bass_utils import run_bass_kernel_spmd

F32 = mybir.dt.float32
BF16 = mybir.dt.bfloat16
AF = mybir.ActivationFunctionType
ALU = mybir.AluOpType

NT = 33
OWN0 = 25
NH = 16
EPS = 1e-6
BLKS = [(0, 1)] + [(1 + 4 * i, 4) for i in range(8)]
EPOCH = 8192
NEG = -30000.0
DEBUG = False
INLINE_WAITS = True


class _Stop(Exception):
    pass


STG = {"upto": "D", "nheads": NH, "gdn_c": True, "fox_c": True, "gdn_i": True, "fox_i": True}


class Buf:
    __slots__ = ("name", "t", "w", "r", "x")

    def __init__(self, name, t, excl=False):
        self.name = name
        self.t = t
        self.w = None
        self.r = {}
        self.x = excl


class Prog:
    CE = ("pe", "dve", "act", "pool", "sp")

    def __init__(self, nc, es, need=None):
        self.nc = nc
        self.es = es
        self.need = need
        self.used = set()
        self.eng = {"pe": nc.tensor, "dve": nc.vector, "act": nc.scalar, "pool": nc.gpsimd, "sp": nc.sync}
        self.n = 0
        self.info = {}
        self.sigc = {e: 0 for e in self.CE}
        self.esem = {e: [] for e in self.CE}
        self.last = {e: None for e in self.CE}
        self.waited = {e: {} for e in self.CE}
        self.widx = {e: {s: -1 for s in self.CE} for e in self.CE}
        self.dsem = {}
        self.dpos = {}
        self.dval = {}
        self.dlast = {}
        self.nsem = 0
        for q in ("sp", "pool", "act"):
            self.dsem[q] = [self._newsem(f"d{q}{i}") for i in range(12)]
            self.dpos[q] = 0

    def _newsem(self, name):
        self.nsem += 1
        return self.es.enter_context(self.nc.semaphore(name))

    def _esem(self, e, idx):
        ep = idx // EPOCH
        while len(self.esem[e]) <= ep:
            self.esem[e].append(self._newsem(f"e{e}{len(self.esem[e])}"))
        return self.esem[e][ep]

    def _wait(self, E, d, raw, isdma):
        inf = self.info.get(d)
        if inf is None:
            return
        if inf[0] == "c":
            _, src, order = inf
            if src == E and not isdma and E == "pe":
                return
            if self.need is not None and d not in self.need:
                return
            if self.widx[E][src] >= order:
                return
            self.used.add(d)
            sidx = self.sigmap[d]
            self._emit_wait(E, self._esem(src, sidx), sidx % EPOCH + 1)
            self.widx[E][src] = order
        else:
            _, sem, val, key = inf
            if self.waited[E].get(key, 0) >= val:
                return
            self._emit_wait(E, sem, val)
            self.waited[E][key] = val

    defer = None

    def _emit_wait(self, E, sem, val):
        if self.defer is not None:
            self.defer.append((sem, val))
        else:
            self.eng[E].wait_ge(sem, val)

    sigmap = {}

    def _deps(self, E, reads, writes, isdma):
        for b in reads:
            if b.w is not None:
                self._wait(E, b.w, True, isdma)
        for b in writes:
            if b.w is not None:
                self._wait(E, b.w, False, isdma)
            for d in b.r.values():
                self._wait(E, d, False, isdma)

    def _mark(self, i, key, reads, writes):
        for b in reads:
            b.r[key] = i
        for b in writes:
            b.w = i
            b.r = {}

    def op(self, E, fn, reads=(), writes=()):
        xr = [b for b in reads if b.x]
        if xr:
            reads = [b for b in reads if not b.x]
            writes = list(writes) + [b for b in xr if b not in writes]
        if INLINE_WAITS:
            if E == "pe" and reads:
                self._deps(E, reads[:1], (), False)
                rest_r = reads[1:]
            else:
                rest_r = reads
            self.defer = []
            self._deps(E, rest_r, writes, False)
            pend, self.defer = self.defer, None
            for (sm_, vl_) in pend[:-1]:
                self.eng[E].wait_ge(sm_, vl_)
            i = self.n
            self.n += 1
            ins = fn(self.eng[E])
            if pend:
                ins._wait_ge(pend[-1][0], pend[-1][1])
        else:
            self._deps(E, reads, writes, False)
            i = self.n
            self.n += 1
            ins = fn(self.eng[E])
        if self.need is None or i in self.need:
            sidx = self.sigc[E]
            self.sigc[E] += 1
            ins.then_inc(self._esem(E, sidx), 1)
            self.sigmap[i] = sidx
        self.info[i] = ("c", E, i)
        self.last[E] = i
        self._mark(i, E, reads, writes)
        return i

    def mm(self, out, lhsT, rhs, start, stop, reads, writes):
        return self.op("pe", lambda e: e.matmul(out, lhsT=lhsT, rhs=rhs, start=start, stop=stop),
                       reads=reads, writes=writes)

    def tr(self, out, in_, ident, reads, writes):
        return self.op("pe", lambda e: e.transpose(out=out, in_=in_, identity=ident), reads=reads, writes=writes)

    def dma(self, q, out, in_, reads=(), writes=()):
        self._deps(q, reads, writes, True)
        k = self.dpos[q]
        self.dpos[q] = (k + 1) % len(self.dsem[q])
        sem = self.dsem[q][k]
        key = (q, k)
        if key in self.dlast:
            self._wait(q, self.dlast[key], False, True)
        val = self.dval.get(key, 0) + 16
        self.dval[key] = val
        i = self.n
        self.n += 1
        self.eng[q].dma_start(out=out, in_=in_).then_inc(sem, 16)
        self.info[i] = ("d", sem, val, key)
        self.dlast[key] = i
        self._mark(i, ("d", q, k), reads, writes)
        return i

    def barrier(self):
        lasts = [self.last[e] for e in self.CE if self.last[e] is not None]
        dl = list(self.dlast.values())
        for E in self.CE:
            for d in lasts:
                inf = self.info[d]
                if inf[1] == E:
                    continue
                self._wait(E, d, True, True)
            for d in dl:
                self._wait(E, d, True, True)


def build(need=None):
    nc = bass.Bass("TRN2", target_bir_lowering=False)
    dt = lambda n, s, d, k: nc.dram_tensor(n, s, d, kind=k).ap()
    xs = dt("xs", [NT * 128, 2048], F32, "ExternalInput")
    vmask = dt("vmask", [128, NT], F32, "ExternalInput")
    w_head = dt("w_head", [2048, NH * 896], F32, "ExternalInput")
    w_small = dt("w_small", [2048, 48], F32, "ExternalInput")
    w_gate = dt("w_gate", [2048, 4096], F32, "ExternalInput")
    convw = dt("convw", [128, NH * 12], F32, "ExternalInput")
    hp = dt("hp", [128, 48], F32, "ExternalInput")
    gcol = dt("gcol", [128, 3], F32, "ExternalInput")
    gvec = dt("gvec", [128, 3 * 2048], F32, "ExternalInput")
    w_og = dt("w_og", [2048, 2048], F32, "ExternalInput")
    w_of = dt("w_of", [2048, 2048], F32, "ExternalInput")
    w_out = dt("w_out", [2048, 2048], F32, "ExternalInput")
    w_up = dt("w_up", [2048, 8192], F32, "ExternalInput")
    w_down = dt("w_down", [8192, 2048], F32, "ExternalInput")
    cbf_in = dt("cbf", [128, 3712], F32, "ExternalInput")
    cf_in = dt("cf", [128, 512], F32, "ExternalInput")
    out = dt("out", [1024, 2048], F32, "ExternalOutput")
    dbg = dt("dbg", [128, 8192], F32, "ExternalOutput") if DEBUG else None
    stash = dt("stash", [9, 128, 16, 512], BF16, "Internal")
    ycat = dt("ycat", [32, 128, 1024], BF16, "Internal")
    h2d = dt("h2d", [8, 128, 2048], F32, "Internal")

    with ExitStack() as es:
        P = Prog(nc, es, need)
        Prog.sigmap = {}
        P.sigmap = {}

        uid = [0]

        def sb(es_, name, shape, dtype):
            uid[0] += 1
            return Buf(name, es_.enter_context(nc.sbuf_tensor(f"s{uid[0]}_{name}", shape, dtype)))

        def ps(es_, name, shape, dtype):
            uid[0] += 1
            return Buf(name, es_.enter_context(nc.psum_tensor(f"p{uid[0]}_{name}", shape, dtype)), excl=True)

        cbf = sb(es, "cbf", [128, 3712], BF16)
        cf = sb(es, "cf", [128, 512], F32)
        P.dma("pool", cbf.t[:], cbf_in[:, :], writes=[cbf])
        P.dma("sp", cf.t[:], cf_in[:, :], writes=[cf])
        identb = cbf.t[:, 0:128]
        onesb = cbf.t[:, 128:256]
        maskT4 = cbf.t[:, 256:768]
        maskG = lambda r: cbf.t[:, 768 + r * 512: 768 + (r + 1) * 512]
        lvl = lambda l: cbf.t[:, 2816 + l * 128: 2816 + (l + 1) * 128]
        identf = cf.t[:, 0:128]
        utri = cf.t[:, 128:256]
        slow = cf.t[:, 256:384]
        onesf = cf.t[:, 384:512]

        gct = sb(es, "gct", [128, 4], F32)
        esB = ExitStack()
        vm = sb(esB, "vm", [128, NT], F32)
        hpt = sb(esB, "hpt", [128, 48], F32)
        cwt = sb(esB, "cwt", [128, NH * 12], F32)
        P.dma("sp", vm.t[:], vmask[:, :], writes=[vm])
        P.dma("sp", hpt.t[:], hp[:, :], writes=[hpt])
        P.dma("sp", gct.t[:, 0:3], gcol[:, :], writes=[gct])
        P.dma("sp", cwt.t[:], convw[:, :], writes=[cwt])
        P.op("dve", lambda e: e.tensor_scalar(out=gct.t[:, 3:4], in0=gct.t[:, 1:2], scalar1=float(128 ** -0.5),
                                              scalar2=None, op0=ALU.mult), reads=[gct], writes=[gct])
        sm = sb(esB, "sm", [128, NT, 48], F32)
        S3 = [128, NT, NH]
        beta = sb(esB, "beta", S3, F32)
        nbeta = sb(esB, "nbeta", S3, F32)
        gg = sb(esB, "gg", S3, F32)
        egc = sb(esB, "egc", S3, F32)
        ekd = sb(esB, "ekd", S3, F32)
        egl = sb(esB, "egl", S3, F32)
        fb = sb(esB, "fb", [128, NH, 2, NT], F32)
        wsm = sb(esB, "wsm", [128, 16, 48], BF16)
        P.dma("pool", wsm.t[:], w_small.rearrange("(kc p) c -> p kc c", p=128), writes=[wsm])
        wh = sb(esB, "wh", [128, 16, 896], BF16)
        if STG["nheads"] > 0:
            P.dma("pool", wh.t[:], w_head[:, 0:896].rearrange("(kc p) c -> p kc c", p=128), writes=[wh])

        try:
            with ExitStack() as ea:
                gv0 = sb(ea, "gv0", [128, 2048], F32)
                P.dma("sp", gv0.t[:], gvec[:, 0:2048], writes=[gv0])
                xts = [sb(ea, f"xt{i}", [128, 2048], F32) for i in range(3)]
                sqj = sb(ea, "sqj", [128, 2048], BF16)
                ub = [sb(ea, f"ub{i}", [128, 2048], BF16) for i in range(2)]
                ssA = [sb(ea, f"ssA{i}", [128, 2], F32) for i in range(3)]
                uTb = [sb(ea, f"uTbA{i}", [128, 16, 512], BF16) for i in range(2)]
                tp = [ps(ea, f"tp{i}", [128, 1024], BF16) for i in range(4)]
                psS = [ps(ea, f"psS{i}", [128, 512], F32) for i in range(2)]
                tiles = [(bi, t0, ntl, ti) for bi, (t0, ntl) in enumerate(BLKS) for ti in range(ntl)]

                def stage1(t):
                    xt, ss = xts[t % 3], ssA[t % 3]
                    P.dma("sp", xt.t[:], xs[t * 128:(t + 1) * 128, :], writes=[xt])
                    P.op("act", lambda e: e.activation(out=sqj.t[:], in_=xt.t[:], func=AF.Square,
                                                       accum_out=ss.t[:, 0:1]), reads=[xt], writes=[sqj, ss])
                    P.op("act", lambda e: e.activation(out=ss.t[:, 1:2], in_=ss.t[:, 0:1], func=AF.Ln,
                                                       bias=EPS, scale=1.0 / 2048), reads=[ss], writes=[ss])
                    P.op("act", lambda e: e.activation(out=ss.t[:, 0:1], in_=ss.t[:, 1:2], func=AF.Exp,
                                                       scale=-0.5), reads=[ss], writes=[ss])

                def stage2(bi, t0, ntl, ti):
                    t = t0 + ti
                    uT = uTb[bi % 2]
                    pS = psS[bi % 2]
                    xt, ss, u1 = xts[t % 3], ssA[t % 3], ub[t % 2]
                    P.op("dve", lambda e: e.scalar_tensor_tensor(out=u1.t[:], in0=xt.t[:], scalar=ss.t[:, 0:1],
                                                                 in1=gv0.t[:], op0=ALU.mult, op1=ALU.mult),
                         reads=[xt, ss, gv0], writes=[u1])
                    for half in range(2):
                        tph = tp[(t % 2) * 2 + half]
                        for k8 in range(8):
                            kc = half * 8 + k8
                            P.tr(tph.t[:, k8 * 128:(k8 + 1) * 128], u1.t[:, kc * 128:(kc + 1) * 128], identb,
                                 reads=[u1, cbf], writes=[tph])
                        src = tph.t[:, :].rearrange("p (k c) -> p k c", k=8)
                        dst = uT.t[:, half * 8:(half + 1) * 8, ti * 128:(ti + 1) * 128]
                        if half == 0:
                            P.op("act", lambda e: e.activation(out=dst, in_=src, func=AF.Copy), reads=[tph], writes=[uT])
                        else:
                            P.op("dve", lambda e: e.tensor_copy(out=dst, in_=src), reads=[tph], writes=[uT])
                    for kc in range(16):
                        P.mm(pS.t[:, ti * 48:(ti + 1) * 48], lhsT=uT.t[:, kc, ti * 128:(ti + 1) * 128],
                             rhs=wsm.t[:, kc, :], start=(kc == 0), stop=(kc == 15), reads=[uT, wsm], writes=[pS])
                    if ti == ntl - 1:
                        P.op("dve", lambda e: e.tensor_copy(
                            out=sm.t[:, t0:t0 + ntl, :], in_=pS.t[:, 0:ntl * 48].rearrange("p (t c) -> p t c", t=ntl)),
                            reads=[pS], writes=[sm])
                        P.dma("sp", stash[bi, :, :, 0:ntl * 128], uT.t[:, :, 0:ntl * 128], reads=[uT], writes=[])

                stage1(0)
                for i_, (bi, t0, ntl, ti) in enumerate(tiles):
                    if i_ + 1 < len(tiles):
                        stage1(i_ + 1)
                    stage2(bi, t0, ntl, ti)
                P.barrier()
            if STG["upto"] == "A":
                raise _Stop()

            with ExitStack() as e0:
                t1 = sb(e0, "t1", S3, F32)
                t2 = sb(e0, "t2", S3, F32)
                lf = sb(e0, "lf", S3, F32)
                gc = sb(e0, "gc", S3, F32)
                cc = sb(e0, "cc", S3, F32)
                pre = sb(e0, "pre", S3, F32)
                tot = sb(e0, "tot", S3, F32)
                nega = sb(e0, "nega", [128, 16], F32)
                pq = [ps(e0, f"pq{i}", [128, 512], F32) for i in range(2)]
                vmb = vm.t[:, :].unsqueeze(2).to_broadcast(S3)
                hb = lambda a: hpt.t[:, a:a + 16].unsqueeze(1).to_broadcast(S3)
                A = lambda b_, lo: b_.t[:, :, lo:lo + 16]
                P.op("act", lambda e: e.activation(out=t1.t[:], in_=A(sm, 0), func=AF.Exp, scale=-1.0), reads=[sm], writes=[t1])
                P.op("dve", lambda e: e.tensor_scalar(out=t1.t[:], in0=t1.t[:], scalar1=1.0, scalar2=None, op0=ALU.add), reads=[t1], writes=[t1])
                P.op("dve", lambda e: e.reciprocal(out=t1.t[:], in_=t1.t[:]), reads=[t1], writes=[t1])
                P.op("dve", lambda e: e.tensor_tensor(out=beta.t[:], in0=t1.t[:], in1=vmb, op=ALU.mult), reads=[t1, vm], writes=[beta])
                P.op("dve", lambda e: e.tensor_scalar(out=nbeta.t[:], in0=beta.t[:], scalar1=-1.0, scalar2=None, op0=ALU.mult), reads=[beta], writes=[nbeta])
                P.op("act", lambda e: e.activation(out=nega.t[:], in_=hpt.t[:, 0:16], func=AF.Exp), reads=[hpt], writes=[nega])
                P.op("dve", lambda e: e.tensor_scalar(out=nega.t[:], in0=nega.t[:], scalar1=-1.0, scalar2=None, op0=ALU.mult), reads=[nega], writes=[nega])
                P.op("dve", lambda e: e.tensor_tensor(out=t2.t[:], in0=A(sm, 16), in1=hb(16), op=ALU.add), reads=[sm, hpt], writes=[t2])
                P.op("act", lambda e: e.activation(out=t2.t[:], in_=t2.t[:], func=AF.Exp), reads=[t2], writes=[t2])
                P.op("act", lambda e: e.activation(out=t2.t[:], in_=t2.t[:], func=AF.Ln, bias=1.0, scale=1.0), reads=[t2], writes=[t2])
                P.op("dve", lambda e: e.tensor_tensor(out=t2.t[:], in0=t2.t[:], in1=nega.t[:, :].unsqueeze(1).to_broadcast(S3), op=ALU.mult), reads=[t2, nega], writes=[t2])
                P.op("dve", lambda e: e.tensor_tensor(out=gg.t[:], in0=t2.t[:], in1=vmb, op=ALU.mult), reads=[t2, vm], writes=[gg])
                P.op("dve", lambda e: e.tensor_tensor(out=t1.t[:], in0=A(sm, 32), in1=hb(32), op=ALU.add), reads=[sm, hpt], writes=[t1])
                P.op("act", lambda e: e.activation(out=t1.t[:], in_=t1.t[:], func=AF.Exp, scale=-1.0), reads=[t1], writes=[t1])
                P.op("act", lambda e: e.activation(out=t1.t[:], in_=t1.t[:], func=AF.Ln, bias=1.0, scale=1.0), reads=[t1], writes=[t1])
                P.op("dve", lambda e: e.scalar_tensor_tensor(out=lf.t[:], in0=t1.t[:], scalar=-1.0, in1=vmb, op0=ALU.mult, op1=ALU.mult), reads=[t1, vm], writes=[lf])

                fl = lambda b_: b_.t[:, :, :].rearrange("p t h -> p (t h)")

                def colmm(lhsT, src, dst, func=None):
                    for pi in range(3):
                        pp = pq[pi % 2]
                        sl = slice(pi * 176, (pi + 1) * 176)
                        P.mm(pp.t[:, 0:176], lhsT=lhsT, rhs=fl(src)[:, sl], start=True, stop=True, reads=[cf, src], writes=[pp])
                        if func is None:
                            P.op("dve", lambda e: e.tensor_copy(out=fl(dst)[:, sl], in_=pp.t[:, 0:176]), reads=[pp], writes=[dst])
                        else:
                            P.op("act", lambda e: e.activation(out=fl(dst)[:, sl], in_=pp.t[:, 0:176], func=func), reads=[pp], writes=[dst])

                colmm(utri, gg, gc)
                colmm(onesf, gg, egl, AF.Exp)
                P.op("act", lambda e: e.activation(out=egc.t[:], in_=gc.t[:], func=AF.Exp), reads=[gc], writes=[egc])
                colmm(onesf, gg, t2)
                P.op("dve", lambda e: e.tensor_tensor(out=t2.t[:], in0=t2.t[:], in1=gc.t[:], op=ALU.subtract), reads=[t2, gc], writes=[t2])
                P.op("act", lambda e: e.activation(out=ekd.t[:], in_=t2.t[:], func=AF.Exp), reads=[t2], writes=[ekd])
                colmm(utri, lf, cc)
                colmm(onesf, lf, tot)
                P.op("dve", lambda e: e.memset(pre.t[:, 0, :], 0.0), writes=[pre])
                for t in range(1, NT):
                    P.op("dve", lambda e: e.tensor_tensor(out=pre.t[:, t, :], in0=pre.t[:, t - 1, :], in1=tot.t[:, t - 1, :],
                                                          op=ALU.add), reads=[pre, tot], writes=[pre])
                P.op("dve", lambda e: e.tensor_tensor(out=cc.t[:], in0=cc.t[:], in1=pre.t[:], op=ALU.add), reads=[cc, pre], writes=[cc])
                P.op("dve", lambda e: e.tensor_scalar(out=t1.t[:], in0=vmb, scalar1=-1.0, scalar2=-NEG, op0=ALU.add, op1=ALU.mult), reads=[vm], writes=[t1])
                P.op("dve", lambda e: e.tensor_tensor(out=t1.t[:], in0=t1.t[:], in1=cc.t[:], op=ALU.subtract), reads=[t1, cc], writes=[t1])
                for h in range(NH):
                    for G in range(2):
                        tref = 27 + 4 * G
                        P.op("dve", lambda e: e.tensor_scalar(out=fb.t[:, h, G, :], in0=t1.t[:, :, h], scalar1=pre.t[:, tref, h:h + 1],
                                                              scalar2=None, op0=ALU.add), reads=[t1, pre], writes=[fb])
                if DEBUG:
                    P.dma("sp", dbg[:, 0:528], fl(beta), reads=[beta])
                    P.dma("sp", dbg[:, 528:1056], fl(gg), reads=[gg])
                    P.dma("sp", dbg[:, 1056:1584], fl(cc), reads=[cc])
                P.barrier()

            if STG["upto"] == "S":
                raise _Stop()
            def norm_block(es_unused, src, ntok, dstap, dstbuf, a, scalar, tmp):
                sqb, lnb, rsb, psN = tmp
                P.op("dve", lambda e: e.tensor_tensor(out=sqb.t[:, 0:ntok], in0=src.t[:, 0:ntok], in1=src.t[:, 0:ntok], op=ALU.mult),
                     reads=[src], writes=[sqb])
                P.mm(psN.t[:, 0:ntok], lhsT=onesb, rhs=sqb.t[:, 0:ntok], start=True, stop=True, reads=[cbf, sqb], writes=[psN])
                P.op("act", lambda e: e.activation(out=lnb.t[:, 0:ntok], in_=psN.t[:, 0:ntok], func=AF.Ln, bias=EPS, scale=a),
                     reads=[psN], writes=[lnb])
                P.op("act", lambda e: e.activation(out=rsb.t[:, 0:ntok], in_=lnb.t[:, 0:ntok], func=AF.Exp, scale=-0.5),
                     reads=[lnb], writes=[rsb])
                P.op("dve", lambda e: e.scalar_tensor_tensor(out=dstap, in0=src.t[:, 0:ntok], scalar=scalar, in1=rsb.t[:, 0:ntok],
                                                             op0=ALU.mult, op1=ALU.mult), reads=[src, rsb, gct], writes=[dstbuf])

            with ExitStack() as eb:
                uTb = [sb(eb, f"uTbB{i}", [128, 16, 512], BF16) for i in range(2)]
                yblk = [sb(eb, f"yblk{i}", [128, 512], BF16) for i in range(2)]
                for h in range(STG["nheads"]):
                    with ExitStack() as eg:
                        gqT = sb(eg, "gqT", [128, 1024], BF16)
                        gkT = sb(eg, "gkT", [128, NT * 128], BF16)
                        szT = sb(eg, "szT", [128, 1024], BF16)
                        gv = sb(eg, "gv", [128, NT, 128], BF16)
                        fqT = sb(eg, "fqT", [128, 1024], BF16)
                        fkT = sb(eg, "fkT", [128, NT * 128], BF16)
                        fvt = sb(eg, "fvt", [128, NT, 128], BF16)
                        with ExitStack() as ei:
                            psA = [ps(ei, f"psA{i}", [128, 512], F32) for i in range(3)]
                            psNl = [ps(ei, f"psN{i}", [128, 512], F32) for i in range(2)]
                            pstl = [ps(ei, f"pst{i}", [128, 1024], BF16) for i in range(2)]
                            rb = [[sb(ei, f"rb{c}{i}", [128, 515], F32) for i in range(2)] for c in range(3)]
                            accl = [sb(ei, f"acc{i}", [128, 512], F32) for i in range(5)]
                            csl = [sb(ei, f"cs{i}", [128, 512], BF16) for i in range(16)]
                            sql = [sb(ei, f"sqb{i}", [128, 512], BF16) for i in range(6)]
                            lnl = [sb(ei, f"lnb{i}", [128, 512], F32) for i in range(6)]
                            rsl = [sb(ei, f"rsb{i}", [128, 512], F32) for i in range(6)]
                            cnt = {"cs": 0, "acc": 0, "n": 0, "pn": 0, "pt": 0, "sq": 0}

                            def take(kind, pool):
                                i_ = cnt[kind]
                                cnt[kind] += 1
                                return pool[i_ % len(pool)]

                            def norm_gen(cs, ntok, dstap, dstbuf, a, scalar):
                                sqb = take("sq", sql)
                                P.op("dve", lambda e: e.tensor_tensor(out=sqb.t[:, 0:ntok], in0=cs.t[:, 0:ntok], in1=cs.t[:, 0:ntok], op=ALU.mult),
                                     reads=[cs], writes=[sqb])
                                yield
                                psN = take("pn", psNl)
                                lnb = take("n", lnl)
                                rsb = rsl[(cnt["n"] - 1) % len(rsl)]
                                P.mm(psN.t[:, 0:ntok], lhsT=onesb, rhs=sqb.t[:, 0:ntok], start=True, stop=True, reads=[cbf, sqb], writes=[psN])
                                P.op("act", lambda e: e.activation(out=lnb.t[:, 0:ntok], in_=psN.t[:, 0:ntok], func=AF.Ln, bias=EPS, scale=a),
                                     reads=[psN], writes=[lnb])
                                P.op("act", lambda e: e.activation(out=rsb.t[:, 0:ntok], in_=lnb.t[:, 0:ntok], func=AF.Exp, scale=-0.5),
                                     reads=[lnb], writes=[rsb])
                                yield
                                yield
                                P.op("dve", lambda e: e.scalar_tensor_tensor(out=dstap, in0=cs.t[:, 0:ntok], scalar=scalar, in1=rsb.t[:, 0:ntok],
                                                                             op0=ALU.mult, op1=ALU.mult), reads=[cs, rsb, gct], writes=[dstbuf])

                            def tr_gen(cs, ntl, ntok, dst, t0, eng):
                                pst = take("pt", pstl)
                                for ti in range(ntl):
                                    P.tr(pst.t[:, ti * 128:(ti + 1) * 128], cs.t[:, ti * 128:(ti + 1) * 128], identb, reads=[cs, cbf], writes=[pst])
                                src = pst.t[:, 0:ntok].rearrange("p (t c) -> p t c", t=ntl)
                                if eng == "act":
                                    P.op("act", lambda e: e.activation(out=dst.t[:, t0:t0 + ntl, :], in_=src, func=AF.Copy), reads=[pst], writes=[dst])
                                else:
                                    P.op("dve", lambda e: e.tensor_copy(out=dst.t[:, t0:t0 + ntl, :], in_=src), reads=[pst], writes=[dst])
                                return
                                yield

                            def post_stream(ct, bi, t0, ntl, ntok, pA):
                                if STG.get("ip_nopost"):
                                    cs0 = take("cs", csl)
                                    P.op("dve" if ct % 2 else "act", (lambda e: e.tensor_copy(out=cs0.t[:, 0:ntok], in_=pA.t[:, 0:ntok])) if ct % 2 else (lambda e: e.activation(out=cs0.t[:, 0:ntok], in_=pA.t[:, 0:ntok], func=AF.Copy)), reads=[pA], writes=[cs0])
                                    return
                                if ct == 3:
                                    P.op("act", lambda e: e.activation(out=szT.t[:, (bi - 7) * 512:(bi - 6) * 512], in_=pA.t[:, :], func=AF.Silu),
                                         reads=[pA], writes=[szT])
                                    return
                                cs = take("cs", csl)
                                if ct == 6:
                                    P.op("act", lambda e: e.activation(out=cs.t[:, 0:ntok], in_=pA.t[:, 0:ntok], func=AF.Copy), reads=[pA], writes=[cs])
                                    yield
                                    yield from tr_gen(cs, ntl, ntok, fvt, t0, "dve")
                                    return
                                if ct >= 4:
                                    P.op("dve", lambda e: e.tensor_copy(out=cs.t[:, 0:ntok], in_=pA.t[:, 0:ntok]), reads=[pA], writes=[cs])
                                    if ct == 4:
                                        yield from norm_gen(cs, ntok, fqT.t[:, (bi - 7) * 512:(bi - 6) * 512], fqT, 1.0 / 128, gct.t[:, 3:4])
                                    else:
                                        yield from norm_gen(cs, ntok, fkT.t[:, t0 * 128: t0 * 128 + ntok], fkT, 1.0 / 128, gct.t[:, 2:3])
                                    return
                                r = rb[ct][bi % 2]
                                rp = rb[ct][(bi + 1) % 2]
                                P.op("dve", lambda e: e.tensor_copy(out=r.t[:, 3:3 + ntok], in_=pA.t[:, 0:ntok]), reads=[pA], writes=[r])
                                if bi == 0:
                                    P.op("pool", lambda e: e.memset(r.t[:, 0:3], 0.0), writes=[r])
                                elif ct == 0 and bi == 7:
                                    uTp = uTb[(bi + 1) % 2]
                                    pN = take("pn", psNl)
                                    for kc in range(16):
                                        P.mm(pN.t[:, 0:3], lhsT=wh.t[:, kc, 0:128], rhs=uTp.t[:, kc, 509:512],
                                             start=(kc == 0), stop=(kc == 15), reads=[wh, uTp], writes=[pN])
                                    P.op("dve", lambda e: e.tensor_copy(out=r.t[:, 0:3], in_=pN.t[:, 0:3]), reads=[pN], writes=[r])
                                else:
                                    pn = BLKS[bi - 1][1] * 128
                                    P.op("pool", lambda e: e.tensor_copy(out=r.t[:, 0:3], in_=rp.t[:, pn:pn + 3]), reads=[rp], writes=[r])
                                yield
                                acc = take("acc", accl)
                                cw = lambda kk: cwt.t[:, h * 12 + ct * 4 + kk: h * 12 + ct * 4 + kk + 1]
                                P.op("dve", lambda e: e.tensor_scalar(out=acc.t[:, 0:ntok], in0=r.t[:, 0:ntok], scalar1=cw(0), scalar2=None,
                                                                      op0=ALU.mult), reads=[r, cwt], writes=[acc])
                                for kk in range(1, 4):
                                    P.op("dve", lambda e: e.scalar_tensor_tensor(out=acc.t[:, 0:ntok], in0=r.t[:, kk:kk + ntok], scalar=cw(kk),
                                                                                 in1=acc.t[:, 0:ntok], op0=ALU.mult, op1=ALU.add),
                                         reads=[r, cwt, acc], writes=[acc])
                                yield
                                P.op("act", lambda e: e.activation(out=cs.t[:, 0:ntok], in_=acc.t[:, 0:ntok], func=AF.Silu), reads=[acc], writes=[cs])
                                yield
                                yield
                                if ct == 0:
                                    yield from norm_gen(cs, ntok, gqT.t[:, (bi - 7) * 512:(bi - 6) * 512], gqT, 1.0, float(128 ** -0.5))
                                elif ct == 1:
                                    yield from norm_gen(cs, ntok, gkT.t[:, t0 * 128: t0 * 128 + ntok], gkT, 1.0, 1.0)
                                else:
                                    yield from tr_gen(cs, ntl, ntok, gv, t0, "act")

                            active = []

                            def advance():
                                for g_ in list(active):
                                    try:
                                        next(g_)
                                    except StopIteration:
                                        active.remove(g_)

                            k = 0
                            for bi, (t0, ntl) in enumerate(BLKS if STG["gdn_i"] else []):
                                ntok = ntl * 128
                                uT = uTb[bi % 2]
                                P.dma("sp", uT.t[:, :, 0:ntok], stash[bi, :, :, 0:ntok], writes=[uT])
                                own = bi >= 7
                                for ct in ([0, 1, 2, 3, 4, 5, 6] if own else [1, 2, 5, 6]):
                                    pA = psA[k % 3]
                                    k += 1
                                    for kc in range(16):
                                        P.mm(pA.t[:, 0:ntok], lhsT=wh.t[:, kc, ct * 128:(ct + 1) * 128], rhs=uT.t[:, kc, 0:ntok],
                                             start=(kc == 0), stop=(kc == 15), reads=[wh, uT], writes=[pA])
                                    active.append(post_stream(ct, bi, t0, ntl, ntok, pA))
                                    advance()
                            while active:
                                advance()
                            P.barrier()
                        if h + 1 < STG["nheads"]:
                            P.dma("pool", wh.t[:], w_head[:, (h + 1) * 896:(h + 2) * 896].rearrange("(kc p) c -> p kc c", p=128), writes=[wh])
                        if DEBUG and h == 0:
                            with ExitStack() as ed:
                                d32 = sb(ed, "d32", [128, 4224], F32)
                                P.op("dve", lambda e: e.tensor_copy(out=d32.t[:], in_=gkT.t[:]), reads=[gkT], writes=[d32])
                                P.dma("sp", dbg[:, 1584:1584 + 4224], d32.t[:], reads=[d32])
                                P.barrier()
                        with ExitStack() as ec:
                            bankB = [Buf(f"gbank{i}", ps(ec, f"gbank{i}", [128, 512], F32).t, excl=True) for i in range(6)]
                            psq = ps(ec, "psq", [128, 512], F32).t
                            psWS = psO = psdS = Buf("psqB", psq, excl=True)
                            psTt = ps(ec, "psTt", [128, 1024], BF16).t
                            psTB = Buf("psTtB", psTt, excl=True)

                            class Slot:
                                pass

                            slots = []
                            NSL = 3
                            for si in range(NSL):
                                S_ = Slot()
                                S_.bD = S_.bY = S_.bK = S_.bR = bankB[2 * si]
                                S_.bQ = S_.bRT = bankB[2 * si + 1]
                                S_.D = S_.bD.t[:, 0:256]
                                S_.Y = S_.bD.t[:, 0:256]
                                S_.K = S_.bD.t[:, 256:512]
                                S_.R = S_.bD.t[:, 256:512]
                                S_.Q = S_.bQ.t[:, 0:256]
                                S_.RT = S_.bQ.t[:, 0:256]
                                S_.tp = psTt[:, si * 256:(si + 1) * 256]
                                S_.Gt = sb(ec, f"Gt{si}", [128, 2, 128], F32)
                                S_.ET = sb(ec, f"ET{si}", [128, 256], BF16)
                                S_.ApT = sb(ec, f"ApT{si}", [128, 2, 128], BF16)
                                S_.Mt = [sb(ec, f"Mt{si}{l}", [128, 2, 128], BF16) for l in range(7)]
                                S_.TT = [sb(ec, f"TT{si}{i}", [128, 2, 128], BF16) for i in range(2)]
                                S_.Tm = [sb(ec, f"Tm{si}{i}", [128, 2, 128], BF16) for i in range(2)]
                                S_.Yb = sb(ec, f"Yb{si}", [128, 2, 128], BF16)
                                S_.Kg = sb(ec, f"Kg{si}", [128, 2, 128], BF16)
                                S_.dg = sb(ec, f"dg{si}", [128, 2, 128], BF16)
                                slots.append(S_)
                            outs = []
                            for oi in range(2 * NSL):
                                O_ = Slot()
                                O_.QKm = sb(ec, f"QKm{oi}", [128, 256], BF16)
                                O_.QdT = sb(ec, f"QdT{oi}", [128, 256], BF16)
                                O_.WpT = sb(ec, f"WpT{oi}", [128, 256], BF16)
                                O_.Ubt = sb(ec, f"Ubt{oi}", [128, 2, 128], F32)
                                O_.Kd = sb(ec, f"Kd{oi}", [128, 2, 128], BF16)
                                outs.append(O_)
                            St = sb(ec, "St", [128, 128], F32)
                            Sbf = sb(ec, "Sbf", [128, 128], BF16)
                            vn = sb(ec, "vn", [128, 128], BF16)
                            oj = sb(ec, "oj", [128, 128], BF16)
                            oss = sb(ec, "oss", [128, 4], F32)
                            on4 = sb(ec, "on4", [128, 4, 128], BF16)
                            P.op("pool", lambda e: e.memset(St.t[:], 0.0), writes=[St])
                            P.op("pool", lambda e: e.memset(Sbf.t[:], 0.0), writes=[Sbf])
                            hsl = slice(h, h + 1)

                            def par_phase(t0, nb, S_, O_):
                                W = nb * 128
                                bc3 = [128, nb, 128]
                                own = t0 >= OWN0
                                qo = (t0 - OWN0) * 128
                                colb = lambda b_: b_.t[:, t0:t0 + nb, hsl].to_broadcast(bc3)
                                tk = lambda p: slice((t0 + p) * 128, (t0 + p + 1) * 128)
                                pc = lambda p: slice(p * 128, (p + 1) * 128)
                                f3 = lambda ap: ap[:, 0:W].rearrange("p (t c) -> p t c", t=nb)
                                P.op("dve", lambda e: e.tensor_tensor(out=S_.Gt.t[:, 0:nb, :], in0=colb(gg), in1=slow.unsqueeze(1).to_broadcast(bc3),
                                                                      op=ALU.mult), reads=[gg, cf], writes=[S_.Gt])
                                P.mm(S_.D[:, 0:W], lhsT=identb, rhs=maskT4[:, 0:W], start=True, stop=False, reads=[cbf], writes=[S_.bD])
                                for p in range(nb):
                                    P.mm(S_.D[:, pc(p)], lhsT=S_.Gt.t[:, p, :], rhs=utri, start=False, stop=(p == nb - 1), reads=[S_.Gt, cf], writes=[S_.bD])
                                for p in range(nb):
                                    P.mm(S_.K[:, pc(p)], lhsT=gkT.t[:, tk(p)], rhs=gkT.t[:, tk(p)], start=True, stop=True, reads=[gkT], writes=[S_.bK])
                                if own:
                                    for p in range(nb):
                                        P.mm(S_.Q[:, pc(p)], lhsT=gkT.t[:, tk(p)], rhs=gqT.t[:, qo + p * 128: qo + (p + 1) * 128],
                                             start=True, stop=True, reads=[gkT, gqT], writes=[S_.bQ])
                                yield
                                P.op("act", lambda e: e.activation(out=S_.ET.t[:, 0:W], in_=S_.D[:, 0:W], func=AF.Exp), reads=[S_.bD], writes=[S_.ET])
                                yield
                                for p in range(nb):
                                    P.op("dve", lambda e: e.scalar_tensor_tensor(out=S_.ApT.t[:, p, :], in0=S_.K[:, pc(p)], scalar=beta.t[:, t0 + p, hsl],
                                                                                 in1=S_.ET.t[:, pc(p)], op0=ALU.mult, op1=ALU.mult),
                                         reads=[S_.bK, beta, S_.ET], writes=[S_.ApT])
                                if own:
                                    P.op("dve", lambda e: e.tensor_tensor(out=O_.QKm.t[:, 0:W], in0=S_.Q[:, 0:W], in1=S_.ET.t[:, 0:W], op=ALU.mult),
                                         reads=[S_.bQ, S_.ET], writes=[O_.QKm])
                                yield
                                def mask_op(l):
                                    P.op("pool", lambda e: e.tensor_tensor(out=S_.Mt[l].t[:, 0:nb, :], in0=S_.ApT.t[:, 0:nb, :],
                                                                           in1=lvl(l).unsqueeze(1).to_broadcast(bc3), op=ALU.mult),
                                         reads=[S_.ApT, cbf], writes=[S_.Mt[l]])

                                P.op("dve", lambda e: e.tensor_tensor(out=S_.Mt[0].t[:, 0:nb, :], in0=S_.ApT.t[:, 0:nb, :],
                                                                      in1=lvl(0).unsqueeze(1).to_broadcast(bc3), op=ALU.mult),
                                     reads=[S_.ApT, cbf], writes=[S_.Mt[0]])
                                P.op("dve", lambda e: e.tensor_tensor(out=S_.TT[0].t[:, 0:nb, :], in0=identb.unsqueeze(1).to_broadcast(bc3),
                                                                       in1=S_.Mt[0].t[:, 0:nb, :], op=ALU.subtract), reads=[S_.Mt[0], cbf], writes=[S_.TT[0]])
                                mask_op(1)
                                yield
                                for p in range(nb):
                                    P.tr(S_.tp[:, pc(p)], S_.TT[0].t[:, p, :], identb, reads=[S_.TT[0], cbf], writes=[psTB])
                                P.op("act", lambda e: e.activation(out=S_.Tm[0].t[:, 0:nb, :], in_=f3(S_.tp), func=AF.Copy), reads=[psTB], writes=[S_.Tm[0]])
                                yield
                                for l in range(1, 7):
                                    cur, nxt = (l - 1) % 2, l % 2
                                    if l + 1 < 7:
                                        mask_op(l + 1)
                                    for p in range(nb):
                                        P.mm(S_.Y[:, pc(p)], lhsT=S_.Mt[l].t[:, p, :], rhs=S_.Tm[cur].t[:, p, :], start=True, stop=True,
                                             reads=[S_.Mt[l], S_.Tm[cur]], writes=[S_.bY])
                                    yield
                                    P.op("act", lambda e: e.activation(out=S_.Yb.t[:, 0:nb, :], in_=f3(S_.Y), func=AF.Copy), reads=[S_.bY], writes=[S_.Yb])
                                    yield
                                    if l < 6:
                                        for p in range(nb):
                                            P.mm(S_.R[:, pc(p)], lhsT=S_.TT[cur].t[:, p, :], rhs=S_.Yb.t[:, p, :], start=True, stop=True,
                                                 reads=[S_.TT[cur], S_.Yb], writes=[S_.bR])
                                    for p in range(nb):
                                        P.mm(S_.RT[:, pc(p)], lhsT=S_.Yb.t[:, p, :], rhs=S_.TT[cur].t[:, p, :], start=True, stop=True,
                                             reads=[S_.Yb, S_.TT[cur]], writes=[S_.bRT])
                                    yield
                                    if l < 6:
                                        P.op("dve", lambda e: e.tensor_tensor(out=S_.Tm[nxt].t[:, 0:nb, :], in0=S_.Tm[cur].t[:, 0:nb, :], in1=f3(S_.R),
                                                                              op=ALU.subtract), reads=[S_.Tm[cur], S_.bR], writes=[S_.Tm[nxt]])
                                    P.op("dve", lambda e: e.tensor_tensor(out=S_.TT[nxt].t[:, 0:nb, :], in0=S_.TT[cur].t[:, 0:nb, :], in1=f3(S_.RT),
                                                                          op=ALU.subtract), reads=[S_.TT[cur], S_.bRT], writes=[S_.TT[nxt]])
                                    yield
                                TF = S_.TT[0]
                                for p in range(nb):
                                    P.mm(S_.K[:, pc(p)], lhsT=TF.t[:, p, :], rhs=gv.t[:, t0 + p, :], start=True, stop=True, reads=[TF, gv], writes=[S_.bK])
                                for p in range(nb):
                                    P.tr(S_.tp[:, pc(p)], gkT.t[:, tk(p)], identb, reads=[gkT, cbf], writes=[psTB])
                                yield
                                for p in range(nb):
                                    P.op("act", lambda e: e.activation(out=O_.Ubt.t[:, p, :], in_=S_.K[:, pc(p)], func=AF.Copy,
                                                                       scale=beta.t[:, t0 + p, hsl]), reads=[S_.bK, beta], writes=[O_.Ubt])
                                    P.op("act", lambda e: e.activation(out=S_.Kg.t[:, p, :], in_=S_.tp[:, pc(p)], func=AF.Copy,
                                                                       scale=egc.t[:, t0 + p, hsl]), reads=[psTB, egc], writes=[S_.Kg])
                                    P.op("dve", lambda e: e.tensor_scalar(out=O_.Kd.t[:, p, :], in0=S_.tp[:, pc(p)], scalar1=ekd.t[:, t0 + p, hsl],
                                                                          scalar2=None, op0=ALU.mult), reads=[psTB, ekd], writes=[O_.Kd])
                                if own:
                                    P.op("pool", lambda e: e.tensor_tensor(out=S_.dg.t[:, 0:nb, :], in0=identb.unsqueeze(1).to_broadcast(bc3),
                                                                           in1=colb(egc), op=ALU.mult), reads=[egc, cbf], writes=[S_.dg])
                                yield
                                for p in range(nb):
                                    P.mm(S_.Q[:, pc(p)], lhsT=S_.Kg.t[:, p, :], rhs=TF.t[:, p, :], start=True, stop=True, reads=[S_.Kg, TF], writes=[S_.bQ])
                                if own:
                                    P.mm(S_.D[:, 0:W], lhsT=onesb, rhs=S_.dg.t[:, 0:nb, :].rearrange("p t c -> p (t c)"), start=True, stop=True,
                                         reads=[cbf, S_.dg], writes=[S_.bD])
                                yield
                                P.op("act", lambda e: e.activation(out=O_.WpT.t[:, 0:W], in_=S_.Q[:, 0:W], func=AF.Copy), reads=[S_.bQ], writes=[O_.WpT])
                                if own:
                                    P.op("dve", lambda e: e.tensor_tensor(out=O_.QdT.t[:, 0:W], in0=S_.D[:, 0:W], in1=gqT.t[:, qo:qo + W], op=ALU.mult),
                                         reads=[S_.bD, gqT], writes=[O_.QdT])
                                yield

                            def seq_phase(t0, nb, O_):
                                pc = lambda p: slice(p * 128, (p + 1) * 128)
                                for p in range(nb):
                                    t = t0 + p
                                    own = t >= OWN0
                                    P.mm(psq[:, 0:128], lhsT=O_.WpT.t[:, pc(p)], rhs=Sbf.t[:], start=True, stop=True, reads=[O_.WpT, Sbf], writes=[psWS])
                                    P.op("dve", lambda e: e.scalar_tensor_tensor(out=vn.t[:], in0=psq[:, 0:128], scalar=nbeta.t[:, t, hsl],
                                                                                 in1=O_.Ubt.t[:, p, :], op0=ALU.mult, op1=ALU.add),
                                         reads=[psWS, nbeta, O_.Ubt], writes=[vn])
                                    yield
                                    if own:
                                        P.mm(psq[:, 128:256], lhsT=O_.QdT.t[:, pc(p)], rhs=Sbf.t[:], start=True, stop=False, reads=[O_.QdT, Sbf], writes=[psO])
                                        P.mm(psq[:, 128:256], lhsT=O_.QKm.t[:, pc(p)], rhs=vn.t[:], start=False, stop=True, reads=[O_.QKm, vn], writes=[psO])
                                    P.mm(psq[:, 256:384], lhsT=O_.Kd.t[:, p, :], rhs=vn.t[:], start=True, stop=True, reads=[O_.Kd, vn], writes=[psdS])
                                    P.op("dve", lambda e: e.scalar_tensor_tensor(out=St.t[:], in0=St.t[:], scalar=egl.t[:, t, hsl], in1=psq[:, 256:384],
                                                                                 op0=ALU.mult, op1=ALU.add), reads=[St, egl, psdS], writes=[St])
                                    P.op("act", lambda e: e.activation(out=Sbf.t[:], in_=St.t[:], func=AF.Copy), reads=[St], writes=[Sbf])
                                    yield
                                    if own:
                                        oi = (t - OWN0) % 4
                                        P.op("act", lambda e: e.activation(out=oj.t[:], in_=psq[:, 128:256], func=AF.Square, accum_out=oss.t[:, 0:1]),
                                             reads=[psO], writes=[oj, oss])
                                        P.op("act", lambda e: e.activation(out=oss.t[:, 1:2], in_=oss.t[:, 0:1], func=AF.Ln, bias=EPS, scale=1.0 / 128),
                                             reads=[oss], writes=[oss])
                                        P.op("act", lambda e: e.activation(out=oss.t[:, 2:3], in_=oss.t[:, 1:2], func=AF.Exp, scale=-0.5),
                                             reads=[oss], writes=[oss])
                                        P.op("act", lambda e: e.activation(out=on4.t[:, oi, :], in_=psq[:, 128:256], func=AF.Copy, scale=oss.t[:, 2:3]),
                                             reads=[psO, oss], writes=[on4])
                                        if oi == 3:
                                            yield
                                            G = (t - OWN0) // 4
                                            yb = yblk[G]
                                            for hh in range(2):
                                                for p2 in range(2):
                                                    p4 = hh * 2 + p2
                                                    P.tr(psTt[:, 768 + p2 * 128: 768 + (p2 + 1) * 128], on4.t[:, p4, :], identb, reads=[on4, cbf], writes=[psTB])
                                                P.op("dve", lambda e: e.scalar_tensor_tensor(out=yb.t[:, hh * 256:(hh + 1) * 256], in0=psTt[:, 768:1024], scalar=gct.t[:, 0:1],
                                                                                             in1=szT.t[:, G * 512 + hh * 256: G * 512 + (hh + 1) * 256], op0=ALU.mult, op1=ALU.mult),
                                                     reads=[psTB, gct, szT], writes=[yb])
                                            P.dma("sp", ycat[h, :, G * 512:(G + 1) * 512], yb.t[:], reads=[yb])
                                        yield

                            subs = [(0, 1)] + [(1 + 2 * i, 2) for i in range(16)]
                            if not STG["gdn_c"]:
                                subs = []
                            subs = subs[STG.get("sub0", 0):STG.get("sub1", 99)]
                            pairs = [subs[i:i + NSL] for i in range(0, len(subs), NSL)][:STG.get("npairs", 99)]
                            prev = []
                            for pi, pr in enumerate(pairs + [[]]):
                                gens = []
                                cur_out = []
                                for j, (t0, nb) in enumerate(pr):
                                    O_ = outs[(pi % 2) * NSL + j]
                                    gens.append(par_phase(t0, nb, slots[j], O_))
                                    cur_out.append((t0, nb, O_))

                                def seq_all(items):
                                    for (t0_, nb_, O2) in items:
                                        yield from seq_phase(t0_, nb_, O2)

                                sg = seq_all(prev if STG.get("doseq", 1) else [])
                                live = list(gens)
                                sdone = False
                                rnd = 0
                                while live or not sdone:
                                    for g_ in list(live):
                                        if rnd >= STG.get("pstop", 9999):
                                            live.remove(g_)
                                            continue
                                        try:
                                            next(g_)
                                        except StopIteration:
                                            live.remove(g_)
                                    if not sdone and (rnd % 2 == 0 or not live):
                                        try:
                                            next(sg)
                                        except StopIteration:
                                            sdone = True
                                    rnd += 1
                                prev = cur_out
                            psS = [bankB[0], bankB[1], psWS]
                            psO = [bankB[2], bankB[3]]
                            psL = [bankB[4], bankB[5]]
                            PT = [sb(ec, f"PT{i}", [128, 512], BF16) for i in range(4)]
                            rl = sb(ec, "rl", [128, 512], F32)
                            units = []
                            for G in range(2 if STG["fox_c"] else 0):
                                qt0 = OWN0 + 4 * G
                                for kt in range(qt0 + 4):
                                    units.append((G, kt, qt0))

                            def emit_S(u):
                                G, kt, qt0 = units[u]
                                pS_ = psS[u % 3]
                                diag = kt >= qt0
                                P.mm(pS_.t[:, :], lhsT=fkT.t[:, kt * 128:(kt + 1) * 128], rhs=fqT.t[:, G * 512:(G + 1) * 512],
                                     start=True, stop=not diag, reads=[fkT, fqT], writes=[pS_])
                                if diag:
                                    P.mm(pS_.t[:, :], lhsT=identb, rhs=maskG(kt - qt0), start=False, stop=True, reads=[cbf], writes=[pS_])

                            for u0 in range(min(2, len(units))):
                                emit_S(u0)
                            for u in range(len(units)):
                                G, kt, qt0 = units[u]
                                last = qt0 + 3
                                pS_ = psS[u % 3]
                                pt = PT[u % 4]
                                P.op("act", lambda e: e.activation(out=pt.t[:], in_=pS_.t[:, :], func=AF.Exp, bias=fb.t[:, h, G, kt:kt + 1], scale=1.0),
                                     reads=[pS_, fb], writes=[pt])
                                if u + 2 < len(units):
                                    emit_S(u + 2)
                                P.mm(psO[G].t[:, :], lhsT=fvt.t[:, kt, :], rhs=pt.t[:], start=(kt == 0), stop=(kt == last), reads=[fvt, pt], writes=[psO[G]])
                                P.mm(psL[G].t[:, :], lhsT=onesb, rhs=pt.t[:], start=(kt == 0), stop=(kt == last), reads=[cbf, pt], writes=[psL[G]])
                                if kt == last:
                                    P.op("dve", lambda e: e.reciprocal(out=rl.t[:], in_=psL[G].t[:, :]), reads=[psL[G]], writes=[rl])
                                    yb = yblk[G]
                                    P.op("dve", lambda e: e.tensor_tensor(out=yb.t[:], in0=psO[G].t[:, :], in1=rl.t[:], op=ALU.mult), reads=[psO[G], rl], writes=[yb])
                                    P.dma("sp", ycat[16 + h, :, G * 512:(G + 1) * 512], yb.t[:], reads=[yb])
                            P.barrier()
                P.barrier()

            if DEBUG:
                P.dma("pool", dbg[:, 5808:6832], ycat[0, :, :])
                P.dma("pool", dbg[:, 6832:7856], ycat[16, :, :])
                P.barrier()
            if STG["upto"] == "B":
                raise _Stop()
            esB.close()
            with ExitStack() as edd:
                psd = [ps(edd, f"psd{i}", [128, 512], F32) for i in range(6)]
                tpx = [ps(edd, f"tpx{i}", [128, 1024], BF16) for i in range(2)]
                h2dB = Buf("h2dB", None)
                wb = [sb(edd, f"wbD{i}", [128, 16, 512], BF16) for i in range(3)]
                wcnt = [0]

                def wload(src_ap):
                    w = wb[wcnt[0] % 3]
                    wcnt[0] += 1
                    P.dma("pool", w.t[:], src_ap, writes=[w])
                    return w

                kcp = lambda ap: ap.rearrange("(kc p) c -> p kc c", p=128)
                hbufs = [(wb[i].t, hh * 256, Buf(f"wbh{i}{hh}", None)) for i in range(3) for hh in range(2)]
                hcnt = [0]

                def hload(src_ap):
                    tt, off, hb = hbufs[hcnt[0] % 6]
                    hcnt[0] += 1
                    P.dma("pool", tt[:, :, off:off + 256], src_ap, writes=[hb])
                    return tt, off, hb

                u2T = sb(edd, "u2T", [128, 16, 1024], BF16)
                with ExitStack() as e1:
                    mixT = sb(e1, "mixT", [128, 16, 1024], BF16)
                    with ExitStack() as e2:
                        uTo = sb(e2, "uTo", [128, 16, 1024], BF16)
                        for j in range(2):
                            P.dma("sp", uTo.t[:, :, j * 512:(j + 1) * 512], stash[7 + j, :, :, :], writes=[uTo])
                        yT1 = sb(e2, "yT", [128, 16, 1024], BF16)
                        yT = [yT1, yT1]
                        sg = [sb(e2, f"sg{i}", [128, 512], F32) for i in range(2)]
                        mx = [sb(e2, f"mx{i}", [128, 512], F32) for i in range(2)]
                        pk = 0
                        for br in range(2):
                            for j in range(4):
                                P.dma("sp", yT1.t[:, j * 4:(j + 1) * 4, :],
                                      ycat[16 * br + 4 * j: 16 * br + 4 * j + 4, :, :].rearrange("k p t -> p k t"), writes=[yT1])
                            for cb in range(4):
                                hw = {}
                                for hh in range(2):
                                    c0 = cb * 512 + hh * 256
                                    hw[("g", hh)] = hload(kcp(w_gate[:, br * 2048 + c0: br * 2048 + c0 + 256]))
                                    hw[("o", hh)] = hload(kcp((w_og if br == 0 else w_of)[:, c0:c0 + 256]))
                                for cc_ in range(4):
                                    ch = cb * 4 + cc_
                                    wgt, wgo, wg = hw[("g", cc_ // 2)]
                                    wot, woo, wo = hw[("o", cc_ // 2)]
                                    gsl = slice(wgo + (cc_ % 2) * 128, wgo + (cc_ % 2 + 1) * 128)
                                    osl = slice(woo + (cc_ % 2) * 128, woo + (cc_ % 2 + 1) * 128)
                                    for hf in range(2):
                                        tsl = slice(hf * 512, (hf + 1) * 512)
                                        pgt = psd[pk % 6]
                                        ppt = psd[(pk + 1) % 6]
                                        pk += 2
                                        for kc in range(16):
                                            P.mm(pgt.t[:, :], lhsT=wgt[:, kc, gsl], rhs=uTo.t[:, kc, tsl],
                                                 start=(kc == 0), stop=(kc == 15), reads=[wg, uTo], writes=[pgt])
                                        for kc in range(16):
                                            P.mm(ppt.t[:, :], lhsT=wot[:, kc, osl], rhs=yT[br].t[:, kc, tsl],
                                                 start=(kc == 0), stop=(kc == 15), reads=[wo, yT[br]], writes=[ppt])
                                        s_ = sg[hf]
                                        P.op("act", lambda e: e.activation(out=s_.t[:], in_=pgt.t[:, :], func=AF.Exp, scale=-1.0), reads=[pgt], writes=[s_])
                                        P.op("dve", lambda e: e.tensor_scalar(out=s_.t[:], in0=s_.t[:], scalar1=1.0, scalar2=None, op0=ALU.add), reads=[s_], writes=[s_])
                                        P.op("dve", lambda e: e.reciprocal(out=s_.t[:], in_=s_.t[:]), reads=[s_], writes=[s_])
                                        if br == 0:
                                            P.op("dve", lambda e: e.tensor_tensor(out=mixT.t[:, ch, tsl], in0=ppt.t[:, :], in1=s_.t[:], op=ALU.mult),
                                                 reads=[ppt, s_], writes=[mixT])
                                        else:
                                            m_ = mx[hf]
                                            P.op("dve", lambda e: e.tensor_tensor(out=m_.t[:], in0=ppt.t[:, :], in1=s_.t[:], op=ALU.mult),
                                                 reads=[ppt, s_], writes=[m_])
                                            P.op("pool", lambda e: e.tensor_tensor(out=mixT.t[:, ch, tsl], in0=mixT.t[:, ch, tsl], in1=m_.t[:], op=ALU.add),
                                                 reads=[mixT, m_], writes=[mixT])
                        P.barrier()
                    with ExitStack() as e3:
                        gv1 = sb(e3, "gv1", [128, 2048], F32)
                        P.dma("sp", gv1.t[:], gvec[:, 2048:4096], writes=[gv1])
                        h2t = sb(e3, "h2t", [128, 8, 2048], F32)
                        for ti in range(8):
                            P.dma("sp", h2t.t[:, ti, :], xs[(OWN0 + ti) * 128:(OWN0 + ti + 1) * 128, :], writes=[h2t])
                        pk = 0
                        for cb in range(4):
                            wo = wload(kcp(w_out[:, cb * 512:(cb + 1) * 512]))
                            for ti in range(8):
                                pp = psd[pk % 6]
                                pk += 1
                                for kc in range(16):
                                    P.mm(pp.t[:, :], lhsT=mixT.t[:, kc, ti * 128:(ti + 1) * 128], rhs=wo.t[:, kc, :],
                                         start=(kc == 0), stop=(kc == 15), reads=[mixT, wo], writes=[pp])
                                P.op("dve", lambda e: e.tensor_tensor(out=h2t.t[:, ti, cb * 512:(cb + 1) * 512], in0=pp.t[:, :],
                                                                      in1=h2t.t[:, ti, cb * 512:(cb + 1) * 512], op=ALU.add), reads=[pp, h2t], writes=[h2t])
                        sq2 = sb(e3, "sq2", [128, 2048], BF16)
                        u2 = [sb(e3, f"u2{i}", [128, 2048], BF16) for i in range(2)]
                        ss2 = [sb(e3, f"ss2{i}", [128, 4], F32) for i in range(2)]
                        def d3_stage1(ti):
                            ss = ss2[ti % 2]
                            P.dma("sp", h2d[ti, :, :], h2t.t[:, ti, :], reads=[h2t])
                            P.op("act", lambda e: e.activation(out=sq2.t[:], in_=h2t.t[:, ti, :], func=AF.Square, accum_out=ss.t[:, 0:1]),
                                 reads=[h2t], writes=[sq2, ss])
                            P.op("act", lambda e: e.activation(out=ss.t[:, 1:2], in_=ss.t[:, 0:1], func=AF.Ln, bias=EPS, scale=1.0 / 2048), reads=[ss], writes=[ss])
                            P.op("act", lambda e: e.activation(out=ss.t[:, 2:3], in_=ss.t[:, 1:2], func=AF.Exp, scale=-0.5), reads=[ss], writes=[ss])

                        def d3_stage2(ti):
                            ss = ss2[ti % 2]
                            uu = u2[ti % 2]
                            P.op("dve", lambda e: e.scalar_tensor_tensor(out=uu.t[:], in0=h2t.t[:, ti, :], scalar=ss.t[:, 2:3], in1=gv1.t[:],
                                                                         op0=ALU.mult, op1=ALU.mult), reads=[h2t, ss, gv1], writes=[uu])
                            for half in range(2):
                                pb_ = tb_ = tpx[half]
                                for k8 in range(8):
                                    kc = half * 8 + k8
                                    P.tr(tb_.t[:, k8 * 128:(k8 + 1) * 128], uu.t[:, kc * 128:(kc + 1) * 128], identb, reads=[uu, cbf], writes=[pb_])
                                P.op("act" if half == 0 else "dve",
                                     (lambda e: e.activation(out=u2T.t[:, 0:8, ti * 128:(ti + 1) * 128],
                                                             in_=tb_.t[:, :].rearrange("p (k c) -> p k c", k=8), func=AF.Copy)) if half == 0 else
                                     (lambda e: e.tensor_copy(out=u2T.t[:, 8:16, ti * 128:(ti + 1) * 128],
                                                              in_=tb_.t[:, :].rearrange("p (k c) -> p k c", k=8))),
                                     reads=[pb_], writes=[u2T])

                        d3_stage1(0)
                        for ti in range(8):
                            if ti + 1 < 8:
                                d3_stage1(ti + 1)
                            d3_stage2(ti)
                        P.barrier()
                with ExitStack() as e4:
                    gv2 = sb(e4, "gv2", [128, 2048], F32)
                    P.dma("sp", gv2.t[:], gvec[:, 4096:6144], writes=[gv2])
                    h3a = [sb(e4, f"h3a{i}", [128, 2048], F32) for i in range(8)]
                    for ti in range(8):
                        P.dma("sp", h3a[ti].t[:], h2d[ti, :, :], writes=[h3a[ti]])
                    ab = [sb(e4, f"ab{i}", [128, 4, 1024], BF16) for i in range(2)]
                    rl_ = [sb(e4, f"rlu{i}", [128, 512], F32) for i in range(2)]
                    sq3 = sb(e4, "sq3", [128, 2048], BF16)
                    ss3 = [sb(e4, f"ss3{i}", [128, 4], F32) for i in range(2)]
                    pk = 0
                    pd = 0
                    for fbk in range(16):
                        wu = wload(kcp(w_up[:, fbk * 512:(fbk + 1) * 512]))
                        wdt = wload(w_down[fbk * 512:(fbk + 1) * 512, :].rearrange("(fc p) c -> p fc c", p=128))
                        wd = wdt.t[:].rearrange("p a b -> p (a b)").rearrange("p (f c) -> p f c", f=4)
                        a_ = ab[fbk % 2]
                        for cc_ in range(4):
                            for hf in range(2):
                                pp = psd[4 + pk % 2]
                                r_ = rl_[pk % 2]
                                pk += 1
                                for kc in range(16):
                                    P.mm(pp.t[:, :], lhsT=wu.t[:, kc, cc_ * 128:(cc_ + 1) * 128], rhs=u2T.t[:, kc, hf * 512:(hf + 1) * 512],
                                         start=(kc == 0), stop=(kc == 15), reads=[wu, u2T], writes=[pp])
                                P.op("act", lambda e: e.activation(out=r_.t[:], in_=pp.t[:, :], func=AF.Relu), reads=[pp], writes=[r_])
                                P.op("pool", lambda e: e.tensor_tensor(out=a_.t[:, cc_, hf * 512:(hf + 1) * 512], in0=r_.t[:], in1=r_.t[:], op=ALU.mult),
                                     reads=[r_], writes=[a_])
                        for ti in range(8):
                            for cb in range(4):
                                pp = psd[pd % 4]
                                pd += 1
                                for fc in range(4):
                                    P.mm(pp.t[:, :], lhsT=a_.t[:, fc, ti * 128:(ti + 1) * 128], rhs=wd[:, fc, cb * 512:(cb + 1) * 512],
                                         start=(fc == 0), stop=(fc == 3), reads=[a_, wdt], writes=[pp])
                                P.op("dve", lambda e: e.tensor_tensor(out=h3a[ti].t[:, cb * 512:(cb + 1) * 512], in0=pp.t[:, :],
                                                                      in1=h3a[ti].t[:, cb * 512:(cb + 1) * 512], op=ALU.add),
                                     reads=[pp, h3a[ti]], writes=[h3a[ti]])
                    for ti in range(8):
                        hb_ = h3a[ti]
                        ss = ss3[ti % 2]
                        P.op("act", lambda e: e.activation(out=sq3.t[:], in_=hb_.t[:], func=AF.Square, accum_out=ss.t[:, 0:1]),
                             reads=[hb_], writes=[sq3, ss])
                        P.op("act", lambda e: e.activation(out=ss.t[:, 1:2], in_=ss.t[:, 0:1], func=AF.Ln, bias=EPS, scale=1.0 / 2048), reads=[ss], writes=[ss])
                        P.op("act", lambda e: e.activation(out=ss.t[:, 2:3], in_=ss.t[:, 1:2], func=AF.Exp, scale=-0.5), reads=[ss], writes=[ss])
                        P.op("dve", lambda e: e.scalar_tensor_tensor(out=hb_.t[:], in0=hb_.t[:], scalar=ss.t[:, 2:3], in1=gv2.t[:],
                                                                     op0=ALU.mult, op1=ALU.mult), reads=[hb_, ss, gv2], writes=[hb_])
                        P.dma("sp", out[ti * 128:(ti + 1) * 128, :], hb_.t[:], reads=[hb_])
                    P.barrier()
        except _Stop:
            esB.close()
        P.barrier()
    return nc, P


_CACHE = {}


def _consts():
    p = np.arange(128)[:, None]
    f = np.arange(128)[None, :]
    ident = (p == f).astype(np.float32)
    ones = np.ones((128, 128), np.float32)
    maskT = np.where(f >= p, 0.0, NEG).astype(np.float32)
    q = np.arange(512)[None, :]
    maskG = [np.where(q - 128 * r - p >= 0, 0.0, NEG).astype(np.float32) for r in range(4)]
    lv = []
    for l in range(7):
        s_ = 1 << l
        j, i = p, f
        m = ((i // (2 * s_)) == (j // (2 * s_))) & ((i % (2 * s_)) >= s_) & ((j % (2 * s_)) < s_)
        lv.append(m.astype(np.float32))
    cbf = np.concatenate([ident, ones] + [maskT] * 4 + maskG + lv, axis=1)
    utri = (p <= f).astype(np.float32)
    slow = (p > f).astype(np.float32)
    cf = np.concatenate([ident, utri, slow, ones], axis=1)
    return np.ascontiguousarray(cbf), np.ascontiguousarray(cf)


def kernel(x, meta_tokens, mix_norm_g, w_in, conv_w, a_log, dt_bias, gdn_norm_g, w_o_gdn,
           fox_q_norm_g, fox_k_norm_g, fox_f_bias, w_o_fox, w_out, mlp_norm_g, w_up, w_down, final_norm_g):
    f32 = np.float32
    x = np.asarray(x, f32)
    w_in0 = np.asarray(w_in, f32)[0]
    if "nc" not in _CACHE:
        _, P1 = build(None)
        need = set(P1.used)
        nc, _ = build(need)
        _CACHE["nc"] = nc
    nc = _CACHE["nc"]
    GQ, GZ, GB, GA, FQ, FF, GTA = 0, 6144, 8192, 8208, 8224, 14368, 14384
    cols = []
    for h in range(NH):
        sl = lambda base: w_in0[:, base + h * 128: base + (h + 1) * 128]
        cols += [sl(GQ), sl(GQ + 2048), sl(GQ + 4096), sl(GZ), sl(FQ), sl(FQ + 2048), sl(FQ + 4096)]
    w_head = np.ascontiguousarray(np.concatenate(cols, axis=1))
    w_small = np.ascontiguousarray(np.concatenate([w_in0[:, GB:GB + 16], w_in0[:, GA:GA + 16], w_in0[:, FF:FF + 16]], axis=1))
    w_gate = np.ascontiguousarray(w_in0[:, GTA:GTA + 4096])
    cw = np.asarray(conv_w, f32)[0]
    cwl = np.zeros((128, NH, 3, 4), f32)
    for h in range(NH):
        for c in range(3):
            cwl[:, h, c, :] = cw[:, c * 2048 + h * 128: c * 2048 + (h + 1) * 128].T
    convw = np.ascontiguousarray(cwl.reshape(128, NH * 12))
    hp = np.ascontiguousarray(np.tile(np.concatenate([np.asarray(a_log, f32)[0], np.asarray(dt_bias, f32)[0],
                                                      np.asarray(fox_f_bias, f32)[0]])[None, :], (128, 1)))
    gcol = np.ascontiguousarray(np.stack([np.asarray(gdn_norm_g, f32)[0], np.asarray(fox_q_norm_g, f32)[0],
                                          np.asarray(fox_k_norm_g, f32)[0]], axis=1))
    gvec = np.ascontiguousarray(np.tile(np.concatenate([np.asarray(mix_norm_g, f32)[0], np.asarray(mlp_norm_g, f32)[0],
                                                        np.asarray(final_norm_g, f32)])[None, :], (128, 1)))
    cbf, cf = _consts()
    shared = {"w_head": w_head, "w_small": w_small, "w_gate": w_gate, "convw": convw, "hp": hp, "gcol": gcol,
              "gvec": gvec, "w_og": np.ascontiguousarray(np.asarray(w_o_gdn, f32)[0]),
              "w_of": np.ascontiguousarray(np.asarray(w_o_fox, f32)[0]), "w_out": np.ascontiguousarray(np.asarray(w_out, f32)[0]),
              "w_up": np.ascontiguousarray(np.asarray(w_up, f32)[0]), "w_down": np.ascontiguousarray(np.asarray(w_down, f32)[0]),
              "cbf": cbf, "cf": cf}
    meta = np.asarray(meta_tokens, f32)
    in_maps = []
    for c in range(8):
        b, tq = c // 4, c % 4
        nreal = 16 + (tq + 1) * 1024
        xs = np.zeros((NT * 128, 2048), f32)
        xs[NT * 128 - nreal: NT * 128 - nreal + 16] = meta
        xs[NT * 128 - nreal + 16:] = x[b, :(tq + 1) * 1024]
        vmk = np.zeros((NT * 128,), f32)
        vmk[NT * 128 - nreal:] = 1.0
        m = dict(shared)
        m["xs"] = xs
        m["vmask"] = np.ascontiguousarray(vmk.reshape(NT, 128).T)
        in_maps.append(m)
    if _CACHE.get("dbg_cores"):
        cs_ = _CACHE["dbg_cores"]
        return run_bass_kernel_spmd(nc, [in_maps[c] for c in cs_], core_ids=list(range(len(cs_))), trace=bool(_CACHE.get("trace")))
    res = run_bass_kernel_spmd(nc, in_maps, core_ids=list(range(8)))
    _CACHE["res"] = res
    outp = np.zeros((2, 4096, 2048), f32)
    for c in range(8):
        b, tq = c // 4, c % 4
        outp[b, tq * 1024:(tq + 1) * 1024] = np.asarray(res.results[c]["out"], f32)
    return outp
```

```python
import numpy as np
from contextlib import ExitStack
import concourse.bass as bass
import concourse.mybir as mybir
from concourse.bass_utils import run_bass_kernel_spmd

F32 = mybir.dt.float32
BF16 = mybir.dt.bfloat16
AF = mybir.ActivationFunctionType
ALU = mybir.AluOpType

NT = 33
OWN0 = 25
NH = 16
EPS = 1e-6
BLKS = [(0, 1)] + [(1 + 4 * i, 4) for i in range(8)]
EPOCH = 8192
NEG = -30000.0
DEBUG = False
INLINE_WAITS = True


class _Stop(Exception):
    pass


STG = {"upto": "D", "nheads": NH, "gdn_c": True, "fox_c": True, "gdn_i": True, "fox_i": True}


class Buf:
    __slots__ = ("name", "t", "w", "r", "x")

    def __init__(self, name, t, excl=False):
        self.name = name
        self.t = t
        self.w = None
        self.r = {}
        self.x = excl


class Prog:
    CE = ("pe", "dve", "act", "pool", "sp")

    def __init__(self, nc, es, need=None):
        self.nc = nc
        self.es = es
        self.need = need
        self.used = set()
        self.eng = {"pe": nc.tensor, "dve": nc.vector, "act": nc.scalar, "pool": nc.gpsimd, "sp": nc.sync}
        self.n = 0
        self.info = {}
        self.sigc = {e: 0 for e in self.CE}
        self.esem = {e: [] for e in self.CE}
        self.last = {e: None for e in self.CE}
        self.waited = {e: {} for e in self.CE}
        self.widx = {e: {s: -1 for s in self.CE} for e in self.CE}
        self.dsem = {}
        self.dpos = {}
        self.dval = {}
        self.dlast = {}
        self.nsem = 0
        for q in ("sp", "pool", "act"):
            self.dsem[q] = [self._newsem(f"d{q}{i}") for i in range(12)]
            self.dpos[q] = 0

    def _newsem(self, name):
        self.nsem += 1
        return self.es.enter_context(self.nc.semaphore(name))

    def _esem(self, e, idx):
        ep = idx // EPOCH
        while len(self.esem[e]) <= ep:
            self.esem[e].append(self._newsem(f"e{e}{len(self.esem[e])}"))
        return self.esem[e][ep]

    def _wait(self, E, d, raw, isdma):
        inf = self.info.get(d)
        if inf is None:
            return
        if inf[0] == "c":
            _, src, order = inf
            if src == E and not isdma and E == "pe":
                return
            if self.need is not None and d not in self.need:
                return
            if self.widx[E][src] >= order:
                return
            self.used.add(d)
            sidx = self.sigmap[d]
            self._emit_wait(E, self._esem(src, sidx), sidx % EPOCH + 1)
            self.widx[E][src] = order
        else:
            _, sem, val, key = inf
            if self.waited[E].get(key, 0) >= val:
                return
            self._emit_wait(E, sem, val)
            self.waited[E][key] = val

    defer = None

    def _emit_wait(self, E, sem, val):
        if self.defer is not None:
            self.defer.append((sem, val))
        else:
            self.eng[E].wait_ge(sem, val)

    sigmap = {}

    def _deps(self, E, reads, writes, isdma):
        for b in reads:
            if b.w is not None:
                self._wait(E, b.w, True, isdma)
        for b in writes:
            if b.w is not None:
                self._wait(E, b.w, False, isdma)
            for d in b.r.values():
                self._wait(E, d, False, isdma)

    def _mark(self, i, key, reads, writes):
        for b in reads:
            b.r[key] = i
        for b in writes:
            b.w = i
            b.r = {}

    def op(self, E, fn, reads=(), writes=()):
        xr = [b for b in reads if b.x]
        if xr:
            reads = [b for b in reads if not b.x]
            writes = list(writes) + [b for b in xr if b not in writes]
        if INLINE_WAITS:
            if E == "pe" and reads:
                self._deps(E, reads[:1], (), False)
                rest_r = reads[1:]
            else:
                rest_r = reads
            self.defer = []
            self._deps(E, rest_r, writes, False)
            pend, self.defer = self.defer, None
            for (sm_, vl_) in pend[:-1]:
                self.eng[E].wait_ge(sm_, vl_)
            i = self.n
            self.n += 1
            ins = fn(self.eng[E])
            if pend:
                ins._wait_ge(pend[-1][0], pend[-1][1])
        else:
            self._deps(E, reads, writes, False)
            i = self.n
            self.n += 1
            ins = fn(self.eng[E])
        if self.need is None or i in self.need:
            sidx = self.sigc[E]
            self.sigc[E] += 1
            ins.then_inc(self._esem(E, sidx), 1)
            self.sigmap[i] = sidx
        self.info[i] = ("c", E, i)
        self.last[E] = i
        self._mark(i, E, reads, writes)
        return i

    def mm(self, out, lhsT, rhs, start, stop, reads, writes):
        return self.op("pe", lambda e: e.matmul(out, lhsT=lhsT, rhs=rhs, start=start, stop=stop),
                       reads=reads, writes=writes)

    def tr(self, out, in_, ident, reads, writes):
        return self.op("pe", lambda e: e.transpose(out=out, in_=in_, identity=ident), reads=reads, writes=writes)

    def dma(self, q, out, in_, reads=(), writes=()):
        self._deps(q, reads, writes, True)
        k = self.dpos[q]
        self.dpos[q] = (k + 1) % len(self.dsem[q])
        sem = self.dsem[q][k]
        key = (q, k)
        if key in self.dlast:
            self._wait(q, self.dlast[key], False, True)
        val = self.dval.get(key, 0) + 16
        self.dval[key] = val
        i = self.n
        self.n += 1
        self.eng[q].dma_start(out=out, in_=in_).then_inc(sem, 16)
        self.info[i] = ("d", sem, val, key)
        self.dlast[key] = i
        self._mark(i, ("d", q, k), reads, writes)
        return i

    def barrier(self):
        lasts = [self.last[e] for e in self.CE if self.last[e] is not None]
        dl = list(self.dlast.values())
        for E in self.CE:
            for d in lasts:
                inf = self.info[d]
                if inf[1] == E:
                    continue
                self._wait(E, d, True, True)
            for d in dl:
                self._wait(E, d, True, True)


def build(need=None):
    nc = bass.Bass("TRN2", target_bir_lowering=False)
    dt = lambda n, s, d, k: nc.dram_tensor(n, s, d, kind=k).ap()
    xs = dt("xs", [NT * 128, 2048], F32, "ExternalInput")
    vmask = dt("vmask", [128, NT], F32, "ExternalInput")
    w_head = dt("w_head", [2048, NH * 896], F32, "ExternalInput")
    w_small = dt("w_small", [2048, 48], F32, "ExternalInput")
    w_gate = dt("w_gate", [2048, 4096], F32, "ExternalInput")
    convw = dt("convw", [128, NH * 12], F32, "ExternalInput")
    hp = dt("hp", [128, 48], F32, "ExternalInput")
    gcol = dt("gcol", [128, 3], F32, "ExternalInput")
    gvec = dt("gvec", [128, 3 * 2048], F32, "ExternalInput")
    w_og = dt("w_og", [2048, 2048], F32, "ExternalInput")
    w_of = dt("w_of", [2048, 2048], F32, "ExternalInput")
    w_out = dt("w_out", [2048, 2048], F32, "ExternalInput")
    w_up = dt("w_up", [2048, 8192], F32, "ExternalInput")
    w_down = dt("w_down", [8192, 2048], F32, "ExternalInput")
    cbf_in = dt("cbf", [128, 3712], F32, "ExternalInput")
    cf_in = dt("cf", [128, 512], F32, "ExternalInput")
    out = dt("out", [1024, 2048], F32, "ExternalOutput")
    dbg = dt("dbg", [128, 8192], F32, "ExternalOutput") if DEBUG else None
    stash = dt("stash", [9, 128, 16, 512], BF16, "Internal")
    ycat = dt("ycat", [32, 128, 1024], BF16, "Internal")
    h2d = dt("h2d", [8, 128, 2048], F32, "Internal")

    with ExitStack() as es:
        P = Prog(nc, es, need)
        Prog.sigmap = {}
        P.sigmap = {}

        uid = [0]

        def sb(es_, name, shape, dtype):
            uid[0] += 1
            return Buf(name, es_.enter_context(nc.sbuf_tensor(f"s{uid[0]}_{name}", shape, dtype)))

        def ps(es_, name, shape, dtype):
            uid[0] += 1
            return Buf(name, es_.enter_context(nc.psum_tensor(f"p{uid[0]}_{name}", shape, dtype)), excl=True)

        cbf = sb(es, "cbf", [128, 3712], BF16)
        cf = sb(es, "cf", [128, 512], F32)
        P.dma("pool", cbf.t[:], cbf_in[:, :], writes=[cbf])
        P.dma("sp", cf.t[:], cf_in[:, :], writes=[cf])
        identb = cbf.t[:, 0:128]
        onesb = cbf.t[:, 128:256]
        maskT4 = cbf.t[:, 256:768]
        maskG = lambda r: cbf.t[:, 768 + r * 512: 768 + (r + 1) * 512]
        lvl = lambda l: cbf.t[:, 2816 + l * 128: 2816 + (l + 1) * 128]
        identf = cf.t[:, 0:128]
        utri = cf.t[:, 128:256]
        slow = cf.t[:, 256:384]
        onesf = cf.t[:, 384:512]

        gct = sb(es, "gct", [128, 4], F32)
        esB = ExitStack()
        vm = sb(esB, "vm", [128, NT], F32)
        hpt = sb(esB, "hpt", [128, 48], F32)
        cwt = sb(esB, "cwt", [128, NH * 12], F32)
        P.dma("sp", vm.t[:], vmask[:, :], writes=[vm])
        P.dma("sp", hpt.t[:], hp[:, :], writes=[hpt])
        P.dma("sp", gct.t[:, 0:3], gcol[:, :], writes=[gct])
        P.dma("sp", cwt.t[:], convw[:, :], writes=[cwt])
        P.op("dve", lambda e: e.tensor_scalar(out=gct.t[:, 3:4], in0=gct.t[:, 1:2], scalar1=float(128 ** -0.5),
                                              scalar2=None, op0=ALU.mult), reads=[gct], writes=[gct])
        sm = sb(esB, "sm", [128, NT, 48], F32)
        S3 = [128, NT, NH]
        beta = sb(esB, "beta", S3, F32)
        nbeta = sb(esB, "nbeta", S3, F32)
        gg = sb(esB, "gg", S3, F32)
        egc = sb(esB, "egc", S3, F32)
        ekd = sb(esB, "ekd", S3, F32)
        egl = sb(esB, "egl", S3, F32)
        fb = sb(esB, "fb", [128, NH, 2, NT], F32)
        wsm = sb(esB, "wsm", [128, 16, 48], BF16)
        P.dma("pool", wsm.t[:], w_small.rearrange("(kc p) c -> p kc c", p=128), writes=[wsm])
        wh = sb(esB, "wh", [128, 16, 896], BF16)
        if STG["nheads"] > 0:
            P.dma("pool", wh.t[:], w_head[:, 0:896].rearrange("(kc p) c -> p kc c", p=128), writes=[wh])

        try:
            with ExitStack() as ea:
                gv0 = sb(ea, "gv0", [128, 2048], F32)
                P.dma("sp", gv0.t[:], gvec[:, 0:2048], writes=[gv0])
                xts = [sb(ea, f"xt{i}", [128, 2048], F32) for i in range(3)]
                sqj = sb(ea, "sqj", [128, 2048], BF16)
                ub = [sb(ea, f"ub{i}", [128, 2048], BF16) for i in range(2)]
                ssA = [sb(ea, f"ssA{i}", [128, 2], F32) for i in range(3)]
                uTb = [sb(ea, f"uTbA{i}", [128, 16, 512], BF16) for i in range(2)]
                tp = [ps(ea, f"tp{i}", [128, 1024], BF16) for i in range(4)]
                psS = [ps(ea, f"psS{i}", [128, 512], F32) for i in range(2)]
                tiles = [(bi, t0, ntl, ti) for bi, (t0, ntl) in enumerate(BLKS) for ti in range(ntl)]

                def stage1(t):
                    xt, ss = xts[t % 3], ssA[t % 3]
                    P.dma("sp", xt.t[:], xs[t * 128:(t + 1) * 128, :], writes=[xt])
                    P.op("act", lambda e: e.activation(out=sqj.t[:], in_=xt.t[:], func=AF.Square,
                                                       accum_out=ss.t[:, 0:1]), reads=[xt], writes=[sqj, ss])
                    P.op("act", lambda e: e.activation(out=ss.t[:, 1:2], in_=ss.t[:, 0:1], func=AF.Ln,
                                                       bias=EPS, scale=1.0 / 2048), reads=[ss], writes=[ss])
                    P.op("act", lambda e: e.activation(out=ss.t[:, 0:1], in_=ss.t[:, 1:2], func=AF.Exp,
                                                       scale=-0.5), reads=[ss], writes=[ss])

                def stage2(bi, t0, ntl, ti):
                    t = t0 + ti
                    uT = uTb[bi % 2]
                    pS = psS[bi % 2]
                    xt, ss, u1 = xts[t % 3], ssA[t % 3], ub[t % 2]
                    P.op("dve", lambda e: e.scalar_tensor_tensor(out=u1.t[:], in0=xt.t[:], scalar=ss.t[:, 0:1],
                                                                 in1=gv0.t[:], op0=ALU.mult, op1=ALU.mult),
                         reads=[xt, ss, gv0], writes=[u1])
                    for half in range(2):
                        tph = tp[(t % 2) * 2 + half]
                        for k8 in range(8):
                            kc = half * 8 + k8
                            P.tr(tph.t[:, k8 * 128:(k8 + 1) * 128], u1.t[:, kc * 128:(kc + 1) * 128], identb,
                                 reads=[u1, cbf], writes=[tph])
                        src = tph.t[:, :].rearrange("p (k c) -> p k c", k=8)
                        dst = uT.t[:, half * 8:(half + 1) * 8, ti * 128:(ti + 1) * 128]
                        if half == 0:
                            P.op("act", lambda e: e.activation(out=dst, in_=src, func=AF.Copy), reads=[tph], writes=[uT])
                        else:
                            P.op("dve", lambda e: e.tensor_copy(out=dst, in_=src), reads=[tph], writes=[uT])
                    for kc in range(16):
                        P.mm(pS.t[:, ti * 48:(ti + 1) * 48], lhsT=uT.t[:, kc, ti * 128:(ti + 1) * 128],
                             rhs=wsm.t[:, kc, :], start=(kc == 0), stop=(kc == 15), reads=[uT, wsm], writes=[pS])
                    if ti == ntl - 1:
                        P.op("dve", lambda e: e.tensor_copy(
                            out=sm.t[:, t0:t0 + ntl, :], in_=pS.t[:, 0:ntl * 48].rearrange("p (t c) -> p t c", t=ntl)),
                            reads=[pS], writes=[sm])
                        P.dma("sp", stash[bi, :, :, 0:ntl * 128], uT.t[:, :, 0:ntl * 128], reads=[uT], writes=[])

                stage1(0)
                for i_, (bi, t0, ntl, ti) in enumerate(tiles):
                    if i_ + 1 < len(tiles):
                        stage1(i_ + 1)
                    stage2(bi, t0, ntl, ti)
                P.barrier()
            if STG["upto"] == "A":
                raise _Stop()

            with ExitStack() as e0:
                t1 = sb(e0, "t1", S3, F32)
                t2 = sb(e0, "t2", S3, F32)
                lf = sb(e0, "lf", S3, F32)
                gc = sb(e0, "gc", S3, F32)
                cc = sb(e0, "cc", S3, F32)
                pre = sb(e0, "pre", S3, F32)
                tot = sb(e0, "tot", S3, F32)
                nega = sb(e0, "nega", [128, 16], F32)
                pq = [ps(e0, f"pq{i}", [128, 512], F32) for i in range(2)]
                vmb = vm.t[:, :].unsqueeze(2).to_broadcast(S3)
                hb = lambda a: hpt.t[:, a:a + 16].unsqueeze(1).to_broadcast(S3)
                A = lambda b_, lo: b_.t[:, :, lo:lo + 16]
                P.op("act", lambda e: e.activation(out=t1.t[:], in_=A(sm, 0), func=AF.Exp, scale=-1.0), reads=[sm], writes=[t1])
                P.op("dve", lambda e: e.tensor_scalar(out=t1.t[:], in0=t1.t[:], scalar1=1.0, scalar2=None, op0=ALU.add), reads=[t1], writes=[t1])
                P.op("dve", lambda e: e.reciprocal(out=t1.t[:], in_=t1.t[:]), reads=[t1], writes=[t1])
                P.op("dve", lambda e: e.tensor_tensor(out=beta.t[:], in0=t1.t[:], in1=vmb, op=ALU.mult), reads=[t1, vm], writes=[beta])
                P.op("dve", lambda e: e.tensor_scalar(out=nbeta.t[:], in0=beta.t[:], scalar1=-1.0, scalar2=None, op0=ALU.mult), reads=[beta], writes=[nbeta])
                P.op("act", lambda e: e.activation(out=nega.t[:], in_=hpt.t[:, 0:16], func=AF.Exp), reads=[hpt], writes=[nega])
                P.op("dve", lambda e: e.tensor_scalar(out=nega.t[:], in0=nega.t[:], scalar1=-1.0, scalar2=None, op0=ALU.mult), reads=[nega], writes=[nega])
                P.op("dve", lambda e: e.tensor_tensor(out=t2.t[:], in0=A(sm, 16), in1=hb(16), op=ALU.add), reads=[sm, hpt], writes=[t2])
                P.op("act", lambda e: e.activation(out=t2.t[:], in_=t2.t[:], func=AF.Exp), reads=[t2], writes=[t2])
                P.op("act", lambda e: e.activation(out=t2.t[:], in_=t2.t[:], func=AF.Ln, bias=1.0, scale=1.0), reads=[t2], writes=[t2])
                P.op("dve", lambda e: e.tensor_tensor(out=t2.t[:], in0=t2.t[:], in1=nega.t[:, :].unsqueeze(1).to_broadcast(S3), op=ALU.mult), reads=[t2, nega], writes=[t2])
                P.op("dve", lambda e: e.tensor_tensor(out=gg.t[:], in0=t2.t[:], in1=vmb, op=ALU.mult), reads=[t2, vm], writes=[gg])
                P.op("dve", lambda e: e.tensor_tensor(out=t1.t[:], in0=A(sm, 32), in1=hb(32), op=ALU.add), reads=[sm, hpt], writes=[t1])
                P.op("act", lambda e: e.activation(out=t1.t[:], in_=t1.t[:], func=AF.Exp, scale=-1.0), reads=[t1], writes=[t1])
                P.op("act", lambda e: e.activation(out=t1.t[:], in_=t1.t[:], func=AF.Ln, bias=1.0, scale=1.0), reads=[t1], writes=[t1])
                P.op("dve", lambda e: e.scalar_tensor_tensor(out=lf.t[:], in0=t1.t[:], scalar=-1.0, in1=vmb, op0=ALU.mult, op1=ALU.mult), reads=[t1, vm], writes=[lf])

                fl = lambda b_: b_.t[:, :, :].rearrange("p t h -> p (t h)")

                def colmm(lhsT, src, dst, func=None):
                    for pi in range(3):
                        pp = pq[pi % 2]
                        sl = slice(pi * 176, (pi + 1) * 176)
                        P.mm(pp.t[:, 0:176], lhsT=lhsT, rhs=fl(src)[:, sl], start=True, stop=True, reads=[cf, src], writes=[pp])
                        if func is None:
                            P.op("dve", lambda e: e.tensor_copy(out=fl(dst)[:, sl], in_=pp.t[:, 0:176]), reads=[pp], writes=[dst])
                        else:
                            P.op("act", lambda e: e.activation(out=fl(dst)[:, sl], in_=pp.t[:, 0:176], func=func), reads=[pp], writes=[dst])

                colmm(utri, gg, gc)
                colmm(onesf, gg, egl, AF.Exp)
                P.op("act", lambda e: e.activation(out=egc.t[:], in_=gc.t[:], func=AF.Exp), reads=[gc], writes=[egc])
                colmm(onesf, gg, t2)
                P.op("dve", lambda e: e.tensor_tensor(out=t2.t[:], in0=t2.t[:], in1=gc.t[:], op=ALU.subtract), reads=[t2, gc], writes=[t2])
                P.op("act", lambda e: e.activation(out=ekd.t[:], in_=t2.t[:], func=AF.Exp), reads=[t2], writes=[ekd])
                colmm(utri, lf, cc)
                colmm(onesf, lf, tot)
                P.op("dve", lambda e: e.memset(pre.t[:, 0, :], 0.0), writes=[pre])
                for t in range(1, NT):
                    P.op("dve", lambda e: e.tensor_tensor(out=pre.t[:, t, :], in0=pre.t[:, t - 1, :], in1=tot.t[:, t - 1, :],
                                                          op=ALU.add), reads=[pre, tot], writes=[pre])
                P.op("dve", lambda e: e.tensor_tensor(out=cc.t[:], in0=cc.t[:], in1=pre.t[:], op=ALU.add), reads=[cc, pre], writes=[cc])
                P.op("dve", lambda e: e.tensor_scalar(out=t1.t[:], in0=vmb, scalar1=-1.0, scalar2=-NEG, op0=ALU.add, op1=ALU.mult), reads=[vm], writes=[t1])
                P.op("dve", lambda e: e.tensor_tensor(out=t1.t[:], in0=t1.t[:], in1=cc.t[:], op=ALU.subtract), reads=[t1, cc], writes=[t1])
                for h in range(NH):
                    for G in range(2):
                        tref = 27 + 4 * G
                        P.op("dve", lambda e: e.tensor_scalar(out=fb.t[:, h, G, :], in0=t1.t[:, :, h], scalar1=pre.t[:, tref, h:h + 1],
                                                              scalar2=None, op0=ALU.add), reads=[t1, pre], writes=[fb])
                if DEBUG:
                    P.dma("sp", dbg[:, 0:528], fl(beta), reads=[beta])
                    P.dma("sp", dbg[:, 528:1056], fl(gg), reads=[gg])
                    P.dma("sp", dbg[:, 1056:1584], fl(cc), reads=[cc])
                P.barrier()

            if STG["upto"] == "S":
                raise _Stop()
            def norm_block(es_unused, src, ntok, dstap, dstbuf, a, scalar, tmp):
                sqb, lnb, rsb, psN = tmp
                P.op("dve", lambda e: e.tensor_tensor(out=sqb.t[:, 0:ntok], in0=src.t[:, 0:ntok], in1=src.t[:, 0:ntok], op=ALU.mult),
                     reads=[src], writes=[sqb])
                P.mm(psN.t[:, 0:ntok], lhsT=onesb, rhs=sqb.t[:, 0:ntok], start=True, stop=True, reads=[cbf, sqb], writes=[psN])
                P.op("act", lambda e: e.activation(out=lnb.t[:, 0:ntok], in_=psN.t[:, 0:ntok], func=AF.Ln, bias=EPS, scale=a),
                     reads=[psN], writes=[lnb])
                P.op("act", lambda e: e.activation(out=rsb.t[:, 0:ntok], in_=lnb.t[:, 0:ntok], func=AF.Exp, scale=-0.5),
                     reads=[lnb], writes=[rsb])
                P.op("dve", lambda e: e.scalar_tensor_tensor(out=dstap, in0=src.t[:, 0:ntok], scalar=scalar, in1=rsb.t[:, 0:ntok],
                                                             op0=ALU.mult, op1=ALU.mult), reads=[src, rsb, gct], writes=[dstbuf])

            with ExitStack() as eb:
                uTb = [sb(eb, f"uTbB{i}", [128, 16, 512], BF16) for i in range(2)]
                pref_done = [False]
                yblk = [sb(eb, f"yblk{i}", [128, 512], BF16) for i in range(2)]
                for h in range(STG["nheads"]):
                    with ExitStack() as eg:
                        gqT = sb(eg, "gqT", [128, 1024], BF16)
                        gkT = sb(eg, "gkT", [128, NT * 128], BF16)
                        szT = sb(eg, "szT", [128, 1024], BF16)
                        gv = sb(eg, "gv", [128, NT, 128], BF16)
                        fqT = sb(eg, "fqT", [128, 1024], BF16)
                        fkT = sb(eg, "fkT", [128, NT * 128], BF16)
                        fvt = sb(eg, "fvt", [128, NT, 128], BF16)
                        with ExitStack() as ei:
                            psA = [ps(ei, f"psA{i}", [128, 512], F32) for i in range(3)]
                            psNl = [ps(ei, f"psN{i}", [128, 512], F32) for i in range(2)]
                            pstl = [ps(ei, f"pst{i}", [128, 1024], BF16) for i in range(2)]
                            rb = [[sb(ei, f"rb{c}{i}", [128, 515], F32) for i in range(2)] for c in range(3)]
                            accl = [sb(ei, f"acc{i}", [128, 512], F32) for i in range(5)]
                            csl = [sb(ei, f"cs{i}", [128, 512], BF16) for i in range(16)]
                            sql = [sb(ei, f"sqb{i}", [128, 512], BF16) for i in range(6)]
                            lnl = [sb(ei, f"lnb{i}", [128, 512], F32) for i in range(6)]
                            rsl = [sb(ei, f"rsb{i}", [128, 512], F32) for i in range(6)]
                            cnt = {"cs": 0, "acc": 0, "n": 0, "pn": 0, "pt": 0, "sq": 0}

                            def take(kind, pool):
                                i_ = cnt[kind]
                                cnt[kind] += 1
                                return pool[i_ % len(pool)]

                            def norm_gen(cs, ntok, dstap, dstbuf, a, scalar):
                                sqb = take("sq", sql)
                                P.op("dve", lambda e: e.tensor_tensor(out=sqb.t[:, 0:ntok], in0=cs.t[:, 0:ntok], in1=cs.t[:, 0:ntok], op=ALU.mult),
                                     reads=[cs], writes=[sqb])
                                yield
                                psN = take("pn", psNl)
                                lnb = take("n", lnl)
                                rsb = rsl[(cnt["n"] - 1) % len(rsl)]
                                P.mm(psN.t[:, 0:ntok], lhsT=onesb, rhs=sqb.t[:, 0:ntok], start=True, stop=True, reads=[cbf, sqb], writes=[psN])
                                P.op("act", lambda e: e.activation(out=lnb.t[:, 0:ntok], in_=psN.t[:, 0:ntok], func=AF.Ln, bias=EPS, scale=a),
                                     reads=[psN], writes=[lnb])
                                P.op("act", lambda e: e.activation(out=rsb.t[:, 0:ntok], in_=lnb.t[:, 0:ntok], func=AF.Exp, scale=-0.5),
                                     reads=[lnb], writes=[rsb])
                                yield
                                yield
                                P.op("dve", lambda e: e.scalar_tensor_tensor(out=dstap, in0=cs.t[:, 0:ntok], scalar=scalar, in1=rsb.t[:, 0:ntok],
                                                                             op0=ALU.mult, op1=ALU.mult), reads=[cs, rsb, gct], writes=[dstbuf])

                            def tr_gen(cs, ntl, ntok, dst, t0, eng):
                                pst = take("pt", pstl)
                                for ti in range(ntl):
                                    P.tr(pst.t[:, ti * 128:(ti + 1) * 128], cs.t[:, ti * 128:(ti + 1) * 128], identb, reads=[cs, cbf], writes=[pst])
                                src = pst.t[:, 0:ntok].rearrange("p (t c) -> p t c", t=ntl)
                                if eng == "act":
                                    P.op("act", lambda e: e.activation(out=dst.t[:, t0:t0 + ntl, :], in_=src, func=AF.Copy), reads=[pst], writes=[dst])
                                else:
                                    P.op("dve", lambda e: e.tensor_copy(out=dst.t[:, t0:t0 + ntl, :], in_=src), reads=[pst], writes=[dst])
                                return
                                yield

                            def post_stream(ct, bi, t0, ntl, ntok, pA):
                                if STG.get("ip_nopost"):
                                    cs0 = take("cs", csl)
                                    P.op("dve" if ct % 2 else "act", (lambda e: e.tensor_copy(out=cs0.t[:, 0:ntok], in_=pA.t[:, 0:ntok])) if ct % 2 else (lambda e: e.activation(out=cs0.t[:, 0:ntok], in_=pA.t[:, 0:ntok], func=AF.Copy)), reads=[pA], writes=[cs0])
                                    return
                                if ct == 3:
                                    P.op("act", lambda e: e.activation(out=szT.t[:, (bi - 7) * 512:(bi - 6) * 512], in_=pA.t[:, :], func=AF.Silu),
                                         reads=[pA], writes=[szT])
                                    return
                                cs = take("cs", csl)
                                if ct == 6:
                                    P.op("act", lambda e: e.activation(out=cs.t[:, 0:ntok], in_=pA.t[:, 0:ntok], func=AF.Copy), reads=[pA], writes=[cs])
                                    yield
                                    yield from tr_gen(cs, ntl, ntok, fvt, t0, "dve")
                                    return
                                if ct >= 4:
                                    P.op("dve", lambda e: e.tensor_copy(out=cs.t[:, 0:ntok], in_=pA.t[:, 0:ntok]), reads=[pA], writes=[cs])
                                    if ct == 4:
                                        yield from norm_gen(cs, ntok, fqT.t[:, (bi - 7) * 512:(bi - 6) * 512], fqT, 1.0 / 128, gct.t[:, 3:4])
                                    else:
                                        yield from norm_gen(cs, ntok, fkT.t[:, t0 * 128: t0 * 128 + ntok], fkT, 1.0 / 128, gct.t[:, 2:3])
                                    return
                                r = rb[ct][bi % 2]
                                rp = rb[ct][(bi + 1) % 2]
                                P.op("dve", lambda e: e.tensor_copy(out=r.t[:, 3:3 + ntok], in_=pA.t[:, 0:ntok]), reads=[pA], writes=[r])
                                if bi == 0:
                                    P.op("pool", lambda e: e.memset(r.t[:, 0:3], 0.0), writes=[r])
                                elif ct == 0 and bi == 7:
                                    uTp = uTb[(bi + 1) % 2]
                                    pN = take("pn", psNl)
                                    for kc in range(16):
                                        P.mm(pN.t[:, 0:3], lhsT=wh.t[:, kc, 0:128], rhs=uTp.t[:, kc, 509:512],
                                             start=(kc == 0), stop=(kc == 15), reads=[wh, uTp], writes=[pN])
                                    P.op("dve", lambda e: e.tensor_copy(out=r.t[:, 0:3], in_=pN.t[:, 0:3]), reads=[pN], writes=[r])
                                else:
                                    pn = BLKS[bi - 1][1] * 128
                                    P.op("pool", lambda e: e.tensor_copy(out=r.t[:, 0:3], in_=rp.t[:, pn:pn + 3]), reads=[rp], writes=[r])
                                yield
                                acc = take("acc", accl)
                                cw = lambda kk: cwt.t[:, h * 12 + ct * 4 + kk: h * 12 + ct * 4 + kk + 1]
                                P.op("dve", lambda e: e.tensor_scalar(out=acc.t[:, 0:ntok], in0=r.t[:, 0:ntok], scalar1=cw(0), scalar2=None,
                                                                      op0=ALU.mult), reads=[r, cwt], writes=[acc])
                                for kk in range(1, 4):
                                    P.op("dve", lambda e: e.scalar_tensor_tensor(out=acc.t[:, 0:ntok], in0=r.t[:, kk:kk + ntok], scalar=cw(kk),
                                                                                 in1=acc.t[:, 0:ntok], op0=ALU.mult, op1=ALU.add),
                                         reads=[r, cwt, acc], writes=[acc])
                                yield
                                P.op("act", lambda e: e.activation(out=cs.t[:, 0:ntok], in_=acc.t[:, 0:ntok], func=AF.Silu), reads=[acc], writes=[cs])
                                yield
                                yield
                                if ct == 0:
                                    yield from norm_gen(cs, ntok, gqT.t[:, (bi - 7) * 512:(bi - 6) * 512], gqT, 1.0, float(128 ** -0.5))
                                elif ct == 1:
                                    yield from norm_gen(cs, ntok, gkT.t[:, t0 * 128: t0 * 128 + ntok], gkT, 1.0, 1.0)
                                else:
                                    yield from tr_gen(cs, ntl, ntok, gv, t0, "act")

                            active = []

                            def advance():
                                for g_ in list(active):
                                    try:
                                        next(g_)
                                    except StopIteration:
                                        active.remove(g_)

                            k = 0
                            for bi, (t0, ntl) in enumerate(BLKS if STG["gdn_i"] else []):
                                ntok = ntl * 128
                                uT = uTb[bi % 2]
                                if not (bi < 2 and pref_done[0]):
                                    P.dma("sp", uT.t[:, :, 0:ntok], stash[bi, :, :, 0:ntok], writes=[uT])
                                own = bi >= 7
                                for ct in ([0, 1, 2, 3, 4, 5, 6] if own else [1, 2, 5, 6]):
                                    pA = psA[k % 3]
                                    k += 1
                                    for kc in range(16):
                                        P.mm(pA.t[:, 0:ntok], lhsT=wh.t[:, kc, ct * 128:(ct + 1) * 128], rhs=uT.t[:, kc, 0:ntok],
                                             start=(kc == 0), stop=(kc == 15), reads=[wh, uT], writes=[pA])
                                    active.append(post_stream(ct, bi, t0, ntl, ntok, pA))
                                    advance()
                            while active:
                                advance()
                            P.barrier()
                        if h + 1 < STG["nheads"]:
                            P.dma("pool", wh.t[:], w_head[:, (h + 1) * 896:(h + 2) * 896].rearrange("(kc p) c -> p kc c", p=128), writes=[wh])
                        if DEBUG and h == 0:
                            with ExitStack() as ed:
                                d32 = sb(ed, "d32", [128, 4224], F32)
                                P.op("dve", lambda e: e.tensor_copy(out=d32.t[:], in_=gkT.t[:]), reads=[gkT], writes=[d32])
                                P.dma("sp", dbg[:, 1584:1584 + 4224], d32.t[:], reads=[d32])
                                P.barrier()
                        with ExitStack() as ec:
                            bankB = [Buf(f"gbank{i}", ps(ec, f"gbank{i}", [128, 512], F32).t, excl=True) for i in range(6)]
                            psq = ps(ec, "psq", [128, 512], F32).t
                            psWS = psO = psdS = Buf("psqB", psq, excl=True)
                            psTt = ps(ec, "psTt", [128, 1024], BF16).t
                            psTB = Buf("psTtB", psTt, excl=True)

                            class Slot:
                                pass

                            slots = []
                            NSL = 3
                            for si in range(NSL):
                                S_ = Slot()
                                S_.bD = S_.bY = S_.bK = S_.bR = bankB[2 * si]
                                S_.bQ = S_.bRT = bankB[2 * si + 1]
                                S_.D = S_.bD.t[:, 0:256]
                                S_.Y = S_.bD.t[:, 0:256]
                                S_.K = S_.bD.t[:, 256:512]
                                S_.R = S_.bD.t[:, 256:512]
                                S_.Q = S_.bQ.t[:, 0:256]
                                S_.RT = S_.bQ.t[:, 0:256]
                                S_.tp = psTt[:, si * 256:(si + 1) * 256]
                                S_.Gt = sb(ec, f"Gt{si}", [128, 2, 128], F32)
                                S_.ET = sb(ec, f"ET{si}", [128, 256], BF16)
                                S_.ApT = sb(ec, f"ApT{si}", [128, 2, 128], BF16)
                                S_.Mt = [sb(ec, f"Mt{si}{l}", [128, 2, 128], BF16) for l in range(7)]
                                S_.TT = [sb(ec, f"TT{si}{i}", [128, 2, 128], BF16) for i in range(2)]
                                S_.Tm = [sb(ec, f"Tm{si}{i}", [128, 2, 128], BF16) for i in range(2)]
                                S_.Yb = sb(ec, f"Yb{si}", [128, 2, 128], BF16)
                                S_.Kg = sb(ec, f"Kg{si}", [128, 2, 128], BF16)
                                S_.dg = sb(ec, f"dg{si}", [128, 2, 128], BF16)
                                slots.append(S_)
                            outs = []
                            for oi in range(2 * NSL):
                                O_ = Slot()
                                O_.QKm = sb(ec, f"QKm{oi}", [128, 256], BF16)
                                O_.QdT = sb(ec, f"QdT{oi}", [128, 256], BF16)
                                O_.WpT = sb(ec, f"WpT{oi}", [128, 256], BF16)
                                O_.Ubt = sb(ec, f"Ubt{oi}", [128, 2, 128], F32)
                                O_.Kd = sb(ec, f"Kd{oi}", [128, 2, 128], BF16)
                                outs.append(O_)
                            St = sb(ec, "St", [128, 128], F32)
                            Sbf = sb(ec, "Sbf", [128, 128], BF16)
                            vn = sb(ec, "vn", [128, 128], BF16)
                            oj = sb(ec, "oj", [128, 128], BF16)
                            oss = sb(ec, "oss", [128, 4], F32)
                            on4 = sb(ec, "on4", [128, 4, 128], BF16)
                            P.op("pool", lambda e: e.memset(St.t[:], 0.0), writes=[St])
                            P.op("pool", lambda e: e.memset(Sbf.t[:], 0.0), writes=[Sbf])
                            hsl = slice(h, h + 1)

                            def par_phase(t0, nb, S_, O_):
                                W = nb * 128
                                bc3 = [128, nb, 128]
                                own = t0 >= OWN0
                                qo = (t0 - OWN0) * 128
                                colb = lambda b_: b_.t[:, t0:t0 + nb, hsl].to_broadcast(bc3)
                                tk = lambda p: slice((t0 + p) * 128, (t0 + p + 1) * 128)
                                pc = lambda p: slice(p * 128, (p + 1) * 128)
                                f3 = lambda ap: ap[:, 0:W].rearrange("p (t c) -> p t c", t=nb)
                                P.op("dve", lambda e: e.tensor_tensor(out=S_.Gt.t[:, 0:nb, :], in0=colb(gg), in1=slow.unsqueeze(1).to_broadcast(bc3),
                                                                      op=ALU.mult), reads=[gg, cf], writes=[S_.Gt])
                                P.mm(S_.D[:, 0:W], lhsT=identb, rhs=maskT4[:, 0:W], start=True, stop=False, reads=[cbf], writes=[S_.bD])
                                for p in range(nb):
                                    P.mm(S_.D[:, pc(p)], lhsT=S_.Gt.t[:, p, :], rhs=utri, start=False, stop=(p == nb - 1), reads=[S_.Gt, cf], writes=[S_.bD])
                                for p in range(nb):
                                    P.mm(S_.K[:, pc(p)], lhsT=gkT.t[:, tk(p)], rhs=gkT.t[:, tk(p)], start=True, stop=True, reads=[gkT], writes=[S_.bK])
                                if own:
                                    for p in range(nb):
                                        P.mm(S_.Q[:, pc(p)], lhsT=gkT.t[:, tk(p)], rhs=gqT.t[:, qo + p * 128: qo + (p + 1) * 128],
                                             start=True, stop=True, reads=[gkT, gqT], writes=[S_.bQ])
                                yield
                                P.op("act", lambda e: e.activation(out=S_.ET.t[:, 0:W], in_=S_.D[:, 0:W], func=AF.Exp), reads=[S_.bD], writes=[S_.ET])
                                yield
                                for p in range(nb):
                                    P.op("dve", lambda e: e.scalar_tensor_tensor(out=S_.ApT.t[:, p, :], in0=S_.K[:, pc(p)], scalar=beta.t[:, t0 + p, hsl],
                                                                                 in1=S_.ET.t[:, pc(p)], op0=ALU.mult, op1=ALU.mult),
                                         reads=[S_.bK, beta, S_.ET], writes=[S_.ApT])
                                if own:
                                    P.op("dve", lambda e: e.tensor_tensor(out=O_.QKm.t[:, 0:W], in0=S_.Q[:, 0:W], in1=S_.ET.t[:, 0:W], op=ALU.mult),
                                         reads=[S_.bQ, S_.ET], writes=[O_.QKm])
                                yield
                                def mask_op(l):
                                    P.op("pool", lambda e: e.tensor_tensor(out=S_.Mt[l].t[:, 0:nb, :], in0=S_.ApT.t[:, 0:nb, :],
                                                                           in1=lvl(l).unsqueeze(1).to_broadcast(bc3), op=ALU.mult),
                                         reads=[S_.ApT, cbf], writes=[S_.Mt[l]])

                                P.op("dve", lambda e: e.tensor_tensor(out=S_.Mt[0].t[:, 0:nb, :], in0=S_.ApT.t[:, 0:nb, :],
                                                                      in1=lvl(0).unsqueeze(1).to_broadcast(bc3), op=ALU.mult),
                                     reads=[S_.ApT, cbf], writes=[S_.Mt[0]])
                                P.op("dve", lambda e: e.tensor_tensor(out=S_.TT[0].t[:, 0:nb, :], in0=identb.unsqueeze(1).to_broadcast(bc3),
                                                                       in1=S_.Mt[0].t[:, 0:nb, :], op=ALU.subtract), reads=[S_.Mt[0], cbf], writes=[S_.TT[0]])
                                mask_op(1)
                                yield
                                for p in range(nb):
                                    P.tr(S_.tp[:, pc(p)], S_.TT[0].t[:, p, :], identb, reads=[S_.TT[0], cbf], writes=[psTB])
                                P.op("act", lambda e: e.activation(out=S_.Tm[0].t[:, 0:nb, :], in_=f3(S_.tp), func=AF.Copy), reads=[psTB], writes=[S_.Tm[0]])
                                yield
                                for l in range(1, 7):
                                    cur, nxt = (l - 1) % 2, l % 2
                                    if l + 1 < 7:
                                        mask_op(l + 1)
                                    for p in range(nb):
                                        P.mm(S_.Y[:, pc(p)], lhsT=S_.Mt[l].t[:, p, :], rhs=S_.Tm[cur].t[:, p, :], start=True, stop=True,
                                             reads=[S_.Mt[l], S_.Tm[cur]], writes=[S_.bY])
                                    yield
                                    P.op("act", lambda e: e.activation(out=S_.Yb.t[:, 0:nb, :], in_=f3(S_.Y), func=AF.Copy), reads=[S_.bY], writes=[S_.Yb])
                                    yield
                                    if l < 6:
                                        for p in range(nb):
                                            P.mm(S_.R[:, pc(p)], lhsT=S_.TT[cur].t[:, p, :], rhs=S_.Yb.t[:, p, :], start=True, stop=True,
                                                 reads=[S_.TT[cur], S_.Yb], writes=[S_.bR])
                                    for p in range(nb):
                                        P.mm(S_.RT[:, pc(p)], lhsT=S_.Yb.t[:, p, :], rhs=S_.TT[cur].t[:, p, :], start=True, stop=True,
                                             reads=[S_.Yb, S_.TT[cur]], writes=[S_.bRT])
                                    yield
                                    if l < 6:
                                        P.op("dve", lambda e: e.tensor_tensor(out=S_.Tm[nxt].t[:, 0:nb, :], in0=S_.Tm[cur].t[:, 0:nb, :], in1=f3(S_.R),
                                                                              op=ALU.subtract), reads=[S_.Tm[cur], S_.bR], writes=[S_.Tm[nxt]])
                                    P.op("dve", lambda e: e.tensor_tensor(out=S_.TT[nxt].t[:, 0:nb, :], in0=S_.TT[cur].t[:, 0:nb, :], in1=f3(S_.RT),
                                                                          op=ALU.subtract), reads=[S_.TT[cur], S_.bRT], writes=[S_.TT[nxt]])
                                    yield
                                TF = S_.TT[0]
                                for p in range(nb):
                                    P.mm(S_.K[:, pc(p)], lhsT=TF.t[:, p, :], rhs=gv.t[:, t0 + p, :], start=True, stop=True, reads=[TF, gv], writes=[S_.bK])
                                for p in range(nb):
                                    P.tr(S_.tp[:, pc(p)], gkT.t[:, tk(p)], identb, reads=[gkT, cbf], writes=[psTB])
                                yield
                                for p in range(nb):
                                    P.op("act", lambda e: e.activation(out=O_.Ubt.t[:, p, :], in_=S_.K[:, pc(p)], func=AF.Copy,
                                                                       scale=beta.t[:, t0 + p, hsl]), reads=[S_.bK, beta], writes=[O_.Ubt])
                                    P.op("act", lambda e: e.activation(out=S_.Kg.t[:, p, :], in_=S_.tp[:, pc(p)], func=AF.Copy,
                                                                       scale=egc.t[:, t0 + p, hsl]), reads=[psTB, egc], writes=[S_.Kg])
                                    P.op("dve", lambda e: e.tensor_scalar(out=O_.Kd.t[:, p, :], in0=S_.tp[:, pc(p)], scalar1=ekd.t[:, t0 + p, hsl],
                                                                          scalar2=None, op0=ALU.mult), reads=[psTB, ekd], writes=[O_.Kd])
                                if own:
                                    P.op("pool", lambda e: e.tensor_tensor(out=S_.dg.t[:, 0:nb, :], in0=identb.unsqueeze(1).to_broadcast(bc3),
                                                                           in1=colb(egc), op=ALU.mult), reads=[egc, cbf], writes=[S_.dg])
                                yield
                                for p in range(nb):
                                    P.mm(S_.Q[:, pc(p)], lhsT=S_.Kg.t[:, p, :], rhs=TF.t[:, p, :], start=True, stop=True, reads=[S_.Kg, TF], writes=[S_.bQ])
                                if own:
                                    P.mm(S_.D[:, 0:W], lhsT=onesb, rhs=S_.dg.t[:, 0:nb, :].rearrange("p t c -> p (t c)"), start=True, stop=True,
                                         reads=[cbf, S_.dg], writes=[S_.bD])
                                yield
                                P.op("act", lambda e: e.activation(out=O_.WpT.t[:, 0:W], in_=S_.Q[:, 0:W], func=AF.Copy), reads=[S_.bQ], writes=[O_.WpT])
                                if own:
                                    P.op("dve", lambda e: e.tensor_tensor(out=O_.QdT.t[:, 0:W], in0=S_.D[:, 0:W], in1=gqT.t[:, qo:qo + W], op=ALU.mult),
                                         reads=[S_.bD, gqT], writes=[O_.QdT])
                                yield

                            def seq_phase(t0, nb, O_):
                                pc = lambda p: slice(p * 128, (p + 1) * 128)
                                for p in range(nb):
                                    t = t0 + p
                                    own = t >= OWN0
                                    P.mm(psq[:, 0:128], lhsT=O_.WpT.t[:, pc(p)], rhs=Sbf.t[:], start=True, stop=True, reads=[O_.WpT, Sbf], writes=[psWS])
                                    P.op("dve", lambda e: e.scalar_tensor_tensor(out=vn.t[:], in0=psq[:, 0:128], scalar=nbeta.t[:, t, hsl],
                                                                                 in1=O_.Ubt.t[:, p, :], op0=ALU.mult, op1=ALU.add),
                                         reads=[psWS, nbeta, O_.Ubt], writes=[vn])
                                    yield
                                    if own:
                                        P.mm(psq[:, 128:256], lhsT=O_.QdT.t[:, pc(p)], rhs=Sbf.t[:], start=True, stop=False, reads=[O_.QdT, Sbf], writes=[psO])
                                        P.mm(psq[:, 128:256], lhsT=O_.QKm.t[:, pc(p)], rhs=vn.t[:], start=False, stop=True, reads=[O_.QKm, vn], writes=[psO])
                                    P.mm(psq[:, 256:384], lhsT=O_.Kd.t[:, p, :], rhs=vn.t[:], start=True, stop=True, reads=[O_.Kd, vn], writes=[psdS])
                                    P.op("dve", lambda e: e.scalar_tensor_tensor(out=St.t[:], in0=St.t[:], scalar=egl.t[:, t, hsl], in1=psq[:, 256:384],
                                                                                 op0=ALU.mult, op1=ALU.add), reads=[St, egl, psdS], writes=[St])
                                    P.op("act", lambda e: e.activation(out=Sbf.t[:], in_=St.t[:], func=AF.Copy), reads=[St], writes=[Sbf])
                                    yield
                                    if own:
                                        oi = (t - OWN0) % 4
                                        P.op("act", lambda e: e.activation(out=oj.t[:], in_=psq[:, 128:256], func=AF.Square, accum_out=oss.t[:, 0:1]),
                                             reads=[psO], writes=[oj, oss])
                                        P.op("act", lambda e: e.activation(out=oss.t[:, 1:2], in_=oss.t[:, 0:1], func=AF.Ln, bias=EPS, scale=1.0 / 128),
                                             reads=[oss], writes=[oss])
                                        P.op("act", lambda e: e.activation(out=oss.t[:, 2:3], in_=oss.t[:, 1:2], func=AF.Exp, scale=-0.5),
                                             reads=[oss], writes=[oss])
                                        P.op("act", lambda e: e.activation(out=on4.t[:, oi, :], in_=psq[:, 128:256], func=AF.Copy, scale=oss.t[:, 2:3]),
                                             reads=[psO, oss], writes=[on4])
                                        if oi == 3:
                                            yield
                                            G = (t - OWN0) // 4
                                            yb = yblk[G]
                                            for hh in range(2):
                                                for p2 in range(2):
                                                    p4 = hh * 2 + p2
                                                    P.tr(psTt[:, 768 + p2 * 128: 768 + (p2 + 1) * 128], on4.t[:, p4, :], identb, reads=[on4, cbf], writes=[psTB])
                                                P.op("dve", lambda e: e.scalar_tensor_tensor(out=yb.t[:, hh * 256:(hh + 1) * 256], in0=psTt[:, 768:1024], scalar=gct.t[:, 0:1],
                                                                                             in1=szT.t[:, G * 512 + hh * 256: G * 512 + (hh + 1) * 256], op0=ALU.mult, op1=ALU.mult),
                                                     reads=[psTB, gct, szT], writes=[yb])
                                            P.dma("sp", ycat[h, :, G * 512:(G + 1) * 512], yb.t[:], reads=[yb])
                                        yield

                            subs = [(0, 1)] + [(1 + 2 * i, 2) for i in range(16)]
                            if not STG["gdn_c"]:
                                subs = []
                            subs = subs[STG.get("sub0", 0):STG.get("sub1", 99)]
                            pairs = [subs[i:i + NSL] for i in range(0, len(subs), NSL)][:STG.get("npairs", 99)]
                            prev = []
                            for pi, pr in enumerate(pairs + [[]]):
                                gens = []
                                cur_out = []
                                for j, (t0, nb) in enumerate(pr):
                                    O_ = outs[(pi % 2) * NSL + j]
                                    gens.append(par_phase(t0, nb, slots[j], O_))
                                    cur_out.append((t0, nb, O_))

                                def seq_all(items):
                                    for (t0_, nb_, O2) in items:
                                        yield from seq_phase(t0_, nb_, O2)

                                sg = seq_all(prev if STG.get("doseq", 1) else [])
                                live = list(gens)
                                sdone = False
                                rnd = 0
                                while live or not sdone:
                                    for g_ in list(live):
                                        if rnd >= STG.get("pstop", 9999):
                                            live.remove(g_)
                                            continue
                                        try:
                                            next(g_)
                                        except StopIteration:
                                            live.remove(g_)
                                    if not sdone and (rnd % 2 == 0 or not live):
                                        try:
                                            next(sg)
                                        except StopIteration:
                                            sdone = True
                                    rnd += 1
                                prev = cur_out
                            pref_done[0] = False
                            if h + 1 < STG["nheads"] and STG["gdn_i"]:
                                for bi_ in range(2):
                                    nt_ = BLKS[bi_][1] * 128
                                    P.dma("sp", uTb[bi_].t[:, :, 0:nt_], stash[bi_, :, :, 0:nt_], writes=[uTb[bi_]])
                                pref_done[0] = True
                            psS = [bankB[0], bankB[1], psWS]
                            psO = [bankB[2], bankB[3]]
                            psL = [bankB[4], bankB[5]]
                            PT = [sb(ec, f"PT{i}", [128, 512], BF16) for i in range(4)]
                            rl = sb(ec, "rl", [128, 512], F32)
                            units = []
                            for G in range(2 if STG["fox_c"] else 0):
                                qt0 = OWN0 + 4 * G
                                for kt in range(qt0 + 4):
                                    units.append((G, kt, qt0))

                            def emit_S(u):
                                G, kt, qt0 = units[u]
                                pS_ = psS[u % 3]
                                diag = kt >= qt0
                                P.mm(pS_.t[:, :], lhsT=fkT.t[:, kt * 128:(kt + 1) * 128], rhs=fqT.t[:, G * 512:(G + 1) * 512],
                                     start=True, stop=not diag, reads=[fkT, fqT], writes=[pS_])
                                if diag:
                                    P.mm(pS_.t[:, :], lhsT=identb, rhs=maskG(kt - qt0), start=False, stop=True, reads=[cbf], writes=[pS_])

                            for u0 in range(min(2, len(units))):
                                emit_S(u0)
                            for u in range(len(units)):
                                G, kt, qt0 = units[u]
                                last = qt0 + 3
                                pS_ = psS[u % 3]
                                pt = PT[u % 4]
                                P.op("act", lambda e: e.activation(out=pt.t[:], in_=pS_.t[:, :], func=AF.Exp, bias=fb.t[:, h, G, kt:kt + 1], scale=1.0),
                                     reads=[pS_, fb], writes=[pt])
                                if u + 2 < len(units):
                                    emit_S(u + 2)
                                P.mm(psO[G].t[:, :], lhsT=fvt.t[:, kt, :], rhs=pt.t[:], start=(kt == 0), stop=(kt == last), reads=[fvt, pt], writes=[psO[G]])
                                P.mm(psL[G].t[:, :], lhsT=onesb, rhs=pt.t[:], start=(kt == 0), stop=(kt == last), reads=[cbf, pt], writes=[psL[G]])
                                if kt == last:
                                    P.op("dve", lambda e: e.reciprocal(out=rl.t[:], in_=psL[G].t[:, :]), reads=[psL[G]], writes=[rl])
                                    yb = yblk[G]
                                    P.op("dve", lambda e: e.tensor_tensor(out=yb.t[:], in0=psO[G].t[:, :], in1=rl.t[:], op=ALU.mult), reads=[psO[G], rl], writes=[yb])
                                    P.dma("sp", ycat[16 + h, :, G * 512:(G + 1) * 512], yb.t[:], reads=[yb])
                            P.barrier()
                P.barrier()

            if DEBUG:
                P.dma("pool", dbg[:, 5808:6832], ycat[0, :, :])
                P.dma("pool", dbg[:, 6832:7856], ycat[16, :, :])
                P.barrier()
            if STG["upto"] == "B":
                raise _Stop()
            esB.close()
            with ExitStack() as edd:
                psd = [ps(edd, f"psd{i}", [128, 512], F32) for i in range(6)]
                tpx = [ps(edd, f"tpx{i}", [128, 1024], BF16) for i in range(2)]
                h2dB = Buf("h2dB", None)
                wb = [sb(edd, f"wbD{i}", [128, 16, 512], BF16) for i in range(3)]
                wcnt = [0]

                def wload(src_ap):
                    w = wb[wcnt[0] % 3]
                    wcnt[0] += 1
                    P.dma("pool", w.t[:], src_ap, writes=[w])
                    return w

                kcp = lambda ap: ap.rearrange("(kc p) c -> p kc c", p=128)
                hbufs = [(wb[i].t, hh * 256, Buf(f"wbh{i}{hh}", None)) for i in range(3) for hh in range(2)]
                hcnt = [0]

                def hload(src_ap):
                    tt, off, hb = hbufs[hcnt[0] % 6]
                    hcnt[0] += 1
                    P.dma("pool", tt[:, :, off:off + 256], src_ap, writes=[hb])
                    return tt, off, hb

                u2T = sb(edd, "u2T", [128, 16, 1024], BF16)
                with ExitStack() as e1:
                    mixT = sb(e1, "mixT", [128, 16, 1024], BF16)
                    with ExitStack() as e2:
                        uTo = sb(e2, "uTo", [128, 16, 1024], BF16)
                        for j in range(2):
                            P.dma("sp", uTo.t[:, :, j * 512:(j + 1) * 512], stash[7 + j, :, :, :], writes=[uTo])
                        yT1 = sb(e2, "yT", [128, 16, 1024], BF16)
                        yT = [yT1, yT1]
                        sg = [sb(e2, f"sg{i}", [128, 512], F32) for i in range(2)]
                        mx = [sb(e2, f"mx{i}", [128, 512], F32) for i in range(2)]
                        pk = 0
                        for br in range(2):
                            for j in range(4):
                                P.dma("sp", yT1.t[:, j * 4:(j + 1) * 4, :],
                                      ycat[16 * br + 4 * j: 16 * br + 4 * j + 4, :, :].rearrange("k p t -> p k t"), writes=[yT1])
                            for cb in range(4):
                                hw = {}
                                for hh in range(2):
                                    c0 = cb * 512 + hh * 256
                                    hw[("g", hh)] = hload(kcp(w_gate[:, br * 2048 + c0: br * 2048 + c0 + 256]))
                                    hw[("o", hh)] = hload(kcp((w_og if br == 0 else w_of)[:, c0:c0 + 256]))
                                for cc_ in range(4):
                                    ch = cb * 4 + cc_
                                    wgt, wgo, wg = hw[("g", cc_ // 2)]
                                    wot, woo, wo = hw[("o", cc_ // 2)]
                                    gsl = slice(wgo + (cc_ % 2) * 128, wgo + (cc_ % 2 + 1) * 128)
                                    osl = slice(woo + (cc_ % 2) * 128, woo + (cc_ % 2 + 1) * 128)
                                    for hf in range(2):
                                        tsl = slice(hf * 512, (hf + 1) * 512)
                                        pgt = psd[pk % 6]
                                        ppt = psd[(pk + 1) % 6]
                                        pk += 2
                                        for kc in range(16):
                                            P.mm(pgt.t[:, :], lhsT=wgt[:, kc, gsl], rhs=uTo.t[:, kc, tsl],
                                                 start=(kc == 0), stop=(kc == 15), reads=[wg, uTo], writes=[pgt])
                                        for kc in range(16):
                                            P.mm(ppt.t[:, :], lhsT=wot[:, kc, osl], rhs=yT[br].t[:, kc, tsl],
                                                 start=(kc == 0), stop=(kc == 15), reads=[wo, yT[br]], writes=[ppt])
                                        s_ = sg[hf]
                                        P.op("act", lambda e: e.activation(out=s_.t[:], in_=pgt.t[:, :], func=AF.Exp, scale=-1.0), reads=[pgt], writes=[s_])
                                        P.op("dve", lambda e: e.tensor_scalar(out=s_.t[:], in0=s_.t[:], scalar1=1.0, scalar2=None, op0=ALU.add), reads=[s_], writes=[s_])
                                        P.op("dve", lambda e: e.reciprocal(out=s_.t[:], in_=s_.t[:]), reads=[s_], writes=[s_])
                                        if br == 0:
                                            P.op("dve", lambda e: e.tensor_tensor(out=mixT.t[:, ch, tsl], in0=ppt.t[:, :], in1=s_.t[:], op=ALU.mult),
                                                 reads=[ppt, s_], writes=[mixT])
                                        else:
                                            m_ = mx[hf]
                                            P.op("dve", lambda e: e.tensor_tensor(out=m_.t[:], in0=ppt.t[:, :], in1=s_.t[:], op=ALU.mult),
                                                 reads=[ppt, s_], writes=[m_])
                                            P.op("pool", lambda e: e.tensor_tensor(out=mixT.t[:, ch, tsl], in0=mixT.t[:, ch, tsl], in1=m_.t[:], op=ALU.add),
                                                 reads=[mixT, m_], writes=[mixT])
                        P.barrier()
                    with ExitStack() as e3:
                        gv1 = sb(e3, "gv1", [128, 2048], F32)
                        P.dma("sp", gv1.t[:], gvec[:, 2048:4096], writes=[gv1])
                        h2t = sb(e3, "h2t", [128, 8, 2048], F32)
                        for ti in range(8):
                            P.dma("sp", h2t.t[:, ti, :], xs[(OWN0 + ti) * 128:(OWN0 + ti + 1) * 128, :], writes=[h2t])
                        pk = 0
                        for cb in range(4):
                            wo = wload(kcp(w_out[:, cb * 512:(cb + 1) * 512]))
                            for ti in range(8):
                                pp = psd[pk % 6]
                                pk += 1
                                for kc in range(16):
                                    P.mm(pp.t[:, :], lhsT=mixT.t[:, kc, ti * 128:(ti + 1) * 128], rhs=wo.t[:, kc, :],
                                         start=(kc == 0), stop=(kc == 15), reads=[mixT, wo], writes=[pp])
                                P.op("dve", lambda e: e.tensor_tensor(out=h2t.t[:, ti, cb * 512:(cb + 1) * 512], in0=pp.t[:, :],
                                                                      in1=h2t.t[:, ti, cb * 512:(cb + 1) * 512], op=ALU.add), reads=[pp, h2t], writes=[h2t])
                        sq2 = sb(e3, "sq2", [128, 2048], BF16)
                        u2 = [sb(e3, f"u2{i}", [128, 2048], BF16) for i in range(2)]
                        ss2 = [sb(e3, f"ss2{i}", [128, 4], F32) for i in range(2)]
                        def d3_stage1(ti):
                            ss = ss2[ti % 2]
                            P.dma("sp", h2d[ti, :, :], h2t.t[:, ti, :], reads=[h2t])
                            P.op("act", lambda e: e.activation(out=sq2.t[:], in_=h2t.t[:, ti, :], func=AF.Square, accum_out=ss.t[:, 0:1]),
                                 reads=[h2t], writes=[sq2, ss])
                            P.op("act", lambda e: e.activation(out=ss.t[:, 1:2], in_=ss.t[:, 0:1], func=AF.Ln, bias=EPS, scale=1.0 / 2048), reads=[ss], writes=[ss])
                            P.op("act", lambda e: e.activation(out=ss.t[:, 2:3], in_=ss.t[:, 1:2], func=AF.Exp, scale=-0.5), reads=[ss], writes=[ss])

                        def d3_stage2(ti):
                            ss = ss2[ti % 2]
                            uu = u2[ti % 2]
                            P.op("dve", lambda e: e.scalar_tensor_tensor(out=uu.t[:], in0=h2t.t[:, ti, :], scalar=ss.t[:, 2:3], in1=gv1.t[:],
                                                                         op0=ALU.mult, op1=ALU.mult), reads=[h2t, ss, gv1], writes=[uu])
                            for half in range(2):
                                pb_ = tb_ = tpx[half]
                                for k8 in range(8):
                                    kc = half * 8 + k8
                                    P.tr(tb_.t[:, k8 * 128:(k8 + 1) * 128], uu.t[:, kc * 128:(kc + 1) * 128], identb, reads=[uu, cbf], writes=[pb_])
                                P.op("act" if half == 0 else "dve",
                                     (lambda e: e.activation(out=u2T.t[:, 0:8, ti * 128:(ti + 1) * 128],
                                                             in_=tb_.t[:, :].rearrange("p (k c) -> p k c", k=8), func=AF.Copy)) if half == 0 else
                                     (lambda e: e.tensor_copy(out=u2T.t[:, 8:16, ti * 128:(ti + 1) * 128],
                                                              in_=tb_.t[:, :].rearrange("p (k c) -> p k c", k=8))),
                                     reads=[pb_], writes=[u2T])

                        d3_stage1(0)
                        for ti in range(8):
                            if ti + 1 < 8:
                                d3_stage1(ti + 1)
                            d3_stage2(ti)
                        P.barrier()
                with ExitStack() as e4:
                    gv2 = sb(e4, "gv2", [128, 2048], F32)
                    P.dma("sp", gv2.t[:], gvec[:, 4096:6144], writes=[gv2])
                    h3a = [sb(e4, f"h3a{i}", [128, 2048], F32) for i in range(8)]
                    for ti in range(8):
                        P.dma("sp", h3a[ti].t[:], h2d[ti, :, :], writes=[h3a[ti]])
                    ab = [sb(e4, f"ab{i}", [128, 4, 1024], BF16) for i in range(2)]
                    rl_ = [sb(e4, f"rlu{i}", [128, 512], F32) for i in range(2)]
                    sq3 = sb(e4, "sq3", [128, 2048], BF16)
                    ss3 = [sb(e4, f"ss3{i}", [128, 4], F32) for i in range(2)]
                    pk = 0
                    pd = 0
                    for fbk in range(16):
                        wu = wload(kcp(w_up[:, fbk * 512:(fbk + 1) * 512]))
                        wdt = wload(w_down[fbk * 512:(fbk + 1) * 512, :].rearrange("(fc p) c -> p fc c", p=128))
                        wd = wdt.t[:].rearrange("p a b -> p (a b)").rearrange("p (f c) -> p f c", f=4)
                        a_ = ab[fbk % 2]
                        for cc_ in range(4):
                            for hf in range(2):
                                pp = psd[4 + pk % 2]
                                r_ = rl_[pk % 2]
                                pk += 1
                                for kc in range(16):
                                    P.mm(pp.t[:, :], lhsT=wu.t[:, kc, cc_ * 128:(cc_ + 1) * 128], rhs=u2T.t[:, kc, hf * 512:(hf + 1) * 512],
                                         start=(kc == 0), stop=(kc == 15), reads=[wu, u2T], writes=[pp])
                                P.op("act", lambda e: e.activation(out=r_.t[:], in_=pp.t[:, :], func=AF.Relu), reads=[pp], writes=[r_])
                                P.op("pool", lambda e: e.tensor_tensor(out=a_.t[:, cc_, hf * 512:(hf + 1) * 512], in0=r_.t[:], in1=r_.t[:], op=ALU.mult),
                                     reads=[r_], writes=[a_])
                        for ti in range(8):
                            for cb in range(4):
                                pp = psd[pd % 4]
                                pd += 1
                                for fc in range(4):
                                    P.mm(pp.t[:, :], lhsT=a_.t[:, fc, ti * 128:(ti + 1) * 128], rhs=wd[:, fc, cb * 512:(cb + 1) * 512],
                                         start=(fc == 0), stop=(fc == 3), reads=[a_, wdt], writes=[pp])
                                P.op("dve", lambda e: e.tensor_tensor(out=h3a[ti].t[:, cb * 512:(cb + 1) * 512], in0=pp.t[:, :],
                                                                      in1=h3a[ti].t[:, cb * 512:(cb + 1) * 512], op=ALU.add),
                                     reads=[pp, h3a[ti]], writes=[h3a[ti]])
                    for ti in range(8):
                        hb_ = h3a[ti]
                        ss = ss3[ti % 2]
                        P.op("act", lambda e: e.activation(out=sq3.t[:], in_=hb_.t[:], func=AF.Square, accum_out=ss.t[:, 0:1]),
                             reads=[hb_], writes=[sq3, ss])
                        P.op("act", lambda e: e.activation(out=ss.t[:, 1:2], in_=ss.t[:, 0:1], func=AF.Ln, bias=EPS, scale=1.0 / 2048), reads=[ss], writes=[ss])
                        P.op("act", lambda e: e.activation(out=ss.t[:, 2:3], in_=ss.t[:, 1:2], func=AF.Exp, scale=-0.5), reads=[ss], writes=[ss])
                        P.op("dve", lambda e: e.scalar_tensor_tensor(out=hb_.t[:], in0=hb_.t[:], scalar=ss.t[:, 2:3], in1=gv2.t[:],
                                                                     op0=ALU.mult, op1=ALU.mult), reads=[hb_, ss, gv2], writes=[hb_])
                        P.dma("sp", out[ti * 128:(ti + 1) * 128, :], hb_.t[:], reads=[hb_])
                    P.barrier()
        except _Stop:
            esB.close()
        P.barrier()
    return nc, P


_CACHE = {}


def _consts():
    p = np.arange(128)[:, None]
    f = np.arange(128)[None, :]
    ident = (p == f).astype(np.float32)
    ones = np.ones((128, 128), np.float32)
    maskT = np.where(f >= p, 0.0, NEG).astype(np.float32)
    q = np.arange(512)[None, :]
    maskG = [np.where(q - 128 * r - p >= 0, 0.0, NEG).astype(np.float32) for r in range(4)]
    lv = []
    for l in range(7):
        s_ = 1 << l
        j, i = p, f
        m = ((i // (2 * s_)) == (j // (2 * s_))) & ((i % (2 * s_)) >= s_) & ((j % (2 * s_)) < s_)
        lv.append(m.astype(np.float32))
    cbf = np.concatenate([ident, ones] + [maskT] * 4 + maskG + lv, axis=1)
    utri = (p <= f).astype(np.float32)
    slow = (p > f).astype(np.float32)
    cf = np.concatenate([ident, utri, slow, ones], axis=1)
    return np.ascontiguousarray(cbf), np.ascontiguousarray(cf)


def kernel(x, meta_tokens, mix_norm_g, w_in, conv_w, a_log, dt_bias, gdn_norm_g, w_o_gdn,
           fox_q_norm_g, fox_k_norm_g, fox_f_bias, w_o_fox, w_out, mlp_norm_g, w_up, w_down, final_norm_g):
    f32 = np.float32
    x = np.asarray(x, f32)
    w_in0 = np.asarray(w_in, f32)[0]
    if "nc" not in _CACHE:
        _, P1 = build(None)
        need = set(P1.used)
        nc, _ = build(need)
        _CACHE["nc"] = nc
    nc = _CACHE["nc"]
    GQ, GZ, GB, GA, FQ, FF, GTA = 0, 6144, 8192, 8208, 8224, 14368, 14384
    cols = []
    for h in range(NH):
        sl = lambda base: w_in0[:, base + h * 128: base + (h + 1) * 128]
        cols += [sl(GQ), sl(GQ + 2048), sl(GQ + 4096), sl(GZ), sl(FQ), sl(FQ + 2048), sl(FQ + 4096)]
    w_head = np.ascontiguousarray(np.concatenate(cols, axis=1))
    w_small = np.ascontiguousarray(np.concatenate([w_in0[:, GB:GB + 16], w_in0[:, GA:GA + 16], w_in0[:, FF:FF + 16]], axis=1))
    w_gate = np.ascontiguousarray(w_in0[:, GTA:GTA + 4096])
    cw = np.asarray(conv_w, f32)[0]
    cwl = np.zeros((128, NH, 3, 4), f32)
    for h in range(NH):
        for c in range(3):
            cwl[:, h, c, :] = cw[:, c * 2048 + h * 128: c * 2048 + (h + 1) * 128].T
    convw = np.ascontiguousarray(cwl.reshape(128, NH * 12))
    hp = np.ascontiguousarray(np.tile(np.concatenate([np.asarray(a_log, f32)[0], np.asarray(dt_bias, f32)[0],
                                                      np.asarray(fox_f_bias, f32)[0]])[None, :], (128, 1)))
    gcol = np.ascontiguousarray(np.stack([np.asarray(gdn_norm_g, f32)[0], np.asarray(fox_q_norm_g, f32)[0],
                                          np.asarray(fox_k_norm_g, f32)[0]], axis=1))
    gvec = np.ascontiguousarray(np.tile(np.concatenate([np.asarray(mix_norm_g, f32)[0], np.asarray(mlp_norm_g, f32)[0],
                                                        np.asarray(final_norm_g, f32)])[None, :], (128, 1)))
    cbf, cf = _consts()
    shared = {"w_head": w_head, "w_small": w_small, "w_gate": w_gate, "convw": convw, "hp": hp, "gcol": gcol,
              "gvec": gvec, "w_og": np.ascontiguousarray(np.asarray(w_o_gdn, f32)[0]),
              "w_of": np.ascontiguousarray(np.asarray(w_o_fox, f32)[0]), "w_out": np.ascontiguousarray(np.asarray(w_out, f32)[0]),
              "w_up": np.ascontiguousarray(np.asarray(w_up, f32)[0]), "w_down": np.ascontiguousarray(np.asarray(w_down, f32)[0]),
              "cbf": cbf, "cf": cf}
    meta = np.asarray(meta_tokens, f32)
    in_maps = []
    for c in range(8):
        b, tq = c // 4, c % 4
        nreal = 16 + (tq + 1) * 1024
        xs = np.zeros((NT * 128, 2048), f32)
        xs[NT * 128 - nreal: NT * 128 - nreal + 16] = meta
        xs[NT * 128 - nreal + 16:] = x[b, :(tq + 1) * 1024]
        vmk = np.zeros((NT * 128,), f32)
        vmk[NT * 128 - nreal:] = 1.0
        m = dict(shared)
        m["xs"] = xs
        m["vmask"] = np.ascontiguousarray(vmk.reshape(NT, 128).T)
        in_maps.append(m)
    if _CACHE.get("dbg_cores"):
        cs_ = _CACHE["dbg_cores"]
        return run_bass_kernel_spmd(nc, [in_maps[c] for c in cs_], core_ids=list(range(len(cs_))), trace=bool(_CACHE.get("trace")))
    res = run_bass_kernel_spmd(nc, in_maps, core_ids=list(range(8)))
    _CACHE["res"] = res
    outp = np.zeros((2, 4096, 2048), f32)
    for c in range(8):
        b, tq = c // 4, c % 4
        outp[b, tq * 1024:(tq + 1) * 1024] = np.asarray(res.results[c]["out"], f32)
    return outp
```

```python
import numpy as np
from contextlib import ExitStack
import concourse.bass as bass
import concourse.mybir as mybir
from concourse.bass_utils import run_bass_kernel_spmd

F32 = mybir.dt.float32
BF16 = mybir.dt.bfloat16
AF = mybir.ActivationFunctionType
ALU = mybir.AluOpType

NT = 33
OWN0 = 25
NH = 16
EPS = 1e-6
BLKS = [(0, 1)] + [(1 + 4 * i, 4) for i in range(8)]
EPOCH = 8192
NEG = -30000.0
DEBUG = False
INLINE_WAITS = True


class _Stop(Exception):
    pass


STG = {"upto": "D", "nheads": NH, "gdn_c": True, "fox_c": True, "gdn_i": True, "fox_i": True}


class Buf:
    __slots__ = ("name", "t", "w", "r", "x")

    def __init__(self, name, t, excl=False):
        self.name = name
        self.t = t
        self.w = None
        self.r = {}
        self.x = excl


class Prog:
    CE = ("pe", "dve", "act", "pool", "sp")

    def __init__(self, nc, es, need=None):
        self.nc = nc
        self.es = es
        self.need = need
        self.used = set()
        self.eng = {"pe": nc.tensor, "dve": nc.vector, "act": nc.scalar, "pool": nc.gpsimd, "sp": nc.sync}
        self.n = 0
        self.info = {}
        self.sigc = {e: 0 for e in self.CE}
        self.esem = {e: [] for e in self.CE}
        self.last = {e: None for e in self.CE}
        self.waited = {e: {} for e in self.CE}
        self.widx = {e: {s: -1 for s in self.CE} for e in self.CE}
        self.dsem = {}
        self.dpos = {}
        self.dval = {}
        self.dlast = {}
        self.nsem = 0
        for q in ("sp", "pool", "act"):
            self.dsem[q] = [self._newsem(f"d{q}{i}") for i in range(12)]
            self.dpos[q] = 0

    def _newsem(self, name):
        self.nsem += 1
        return self.es.enter_context(self.nc.semaphore(name))

    def _esem(self, e, idx):
        ep = idx // EPOCH
        while len(self.esem[e]) <= ep:
            self.esem[e].append(self._newsem(f"e{e}{len(self.esem[e])}"))
        return self.esem[e][ep]

    def _wait(self, E, d, raw, isdma):
        inf = self.info.get(d)
        if inf is None:
            return
        if inf[0] == "c":
            _, src, order = inf
            if src == E and not isdma and E == "pe":
                return
            if self.need is not None and d not in self.need:
                return
            if self.widx[E][src] >= order:
                return
            self.used.add(d)
            sidx = self.sigmap[d]
            self._emit_wait(E, self._esem(src, sidx), sidx % EPOCH + 1)
            self.widx[E][src] = order
        else:
            _, sem, val, key = inf
            if self.waited[E].get(key, 0) >= val:
                return
            self._emit_wait(E, sem, val)
            self.waited[E][key] = val

    defer = None

    def _emit_wait(self, E, sem, val):
        if self.defer is not None:
            self.defer.append((sem, val))
        else:
            self.eng[E].wait_ge(sem, val)

    sigmap = {}

    def _deps(self, E, reads, writes, isdma):
        for b in reads:
            if b.w is not None:
                self._wait(E, b.w, True, isdma)
        for b in writes:
            if b.w is not None:
                self._wait(E, b.w, False, isdma)
            for d in b.r.values():
                self._wait(E, d, False, isdma)

    def _mark(self, i, key, reads, writes):
        for b in reads:
            b.r[key] = i
        for b in writes:
            b.w = i
            b.r = {}

    def op(self, E, fn, reads=(), writes=()):
        xr = [b for b in reads if b.x]
        if xr:
            reads = [b for b in reads if not b.x]
            writes = list(writes) + [b for b in xr if b not in writes]
        if INLINE_WAITS:
            if E == "pe" and reads:
                self._deps(E, reads[:1], (), False)
                rest_r = reads[1:]
            else:
                rest_r = reads
            self.defer = []
            self._deps(E, rest_r, writes, False)
            pend, self.defer = self.defer, None
            for (sm_, vl_) in pend[:-1]:
                self.eng[E].wait_ge(sm_, vl_)
            i = self.n
            self.n += 1
            ins = fn(self.eng[E])
            if pend:
                ins._wait_ge(pend[-1][0], pend[-1][1])
        else:
            self._deps(E, reads, writes, False)
            i = self.n
            self.n += 1
            ins = fn(self.eng[E])
        if self.need is None or i in self.need:
            sidx = self.sigc[E]
            self.sigc[E] += 1
            ins.then_inc(self._esem(E, sidx), 1)
            self.sigmap[i] = sidx
        self.info[i] = ("c", E, i)
        self.last[E] = i
        self._mark(i, E, reads, writes)
        return i

    def mm(self, out, lhsT, rhs, start, stop, reads, writes):
        return self.op("pe", lambda e: e.matmul(out, lhsT=lhsT, rhs=rhs, start=start, stop=stop),
                       reads=reads, writes=writes)

    def tr(self, out, in_, ident, reads, writes):
        return self.op("pe", lambda e: e.transpose(out=out, in_=in_, identity=ident), reads=reads, writes=writes)

    def dma(self, q, out, in_, reads=(), writes=()):
        self._deps(q, reads, writes, True)
        k = self.dpos[q]
        self.dpos[q] = (k + 1) % len(self.dsem[q])
        sem = self.dsem[q][k]
        key = (q, k)
        if key in self.dlast:
            self._wait(q, self.dlast[key], False, True)
        val = self.dval.get(key, 0) + 16
        self.dval[key] = val
        i = self.n
        self.n += 1
        self.eng[q].dma_start(out=out, in_=in_).then_inc(sem, 16)
        self.info[i] = ("d", sem, val, key)
        self.dlast[key] = i
        self._mark(i, ("d", q, k), reads, writes)
        return i

    def barrier(self):
        lasts = [self.last[e] for e in self.CE if self.last[e] is not None]
        dl = list(self.dlast.values())
        for E in self.CE:
            for d in lasts:
                inf = self.info[d]
                if inf[1] == E:
                    continue
                self._wait(E, d, True, True)
            for d in dl:
                self._wait(E, d, True, True)


def build(need=None):
    nc = bass.Bass("TRN2", target_bir_lowering=False)
    dt = lambda n, s, d, k: nc.dram_tensor(n, s, d, kind=k).ap()
    xs = dt("xs", [NT * 128, 2048], F32, "ExternalInput")
    vmask = dt("vmask", [128, NT], F32, "ExternalInput")
    w_head = dt("w_head", [2048, NH * 896], F32, "ExternalInput")
    w_small = dt("w_small", [2048, 48], F32, "ExternalInput")
    w_gate = dt("w_gate", [2048, 4096], F32, "ExternalInput")
    convw = dt("convw", [128, NH * 12], F32, "ExternalInput")
    hp = dt("hp", [128, 48], F32, "ExternalInput")
    gcol = dt("gcol", [128, 3], F32, "ExternalInput")
    gvec = dt("gvec", [128, 3 * 2048], F32, "ExternalInput")
    w_og = dt("w_og", [2048, 2048], F32, "ExternalInput")
    w_of = dt("w_of", [2048, 2048], F32, "ExternalInput")
    w_out = dt("w_out", [2048, 2048], F32, "ExternalInput")
    w_up = dt("w_up", [2048, 8192], F32, "ExternalInput")
    w_down = dt("w_down", [8192, 2048], F32, "ExternalInput")
    cbf_in = dt("cbf", [128, 3712], F32, "ExternalInput")
    cf_in = dt("cf", [128, 512], F32, "ExternalInput")
    out = dt("out", [1024, 2048], F32, "ExternalOutput")
    dbg = dt("dbg", [128, 8192], F32, "ExternalOutput") if DEBUG else None
    stash = dt("stash", [9, 128, 16, 512], BF16, "Internal")
    ycat = dt("ycat", [32, 128, 1024], BF16, "Internal")
    h2d = dt("h2d", [8, 128, 2048], F32, "Internal")

    with ExitStack() as es:
        P = Prog(nc, es, need)
        Prog.sigmap = {}
        P.sigmap = {}

        uid = [0]

        def sb(es_, name, shape, dtype):
            uid[0] += 1
            return Buf(name, es_.enter_context(nc.sbuf_tensor(f"s{uid[0]}_{name}", shape, dtype)))

        def ps(es_, name, shape, dtype):
            uid[0] += 1
            return Buf(name, es_.enter_context(nc.psum_tensor(f"p{uid[0]}_{name}", shape, dtype)), excl=True)

        cbf = sb(es, "cbf", [128, 3712], BF16)
        cf = sb(es, "cf", [128, 512], F32)
        P.dma("pool", cbf.t[:], cbf_in[:, :], writes=[cbf])
        P.dma("sp", cf.t[:], cf_in[:, :], writes=[cf])
        identb = cbf.t[:, 0:128]
        onesb = cbf.t[:, 128:256]
        maskT4 = cbf.t[:, 256:768]
        maskG = lambda r: cbf.t[:, 768 + r * 512: 768 + (r + 1) * 512]
        lvl = lambda l: cbf.t[:, 2816 + l * 128: 2816 + (l + 1) * 128]
        identf = cf.t[:, 0:128]
        utri = cf.t[:, 128:256]
        slow = cf.t[:, 256:384]
        onesf = cf.t[:, 384:512]

        gct = sb(es, "gct", [128, 4], F32)
        esB = ExitStack()
        vm = sb(esB, "vm", [128, NT], F32)
        hpt = sb(esB, "hpt", [128, 48], F32)
        cwt = sb(esB, "cwt", [128, NH * 12], F32)
        P.dma("sp", vm.t[:], vmask[:, :], writes=[vm])
        P.dma("sp", hpt.t[:], hp[:, :], writes=[hpt])
        P.dma("sp", gct.t[:, 0:3], gcol[:, :], writes=[gct])
        P.dma("sp", cwt.t[:], convw[:, :], writes=[cwt])
        P.op("dve", lambda e: e.tensor_scalar(out=gct.t[:, 3:4], in0=gct.t[:, 1:2], scalar1=float(128 ** -0.5),
                                              scalar2=None, op0=ALU.mult), reads=[gct], writes=[gct])
        sm = sb(esB, "sm", [128, NT, 48], F32)
        S3 = [128, NT, NH]
        beta = sb(esB, "beta", S3, F32)
        nbeta = sb(esB, "nbeta", S3, F32)
        gg = sb(esB, "gg", S3, F32)
        egc = sb(esB, "egc", S3, F32)
        ekd = sb(esB, "ekd", S3, F32)
        egl = sb(esB, "egl", S3, F32)
        fb = sb(esB, "fb", [128, NH, 2, NT], F32)
        wsm = sb(esB, "wsm", [128, 16, 48], BF16)
        P.dma("pool", wsm.t[:], w_small.rearrange("(kc p) c -> p kc c", p=128), writes=[wsm])
        wh = sb(esB, "wh", [128, 16, 896], BF16)
        if STG["nheads"] > 0:
            P.dma("pool", wh.t[:], w_head[:, 0:896].rearrange("(kc p) c -> p kc c", p=128), writes=[wh])

        try:
            with ExitStack() as ea:
                gv0 = sb(ea, "gv0", [128, 2048], F32)
                P.dma("sp", gv0.t[:], gvec[:, 0:2048], writes=[gv0])
                xts = [sb(ea, f"xt{i}", [128, 2048], F32) for i in range(3)]
                sqj = sb(ea, "sqj", [128, 2048], BF16)
                ub = [sb(ea, f"ub{i}", [128, 2048], BF16) for i in range(2)]
                ssA = [sb(ea, f"ssA{i}", [128, 2], F32) for i in range(3)]
                uTb = [sb(ea, f"uTbA{i}", [128, 16, 512], BF16) for i in range(2)]
                tp = [ps(ea, f"tp{i}", [128, 1024], BF16) for i in range(4)]
                psS = [ps(ea, f"psS{i}", [128, 512], F32) for i in range(2)]
                tiles = [(bi, t0, ntl, ti) for bi, (t0, ntl) in enumerate(BLKS) for ti in range(ntl)]

                def stage1(t):
                    xt, ss = xts[t % 3], ssA[t % 3]
                    P.dma("sp", xt.t[:], xs[t * 128:(t + 1) * 128, :], writes=[xt])
                    P.op("act", lambda e: e.activation(out=sqj.t[:], in_=xt.t[:], func=AF.Square,
                                                       accum_out=ss.t[:, 0:1]), reads=[xt], writes=[sqj, ss])
                    P.op("act", lambda e: e.activation(out=ss.t[:, 1:2], in_=ss.t[:, 0:1], func=AF.Ln,
                                                       bias=EPS, scale=1.0 / 2048), reads=[ss], writes=[ss])
                    P.op("act", lambda e: e.activation(out=ss.t[:, 0:1], in_=ss.t[:, 1:2], func=AF.Exp,
                                                       scale=-0.5), reads=[ss], writes=[ss])

                def stage2(bi, t0, ntl, ti):
                    t = t0 + ti
                    uT = uTb[bi % 2]
                    pS = psS[bi % 2]
                    xt, ss, u1 = xts[t % 3], ssA[t % 3], ub[t % 2]
                    P.op("dve", lambda e: e.scalar_tensor_tensor(out=u1.t[:], in0=xt.t[:], scalar=ss.t[:, 0:1],
                                                                 in1=gv0.t[:], op0=ALU.mult, op1=ALU.mult),
                         reads=[xt, ss, gv0], writes=[u1])
                    for half in range(2):
                        tph = tp[(t % 2) * 2 + half]
                        for k8 in range(8):
                            kc = half * 8 + k8
                            P.tr(tph.t[:, k8 * 128:(k8 + 1) * 128], u1.t[:, kc * 128:(kc + 1) * 128], identb,
                                 reads=[u1, cbf], writes=[tph])
                        src = tph.t[:, :].rearrange("p (k c) -> p k c", k=8)
                        dst = uT.t[:, half * 8:(half + 1) * 8, ti * 128:(ti + 1) * 128]
                        if half == 0:
                            P.op("act", lambda e: e.activation(out=dst, in_=src, func=AF.Copy), reads=[tph], writes=[uT])
                        else:
                            P.op("dve", lambda e: e.tensor_copy(out=dst, in_=src), reads=[tph], writes=[uT])
                    for kc in range(16):
                        P.mm(pS.t[:, ti * 48:(ti + 1) * 48], lhsT=uT.t[:, kc, ti * 128:(ti + 1) * 128],
                             rhs=wsm.t[:, kc, :], start=(kc == 0), stop=(kc == 15), reads=[uT, wsm], writes=[pS])
                    if ti == ntl - 1:
                        P.op("dve", lambda e: e.tensor_copy(
                            out=sm.t[:, t0:t0 + ntl, :], in_=pS.t[:, 0:ntl * 48].rearrange("p (t c) -> p t c", t=ntl)),
                            reads=[pS], writes=[sm])
                        P.dma("sp", stash[bi, :, :, 0:ntl * 128], uT.t[:, :, 0:ntl * 128], reads=[uT], writes=[])

                stage1(0)
                for i_, (bi, t0, ntl, ti) in enumerate(tiles):
                    if i_ + 1 < len(tiles):
                        stage1(i_ + 1)
                    stage2(bi, t0, ntl, ti)
                P.barrier()
            if STG["upto"] == "A":
                raise _Stop()

            with ExitStack() as e0:
                t1 = sb(e0, "t1", S3, F32)
                t2 = sb(e0, "t2", S3, F32)
                lf = sb(e0, "lf", S3, F32)
                gc = sb(e0, "gc", S3, F32)
                cc = sb(e0, "cc", S3, F32)
                pre = sb(e0, "pre", S3, F32)
                tot = sb(e0, "tot", S3, F32)
                nega = sb(e0, "nega", [128, 16], F32)
                pq = [ps(e0, f"pq{i}", [128, 512], F32) for i in range(2)]
                vmb = vm.t[:, :].unsqueeze(2).to_broadcast(S3)
                hb = lambda a: hpt.t[:, a:a + 16].unsqueeze(1).to_broadcast(S3)
                A = lambda b_, lo: b_.t[:, :, lo:lo + 16]
                P.op("act", lambda e: e.activation(out=t1.t[:], in_=A(sm, 0), func=AF.Exp, scale=-1.0), reads=[sm], writes=[t1])
                P.op("dve", lambda e: e.tensor_scalar(out=t1.t[:], in0=t1.t[:], scalar1=1.0, scalar2=None, op0=ALU.add), reads=[t1], writes=[t1])
                P.op("dve", lambda e: e.reciprocal(out=t1.t[:], in_=t1.t[:]), reads=[t1], writes=[t1])
                P.op("dve", lambda e: e.tensor_tensor(out=beta.t[:], in0=t1.t[:], in1=vmb, op=ALU.mult), reads=[t1, vm], writes=[beta])
                P.op("dve", lambda e: e.tensor_scalar(out=nbeta.t[:], in0=beta.t[:], scalar1=-1.0, scalar2=None, op0=ALU.mult), reads=[beta], writes=[nbeta])
                P.op("act", lambda e: e.activation(out=nega.t[:], in_=hpt.t[:, 0:16], func=AF.Exp), reads=[hpt], writes=[nega])
                P.op("dve", lambda e: e.tensor_scalar(out=nega.t[:], in0=nega.t[:], scalar1=-1.0, scalar2=None, op0=ALU.mult), reads=[nega], writes=[nega])
                P.op("dve", lambda e: e.tensor_tensor(out=t2.t[:], in0=A(sm, 16), in1=hb(16), op=ALU.add), reads=[sm, hpt], writes=[t2])
                P.op("act", lambda e: e.activation(out=t2.t[:], in_=t2.t[:], func=AF.Exp), reads=[t2], writes=[t2])
                P.op("act", lambda e: e.activation(out=t2.t[:], in_=t2.t[:], func=AF.Ln, bias=1.0, scale=1.0), reads=[t2], writes=[t2])
                P.op("dve", lambda e: e.tensor_tensor(out=t2.t[:], in0=t2.t[:], in1=nega.t[:, :].unsqueeze(1).to_broadcast(S3), op=ALU.mult), reads=[t2, nega], writes=[t2])
                P.op("dve", lambda e: e.tensor_tensor(out=gg.t[:], in0=t2.t[:], in1=vmb, op=ALU.mult), reads=[t2, vm], writes=[gg])
                P.op("dve", lambda e: e.tensor_tensor(out=t1.t[:], in0=A(sm, 32), in1=hb(32), op=ALU.add), reads=[sm, hpt], writes=[t1])
                P.op("act", lambda e: e.activation(out=t1.t[:], in_=t1.t[:], func=AF.Exp, scale=-1.0), reads=[t1], writes=[t1])
                P.op("act", lambda e: e.activation(out=t1.t[:], in_=t1.t[:], func=AF.Ln, bias=1.0, scale=1.0), reads=[t1], writes=[t1])
                P.op("dve", lambda e: e.scalar_tensor_tensor(out=lf.t[:], in0=t1.t[:], scalar=-1.0, in1=vmb, op0=ALU.mult, op1=ALU.mult), reads=[t1, vm], writes=[lf])

                fl = lambda b_: b_.t[:, :, :].rearrange("p t h -> p (t h)")

                def colmm(lhsT, src, dst, func=None):
                    for pi in range(3):
                        pp = pq[pi % 2]
                        sl = slice(pi * 176, (pi + 1) * 176)
                        P.mm(pp.t[:, 0:176], lhsT=lhsT, rhs=fl(src)[:, sl], start=True, stop=True, reads=[cf, src], writes=[pp])
                        if func is None:
                            P.op("dve", lambda e: e.tensor_copy(out=fl(dst)[:, sl], in_=pp.t[:, 0:176]), reads=[pp], writes=[dst])
                        else:
                            P.op("act", lambda e: e.activation(out=fl(dst)[:, sl], in_=pp.t[:, 0:176], func=func), reads=[pp], writes=[dst])

                colmm(utri, gg, gc)
                colmm(onesf, gg, egl, AF.Exp)
                P.op("act", lambda e: e.activation(out=egc.t[:], in_=gc.t[:], func=AF.Exp), reads=[gc], writes=[egc])
                colmm(onesf, gg, t2)
                P.op("dve", lambda e: e.tensor_tensor(out=t2.t[:], in0=t2.t[:], in1=gc.t[:], op=ALU.subtract), reads=[t2, gc], writes=[t2])
                P.op("act", lambda e: e.activation(out=ekd.t[:], in_=t2.t[:], func=AF.Exp), reads=[t2], writes=[ekd])
                colmm(utri, lf, cc)
                colmm(onesf, lf, tot)
                P.op("dve", lambda e: e.memset(pre.t[:, 0, :], 0.0), writes=[pre])
                for t in range(1, NT):
                    P.op("dve", lambda e: e.tensor_tensor(out=pre.t[:, t, :], in0=pre.t[:, t - 1, :], in1=tot.t[:, t - 1, :],
                                                          op=ALU.add), reads=[pre, tot], writes=[pre])
                P.op("dve", lambda e: e.tensor_tensor(out=cc.t[:], in0=cc.t[:], in1=pre.t[:], op=ALU.add), reads=[cc, pre], writes=[cc])
                P.op("dve", lambda e: e.tensor_scalar(out=t1.t[:], in0=vmb, scalar1=-1.0, scalar2=-NEG, op0=ALU.add, op1=ALU.mult), reads=[vm], writes=[t1])
                P.op("dve", lambda e: e.tensor_tensor(out=t1.t[:], in0=t1.t[:], in1=cc.t[:], op=ALU.subtract), reads=[t1, cc], writes=[t1])
                for h in range(NH):
                    for G in range(2):
                        tref = 27 + 4 * G
                        P.op("dve", lambda e: e.tensor_scalar(out=fb.t[:, h, G, :], in0=t1.t[:, :, h], scalar1=pre.t[:, tref, h:h + 1],
                                                              scalar2=None, op0=ALU.add), reads=[t1, pre], writes=[fb])
                if DEBUG:
                    P.dma("sp", dbg[:, 0:528], fl(beta), reads=[beta])
                    P.dma("sp", dbg[:, 528:1056], fl(gg), reads=[gg])
                    P.dma("sp", dbg[:, 1056:1584], fl(cc), reads=[cc])
                P.barrier()

            if STG["upto"] == "S":
                raise _Stop()
            def norm_block(es_unused, src, ntok, dstap, dstbuf, a, scalar, tmp):
                sqb, lnb, rsb, psN = tmp
                P.op("dve", lambda e: e.tensor_tensor(out=sqb.t[:, 0:ntok], in0=src.t[:, 0:ntok], in1=src.t[:, 0:ntok], op=ALU.mult),
                     reads=[src], writes=[sqb])
                P.mm(psN.t[:, 0:ntok], lhsT=onesb, rhs=sqb.t[:, 0:ntok], start=True, stop=True, reads=[cbf, sqb], writes=[psN])
                P.op("act", lambda e: e.activation(out=lnb.t[:, 0:ntok], in_=psN.t[:, 0:ntok], func=AF.Ln, bias=EPS, scale=a),
                     reads=[psN], writes=[lnb])
                P.op("act", lambda e: e.activation(out=rsb.t[:, 0:ntok], in_=lnb.t[:, 0:ntok], func=AF.Exp, scale=-0.5),
                     reads=[lnb], writes=[rsb])
                P.op("dve", lambda e: e.scalar_tensor_tensor(out=dstap, in0=src.t[:, 0:ntok], scalar=scalar, in1=rsb.t[:, 0:ntok],
                                                             op0=ALU.mult, op1=ALU.mult), reads=[src, rsb, gct], writes=[dstbuf])

            with ExitStack() as eb:
                uTb = [sb(eb, f"uTbB{i}", [128, 16, 512], BF16) for i in range(2)]
                pref_done = [False]
                yblk = [sb(eb, f"yblk{i}", [128, 512], BF16) for i in range(2)]
                for h in range(STG["nheads"]):
                    with ExitStack() as eg:
                        gqT = sb(eg, "gqT", [128, 1024], BF16)
                        gkT = sb(eg, "gkT", [128, NT * 128], BF16)
                        szT = sb(eg, "szT", [128, 1024], BF16)
                        gv = sb(eg, "gv", [128, NT, 128], BF16)
                        fqT = sb(eg, "fqT", [128, 1024], BF16)
                        fkT = sb(eg, "fkT", [128, NT * 128], BF16)
                        fvt = sb(eg, "fvt", [128, NT, 128], BF16)
                        with ExitStack() as ei:
                            psA = [ps(ei, f"psA{i}", [128, 512], F32) for i in range(3)]
                            psNl = [ps(ei, f"psN{i}", [128, 512], F32) for i in range(2)]
                            pstl = [ps(ei, f"pst{i}", [128, 1024], BF16) for i in range(2)]
                            rb = [[sb(ei, f"rb{c}{i}", [128, 515], F32) for i in range(2)] for c in range(3)]
                            accl = [sb(ei, f"acc{i}", [128, 512], F32) for i in range(5)]
                            csl = [sb(ei, f"cs{i}", [128, 512], BF16) for i in range(16)]
                            sql = [sb(ei, f"sqb{i}", [128, 512], BF16) for i in range(6)]
                            lnl = [sb(ei, f"lnb{i}", [128, 512], F32) for i in range(6)]
                            rsl = [sb(ei, f"rsb{i}", [128, 512], F32) for i in range(6)]
                            cnt = {"cs": 0, "acc": 0, "n": 0, "pn": 0, "pt": 0, "sq": 0}

                            def take(kind, pool):
                                i_ = cnt[kind]
                                cnt[kind] += 1
                                return pool[i_ % len(pool)]

                            def norm_gen(cs, ntok, dstap, dstbuf, a, scalar):
                                sqb = take("sq", sql)
                                P.op("dve", lambda e: e.tensor_tensor(out=sqb.t[:, 0:ntok], in0=cs.t[:, 0:ntok], in1=cs.t[:, 0:ntok], op=ALU.mult),
                                     reads=[cs], writes=[sqb])
                                yield
                                psN = take("pn", psNl)
                                lnb = take("n", lnl)
                                rsb = rsl[(cnt["n"] - 1) % len(rsl)]
                                P.mm(psN.t[:, 0:ntok], lhsT=onesb, rhs=sqb.t[:, 0:ntok], start=True, stop=True, reads=[cbf, sqb], writes=[psN])
                                P.op("act", lambda e: e.activation(out=lnb.t[:, 0:ntok], in_=psN.t[:, 0:ntok], func=AF.Ln, bias=EPS, scale=a),
                                     reads=[psN], writes=[lnb])
                                P.op("act", lambda e: e.activation(out=rsb.t[:, 0:ntok], in_=lnb.t[:, 0:ntok], func=AF.Exp, scale=-0.5),
                                     reads=[lnb], writes=[rsb])
                                yield
                                yield
                                P.op("dve", lambda e: e.scalar_tensor_tensor(out=dstap, in0=cs.t[:, 0:ntok], scalar=scalar, in1=rsb.t[:, 0:ntok],
                                                                             op0=ALU.mult, op1=ALU.mult), reads=[cs, rsb, gct], writes=[dstbuf])

                            def tr_gen(cs, ntl, ntok, dst, t0, eng):
                                pst = take("pt", pstl)
                                for ti in range(ntl):
                                    P.tr(pst.t[:, ti * 128:(ti + 1) * 128], cs.t[:, ti * 128:(ti + 1) * 128], identb, reads=[cs, cbf], writes=[pst])
                                src = pst.t[:, 0:ntok].rearrange("p (t c) -> p t c", t=ntl)
                                if eng == "act":
                                    P.op("act", lambda e: e.activation(out=dst.t[:, t0:t0 + ntl, :], in_=src, func=AF.Copy), reads=[pst], writes=[dst])
                                else:
                                    P.op("dve", lambda e: e.tensor_copy(out=dst.t[:, t0:t0 + ntl, :], in_=src), reads=[pst], writes=[dst])
                                return
                                yield

                            def post_stream(ct, bi, t0, ntl, ntok, pA):
                                if STG.get("ip_nopost"):
                                    cs0 = take("cs", csl)
                                    P.op("dve" if ct % 2 else "act", (lambda e: e.tensor_copy(out=cs0.t[:, 0:ntok], in_=pA.t[:, 0:ntok])) if ct % 2 else (lambda e: e.activation(out=cs0.t[:, 0:ntok], in_=pA.t[:, 0:ntok], func=AF.Copy)), reads=[pA], writes=[cs0])
                                    return
                                if ct == 3:
                                    P.op("act", lambda e: e.activation(out=szT.t[:, (bi - 7) * 512:(bi - 6) * 512], in_=pA.t[:, :], func=AF.Silu),
                                         reads=[pA], writes=[szT])
                                    return
                                cs = take("cs", csl)
                                if ct == 6:
                                    P.op("act", lambda e: e.activation(out=cs.t[:, 0:ntok], in_=pA.t[:, 0:ntok], func=AF.Copy), reads=[pA], writes=[cs])
                                    yield
                                    yield from tr_gen(cs, ntl, ntok, fvt, t0, "dve")
                                    return
                                if ct >= 4:
                                    P.op("dve", lambda e: e.tensor_copy(out=cs.t[:, 0:ntok], in_=pA.t[:, 0:ntok]), reads=[pA], writes=[cs])
                                    if ct == 4:
                                        yield from norm_gen(cs, ntok, fqT.t[:, (bi - 7) * 512:(bi - 6) * 512], fqT, 1.0 / 128, gct.t[:, 3:4])
                                    else:
                                        yield from norm_gen(cs, ntok, fkT.t[:, t0 * 128: t0 * 128 + ntok], fkT, 1.0 / 128, gct.t[:, 2:3])
                                    return
                                r = rb[ct][bi % 2]
                                rp = rb[ct][(bi + 1) % 2]
                                P.op("dve", lambda e: e.tensor_copy(out=r.t[:, 3:3 + ntok], in_=pA.t[:, 0:ntok]), reads=[pA], writes=[r])
                                if bi == 0:
                                    P.op("pool", lambda e: e.memset(r.t[:, 0:3], 0.0), writes=[r])
                                elif ct == 0 and bi == 7:
                                    uTp = uTb[(bi + 1) % 2]
                                    pN = take("pn", psNl)
                                    for kc in range(16):
                                        P.mm(pN.t[:, 0:3], lhsT=wh.t[:, kc, 0:128], rhs=uTp.t[:, kc, 509:512],
                                             start=(kc == 0), stop=(kc == 15), reads=[wh, uTp], writes=[pN])
                                    P.op("dve", lambda e: e.tensor_copy(out=r.t[:, 0:3], in_=pN.t[:, 0:3]), reads=[pN], writes=[r])
                                else:
                                    pn = BLKS[bi - 1][1] * 128
                                    P.op("pool", lambda e: e.tensor_copy(out=r.t[:, 0:3], in_=rp.t[:, pn:pn + 3]), reads=[rp], writes=[r])
                                yield
                                acc = take("acc", accl)
                                cw = lambda kk: cwt.t[:, h * 12 + ct * 4 + kk: h * 12 + ct * 4 + kk + 1]
                                P.op("dve", lambda e: e.tensor_scalar(out=acc.t[:, 0:ntok], in0=r.t[:, 0:ntok], scalar1=cw(0), scalar2=None,
                                                                      op0=ALU.mult), reads=[r, cwt], writes=[acc])
                                for kk in range(1, 4):
                                    P.op("dve", lambda e: e.scalar_tensor_tensor(out=acc.t[:, 0:ntok], in0=r.t[:, kk:kk + ntok], scalar=cw(kk),
                                                                                 in1=acc.t[:, 0:ntok], op0=ALU.mult, op1=ALU.add),
                                         reads=[r, cwt, acc], writes=[acc])
                                yield
                                P.op("act", lambda e: e.activation(out=cs.t[:, 0:ntok], in_=acc.t[:, 0:ntok], func=AF.Silu), reads=[acc], writes=[cs])
                                yield
                                yield
                                if ct == 0:
                                    yield from norm_gen(cs, ntok, gqT.t[:, (bi - 7) * 512:(bi - 6) * 512], gqT, 1.0, float(128 ** -0.5))
                                elif ct == 1:
                                    yield from norm_gen(cs, ntok, gkT.t[:, t0 * 128: t0 * 128 + ntok], gkT, 1.0, 1.0)
                                else:
                                    yield from tr_gen(cs, ntl, ntok, gv, t0, "act")

                            active = []

                            def advance():
                                for g_ in list(active):
                                    try:
                                        next(g_)
                                    except StopIteration:
                                        active.remove(g_)

                            k = 0
                            for bi, (t0, ntl) in enumerate(BLKS if STG["gdn_i"] else []):
                                ntok = ntl * 128
                                uT = uTb[bi % 2]
                                if not (bi < 2 and pref_done[0]):
                                    P.dma("sp", uT.t[:, :, 0:ntok], stash[bi, :, :, 0:ntok], writes=[uT])
                                own = bi >= 7
                                for ct in ([0, 1, 2, 3, 4, 5, 6] if own else [1, 2, 5, 6]):
                                    pA = psA[k % 3]
                                    k += 1
                                    for kc in range(16):
                                        P.mm(pA.t[:, 0:ntok], lhsT=wh.t[:, kc, ct * 128:(ct + 1) * 128], rhs=uT.t[:, kc, 0:ntok],
                                             start=(kc == 0), stop=(kc == 15), reads=[wh, uT], writes=[pA])
                                    active.append(post_stream(ct, bi, t0, ntl, ntok, pA))
                                    advance()
                            while active:
                                advance()
                            P.barrier()
                        if DEBUG and h == 0:
                            with ExitStack() as ed:
                                d32 = sb(ed, "d32", [128, 4224], F32)
                                P.op("dve", lambda e: e.tensor_copy(out=d32.t[:], in_=gkT.t[:]), reads=[gkT], writes=[d32])
                                P.dma("sp", dbg[:, 1584:1584 + 4224], d32.t[:], reads=[d32])
                                P.barrier()
                        with ExitStack() as ec:
                            bankB = [Buf(f"gbank{i}", ps(ec, f"gbank{i}", [128, 512], F32).t, excl=True) for i in range(6)]
                            psq = ps(ec, "psq", [128, 512], F32).t
                            psWS = psO = psdS = Buf("psqB", psq, excl=True)
                            psTt = ps(ec, "psTt", [128, 1024], BF16).t
                            psTB = Buf("psTtB", psTt, excl=True)

                            class Slot:
                                pass

                            slots = []
                            NSL = 3
                            for si in range(NSL):
                                S_ = Slot()
                                S_.bD = S_.bY = S_.bK = S_.bR = bankB[2 * si]
                                S_.bQ = S_.bRT = bankB[2 * si + 1]
                                S_.D = S_.bD.t[:, 0:256]
                                S_.Y = S_.bD.t[:, 0:256]
                                S_.K = S_.bD.t[:, 256:512]
                                S_.R = S_.bD.t[:, 256:512]
                                S_.Q = S_.bQ.t[:, 0:256]
                                S_.RT = S_.bQ.t[:, 0:256]
                                S_.tp = psTt[:, si * 256:(si + 1) * 256]
                                S_.Gt = sb(ec, f"Gt{si}", [128, 2, 128], F32)
                                S_.ET = sb(ec, f"ET{si}", [128, 256], BF16)
                                S_.ApT = sb(ec, f"ApT{si}", [128, 2, 128], BF16)
                                S_.Mt = [sb(ec, f"Mt{si}{l}", [128, 2, 128], BF16) for l in range(7)]
                                S_.TT = [sb(ec, f"TT{si}{i}", [128, 2, 128], BF16) for i in range(2)]
                                S_.Tm = [sb(ec, f"Tm{si}{i}", [128, 2, 128], BF16) for i in range(2)]
                                S_.Yb = sb(ec, f"Yb{si}", [128, 2, 128], BF16)
                                S_.Kg = sb(ec, f"Kg{si}", [128, 2, 128], BF16)
                                S_.dg = sb(ec, f"dg{si}", [128, 2, 128], BF16)
                                slots.append(S_)
                            outs = []
                            for oi in range(2 * NSL):
                                O_ = Slot()
                                O_.QKm = sb(ec, f"QKm{oi}", [128, 256], BF16)
                                O_.QdT = sb(ec, f"QdT{oi}", [128, 256], BF16)
                                O_.WpT = sb(ec, f"WpT{oi}", [128, 256], BF16)
                                O_.Ubt = sb(ec, f"Ubt{oi}", [128, 2, 128], F32)
                                O_.Kd = sb(ec, f"Kd{oi}", [128, 2, 128], BF16)
                                outs.append(O_)
                            St = sb(ec, "St", [128, 128], F32)
                            Sbf = sb(ec, "Sbf", [128, 128], BF16)
                            vn = sb(ec, "vn", [128, 128], BF16)
                            oj = sb(ec, "oj", [128, 128], BF16)
                            oss = sb(ec, "oss", [128, 4], F32)
                            on4 = sb(ec, "on4", [128, 4, 128], BF16)
                            P.op("pool", lambda e: e.memset(St.t[:], 0.0), writes=[St])
                            P.op("pool", lambda e: e.memset(Sbf.t[:], 0.0), writes=[Sbf])
                            hsl = slice(h, h + 1)

                            def par_phase(t0, nb, S_, O_):
                                W = nb * 128
                                bc3 = [128, nb, 128]
                                own = t0 >= OWN0
                                qo = (t0 - OWN0) * 128
                                colb = lambda b_: b_.t[:, t0:t0 + nb, hsl].to_broadcast(bc3)
                                tk = lambda p: slice((t0 + p) * 128, (t0 + p + 1) * 128)
                                pc = lambda p: slice(p * 128, (p + 1) * 128)
                                f3 = lambda ap: ap[:, 0:W].rearrange("p (t c) -> p t c", t=nb)
                                P.op("dve", lambda e: e.tensor_tensor(out=S_.Gt.t[:, 0:nb, :], in0=colb(gg), in1=slow.unsqueeze(1).to_broadcast(bc3),
                                                                      op=ALU.mult), reads=[gg, cf], writes=[S_.Gt])
                                P.mm(S_.D[:, 0:W], lhsT=identb, rhs=maskT4[:, 0:W], start=True, stop=False, reads=[cbf], writes=[S_.bD])
                                for p in range(nb):
                                    P.mm(S_.D[:, pc(p)], lhsT=S_.Gt.t[:, p, :], rhs=utri, start=False, stop=(p == nb - 1), reads=[S_.Gt, cf], writes=[S_.bD])
                                for p in range(nb):
                                    P.mm(S_.K[:, pc(p)], lhsT=gkT.t[:, tk(p)], rhs=gkT.t[:, tk(p)], start=True, stop=True, reads=[gkT], writes=[S_.bK])
                                if own:
                                    for p in range(nb):
                                        P.mm(S_.Q[:, pc(p)], lhsT=gkT.t[:, tk(p)], rhs=gqT.t[:, qo + p * 128: qo + (p + 1) * 128],
                                             start=True, stop=True, reads=[gkT, gqT], writes=[S_.bQ])
                                yield
                                P.op("act", lambda e: e.activation(out=S_.ET.t[:, 0:W], in_=S_.D[:, 0:W], func=AF.Exp), reads=[S_.bD], writes=[S_.ET])
                                yield
                                for p in range(nb):
                                    P.op("dve", lambda e: e.scalar_tensor_tensor(out=S_.ApT.t[:, p, :], in0=S_.K[:, pc(p)], scalar=beta.t[:, t0 + p, hsl],
                                                                                 in1=S_.ET.t[:, pc(p)], op0=ALU.mult, op1=ALU.mult),
                                         reads=[S_.bK, beta, S_.ET], writes=[S_.ApT])
                                if own:
                                    P.op("dve", lambda e: e.tensor_tensor(out=O_.QKm.t[:, 0:W], in0=S_.Q[:, 0:W], in1=S_.ET.t[:, 0:W], op=ALU.mult),
                                         reads=[S_.bQ, S_.ET], writes=[O_.QKm])
                                yield
                                def mask_op(l):
                                    P.op("pool", lambda e: e.tensor_tensor(out=S_.Mt[l].t[:, 0:nb, :], in0=S_.ApT.t[:, 0:nb, :],
                                                                           in1=lvl(l).unsqueeze(1).to_broadcast(bc3), op=ALU.mult),
                                         reads=[S_.ApT, cbf], writes=[S_.Mt[l]])

                                P.op("dve", lambda e: e.tensor_tensor(out=S_.Mt[0].t[:, 0:nb, :], in0=S_.ApT.t[:, 0:nb, :],
                                                                      in1=lvl(0).unsqueeze(1).to_broadcast(bc3), op=ALU.mult),
                                     reads=[S_.ApT, cbf], writes=[S_.Mt[0]])
                                P.op("dve", lambda e: e.tensor_tensor(out=S_.TT[0].t[:, 0:nb, :], in0=identb.unsqueeze(1).to_broadcast(bc3),
                                                                       in1=S_.Mt[0].t[:, 0:nb, :], op=ALU.subtract), reads=[S_.Mt[0], cbf], writes=[S_.TT[0]])
                                mask_op(1)
                                yield
                                for p in range(nb):
                                    P.tr(S_.tp[:, pc(p)], S_.TT[0].t[:, p, :], identb, reads=[S_.TT[0], cbf], writes=[psTB])
                                P.op("act", lambda e: e.activation(out=S_.Tm[0].t[:, 0:nb, :], in_=f3(S_.tp), func=AF.Copy), reads=[psTB], writes=[S_.Tm[0]])
                                yield
                                for l in range(1, 7):
                                    cur, nxt = (l - 1) % 2, l % 2
                                    if l + 1 < 7:
                                        mask_op(l + 1)
                                    for p in range(nb):
                                        P.mm(S_.Y[:, pc(p)], lhsT=S_.Mt[l].t[:, p, :], rhs=S_.Tm[cur].t[:, p, :], start=True, stop=True,
                                             reads=[S_.Mt[l], S_.Tm[cur]], writes=[S_.bY])
                                    yield
                                    P.op("act", lambda e: e.activation(out=S_.Yb.t[:, 0:nb, :], in_=f3(S_.Y), func=AF.Copy), reads=[S_.bY], writes=[S_.Yb])
                                    yield
                                    if l < 6:
                                        for p in range(nb):
                                            P.mm(S_.R[:, pc(p)], lhsT=S_.TT[cur].t[:, p, :], rhs=S_.Yb.t[:, p, :], start=True, stop=True,
                                                 reads=[S_.TT[cur], S_.Yb], writes=[S_.bR])
                                    for p in range(nb):
                                        P.mm(S_.RT[:, pc(p)], lhsT=S_.Yb.t[:, p, :], rhs=S_.TT[cur].t[:, p, :], start=True, stop=True,
                                             reads=[S_.Yb, S_.TT[cur]], writes=[S_.bRT])
                                    yield
                                    if l < 6:
                                        P.op("dve", lambda e: e.tensor_tensor(out=S_.Tm[nxt].t[:, 0:nb, :], in0=S_.Tm[cur].t[:, 0:nb, :], in1=f3(S_.R),
                                                                              op=ALU.subtract), reads=[S_.Tm[cur], S_.bR], writes=[S_.Tm[nxt]])
                                    P.op("dve", lambda e: e.tensor_tensor(out=S_.TT[nxt].t[:, 0:nb, :], in0=S_.TT[cur].t[:, 0:nb, :], in1=f3(S_.RT),
                                                                          op=ALU.subtract), reads=[S_.TT[cur], S_.bRT], writes=[S_.TT[nxt]])
                                    yield
                                TF = S_.TT[0]
                                for p in range(nb):
                                    P.mm(S_.K[:, pc(p)], lhsT=TF.t[:, p, :], rhs=gv.t[:, t0 + p, :], start=True, stop=True, reads=[TF, gv], writes=[S_.bK])
                                for p in range(nb):
                                    P.tr(S_.tp[:, pc(p)], gkT.t[:, tk(p)], identb, reads=[gkT, cbf], writes=[psTB])
                                yield
                                for p in range(nb):
                                    P.op("act", lambda e: e.activation(out=O_.Ubt.t[:, p, :], in_=S_.K[:, pc(p)], func=AF.Copy,
                                                                       scale=beta.t[:, t0 + p, hsl]), reads=[S_.bK, beta], writes=[O_.Ubt])
                                    P.op("act", lambda e: e.activation(out=S_.Kg.t[:, p, :], in_=S_.tp[:, pc(p)], func=AF.Copy,
                                                                       scale=egc.t[:, t0 + p, hsl]), reads=[psTB, egc], writes=[S_.Kg])
                                    P.op("dve", lambda e: e.tensor_scalar(out=O_.Kd.t[:, p, :], in0=S_.tp[:, pc(p)], scalar1=ekd.t[:, t0 + p, hsl],
                                                                          scalar2=None, op0=ALU.mult), reads=[psTB, ekd], writes=[O_.Kd])
                                if own:
                                    P.op("pool", lambda e: e.tensor_tensor(out=S_.dg.t[:, 0:nb, :], in0=identb.unsqueeze(1).to_broadcast(bc3),
                                                                           in1=colb(egc), op=ALU.mult), reads=[egc, cbf], writes=[S_.dg])
                                yield
                                for p in range(nb):
                                    P.mm(S_.Q[:, pc(p)], lhsT=S_.Kg.t[:, p, :], rhs=TF.t[:, p, :], start=True, stop=True, reads=[S_.Kg, TF], writes=[S_.bQ])
                                if own:
                                    P.mm(S_.D[:, 0:W], lhsT=onesb, rhs=S_.dg.t[:, 0:nb, :].rearrange("p t c -> p (t c)"), start=True, stop=True,
                                         reads=[cbf, S_.dg], writes=[S_.bD])
                                yield
                                P.op("act", lambda e: e.activation(out=O_.WpT.t[:, 0:W], in_=S_.Q[:, 0:W], func=AF.Copy), reads=[S_.bQ], writes=[O_.WpT])
                                if own:
                                    P.op("dve", lambda e: e.tensor_tensor(out=O_.QdT.t[:, 0:W], in0=S_.D[:, 0:W], in1=gqT.t[:, qo:qo + W], op=ALU.mult),
                                         reads=[S_.bD, gqT], writes=[O_.QdT])
                                yield

                            def seq_phase(t0, nb, O_):
                                pc = lambda p: slice(p * 128, (p + 1) * 128)
                                for p in range(nb):
                                    t = t0 + p
                                    own = t >= OWN0
                                    P.mm(psq[:, 0:128], lhsT=O_.WpT.t[:, pc(p)], rhs=Sbf.t[:], start=True, stop=True, reads=[O_.WpT, Sbf], writes=[psWS])
                                    P.op("dve", lambda e: e.scalar_tensor_tensor(out=vn.t[:], in0=psq[:, 0:128], scalar=nbeta.t[:, t, hsl],
                                                                                 in1=O_.Ubt.t[:, p, :], op0=ALU.mult, op1=ALU.add),
                                         reads=[psWS, nbeta, O_.Ubt], writes=[vn])
                                    yield
                                    if own:
                                        P.mm(psq[:, 128:256], lhsT=O_.QdT.t[:, pc(p)], rhs=Sbf.t[:], start=True, stop=False, reads=[O_.QdT, Sbf], writes=[psO])
                                        P.mm(psq[:, 128:256], lhsT=O_.QKm.t[:, pc(p)], rhs=vn.t[:], start=False, stop=True, reads=[O_.QKm, vn], writes=[psO])
                                    P.mm(psq[:, 256:384], lhsT=O_.Kd.t[:, p, :], rhs=vn.t[:], start=True, stop=True, reads=[O_.Kd, vn], writes=[psdS])
                                    P.op("dve", lambda e: e.scalar_tensor_tensor(out=St.t[:], in0=St.t[:], scalar=egl.t[:, t, hsl], in1=psq[:, 256:384],
                                                                                 op0=ALU.mult, op1=ALU.add), reads=[St, egl, psdS], writes=[St])
                                    P.op("act", lambda e: e.activation(out=Sbf.t[:], in_=St.t[:], func=AF.Copy), reads=[St], writes=[Sbf])
                                    yield
                                    if own:
                                        oi = (t - OWN0) % 4
                                        P.op("act", lambda e: e.activation(out=oj.t[:], in_=psq[:, 128:256], func=AF.Square, accum_out=oss.t[:, 0:1]),
                                             reads=[psO], writes=[oj, oss])
                                        P.op("act", lambda e: e.activation(out=oss.t[:, 1:2], in_=oss.t[:, 0:1], func=AF.Ln, bias=EPS, scale=1.0 / 128),
                                             reads=[oss], writes=[oss])
                                        P.op("act", lambda e: e.activation(out=oss.t[:, 2:3], in_=oss.t[:, 1:2], func=AF.Exp, scale=-0.5),
                                             reads=[oss], writes=[oss])
                                        P.op("act", lambda e: e.activation(out=on4.t[:, oi, :], in_=psq[:, 128:256], func=AF.Copy, scale=oss.t[:, 2:3]),
                                             reads=[psO, oss], writes=[on4])
                                        if oi == 3:
                                            yield
                                            G = (t - OWN0) // 4
                                            yb = yblk[G]
                                            for hh in range(2):
                                                for p2 in range(2):
                                                    p4 = hh * 2 + p2
                                                    P.tr(psTt[:, 768 + p2 * 128: 768 + (p2 + 1) * 128], on4.t[:, p4, :], identb, reads=[on4, cbf], writes=[psTB])
                                                P.op("dve", lambda e: e.scalar_tensor_tensor(out=yb.t[:, hh * 256:(hh + 1) * 256], in0=psTt[:, 768:1024], scalar=gct.t[:, 0:1],
                                                                                             in1=szT.t[:, G * 512 + hh * 256: G * 512 + (hh + 1) * 256], op0=ALU.mult, op1=ALU.mult),
                                                     reads=[psTB, gct, szT], writes=[yb])
                                            P.dma("sp", ycat[h, :, G * 512:(G + 1) * 512], yb.t[:], reads=[yb])
                                        yield

                            subs = [(0, 1)] + [(1 + 2 * i, 2) for i in range(16)]
                            if not STG["gdn_c"]:
                                subs = []
                            subs = subs[STG.get("sub0", 0):STG.get("sub1", 99)]
                            pairs = [subs[i:i + NSL] for i in range(0, len(subs), NSL)][:STG.get("npairs", 99)]
                            prev = []
                            for pi, pr in enumerate(pairs + [[]]):
                                gens = []
                                cur_out = []
                                for j, (t0, nb) in enumerate(pr):
                                    O_ = outs[(pi % 2) * NSL + j]
                                    gens.append(par_phase(t0, nb, slots[j], O_))
                                    cur_out.append((t0, nb, O_))

                                def seq_all(items):
                                    for (t0_, nb_, O2) in items:
                                        yield from seq_phase(t0_, nb_, O2)

                                sg = seq_all(prev if STG.get("doseq", 1) else [])
                                live = list(gens)
                                sdone = False
                                rnd = 0
                                while live or not sdone:
                                    for g_ in list(live):
                                        if rnd >= STG.get("pstop", 9999):
                                            live.remove(g_)
                                            continue
                                        try:
                                            next(g_)
                                        except StopIteration:
                                            live.remove(g_)
                                    if not sdone and (rnd % 2 == 0 or not live):
                                        try:
                                            next(sg)
                                        except StopIteration:
                                            sdone = True
                                    rnd += 1
                                prev = cur_out
                            if h + 1 < STG["nheads"]:
                                P.dma("pool", wh.t[:], w_head[:, (h + 1) * 896:(h + 2) * 896].rearrange("(kc p) c -> p kc c", p=128), writes=[wh])
                            pref_done[0] = False
                            if h + 1 < STG["nheads"] and STG["gdn_i"]:
                                for bi_ in range(2):
                                    nt_ = BLKS[bi_][1] * 128
                                    P.dma("sp", uTb[bi_].t[:, :, 0:nt_], stash[bi_, :, :, 0:nt_], writes=[uTb[bi_]])
                                pref_done[0] = True
                            psS = [bankB[0], bankB[1], psWS]
                            psO = [bankB[2], bankB[3]]
                            psL = [bankB[4], bankB[5]]
                            PT = [sb(ec, f"PT{i}", [128, 512], BF16) for i in range(4)]
                            rl = sb(ec, "rl", [128, 512], F32)
                            units = []
                            for G in range(2 if STG["fox_c"] else 0):
                                qt0 = OWN0 + 4 * G
                                for kt in range(qt0 + 4):
                                    units.append((G, kt, qt0))

                            def emit_S(u):
                                G, kt, qt0 = units[u]
                                pS_ = psS[u % 3]
                                diag = kt >= qt0
                                P.mm(pS_.t[:, :], lhsT=fkT.t[:, kt * 128:(kt + 1) * 128], rhs=fqT.t[:, G * 512:(G + 1) * 512],
                                     start=True, stop=not diag, reads=[fkT, fqT], writes=[pS_])
                                if diag:
                                    P.mm(pS_.t[:, :], lhsT=identb, rhs=maskG(kt - qt0), start=False, stop=True, reads=[cbf], writes=[pS_])

                            for u0 in range(min(2, len(units))):
                                emit_S(u0)
                            for u in range(len(units)):
                                G, kt, qt0 = units[u]
                                last = qt0 + 3
                                pS_ = psS[u % 3]
                                pt = PT[u % 4]
                                P.op("act", lambda e: e.activation(out=pt.t[:], in_=pS_.t[:, :], func=AF.Exp, bias=fb.t[:, h, G, kt:kt + 1], scale=1.0),
                                     reads=[pS_, fb], writes=[pt])
                                if u + 2 < len(units):
                                    emit_S(u + 2)
                                P.mm(psO[G].t[:, :], lhsT=fvt.t[:, kt, :], rhs=pt.t[:], start=(kt == 0), stop=(kt == last), reads=[fvt, pt], writes=[psO[G]])
                                P.mm(psL[G].t[:, :], lhsT=onesb, rhs=pt.t[:], start=(kt == 0), stop=(kt == last), reads=[cbf, pt], writes=[psL[G]])
                                if kt == last:
                                    P.op("dve", lambda e: e.reciprocal(out=rl.t[:], in_=psL[G].t[:, :]), reads=[psL[G]], writes=[rl])
                                    yb = yblk[G]
                                    P.op("dve", lambda e: e.tensor_tensor(out=yb.t[:], in0=psO[G].t[:, :], in1=rl.t[:], op=ALU.mult), reads=[psO[G], rl], writes=[yb])
                                    P.dma("sp", ycat[16 + h, :, G * 512:(G + 1) * 512], yb.t[:], reads=[yb])
                            P.barrier()
                P.barrier()

            if DEBUG:
                P.dma("pool", dbg[:, 5808:6832], ycat[0, :, :])
                P.dma("pool", dbg[:, 6832:7856], ycat[16, :, :])
                P.barrier()
            if STG["upto"] == "B":
                raise _Stop()
            esB.close()
            with ExitStack() as edd:
                psd = [ps(edd, f"psd{i}", [128, 512], F32) for i in range(6)]
                tpx = [ps(edd, f"tpx{i}", [128, 1024], BF16) for i in range(2)]
                h2dB = Buf("h2dB", None)
                wb = [sb(edd, f"wbD{i}", [128, 16, 512], BF16) for i in range(3)]
                wcnt = [0]

                def wload(src_ap):
                    w = wb[wcnt[0] % 3]
                    wcnt[0] += 1
                    P.dma("pool", w.t[:], src_ap, writes=[w])
                    return w

                kcp = lambda ap: ap.rearrange("(kc p) c -> p kc c", p=128)
                hbufs = [(wb[i].t, hh * 256, Buf(f"wbh{i}{hh}", None)) for i in range(3) for hh in range(2)]
                hcnt = [0]

                def hload(src_ap):
                    tt, off, hb = hbufs[hcnt[0] % 6]
                    hcnt[0] += 1
                    P.dma("pool", tt[:, :, off:off + 256], src_ap, writes=[hb])
                    return tt, off, hb

                u2T = sb(edd, "u2T", [128, 16, 1024], BF16)
                with ExitStack() as e1:
                    mixT = sb(e1, "mixT", [128, 16, 1024], BF16)
                    with ExitStack() as e2:
                        uTo = sb(e2, "uTo", [128, 16, 1024], BF16)
                        for j in range(2):
                            P.dma("sp", uTo.t[:, :, j * 512:(j + 1) * 512], stash[7 + j, :, :, :], writes=[uTo])
                        yT1 = sb(e2, "yT", [128, 16, 1024], BF16)
                        yT = [yT1, yT1]
                        sg = [sb(e2, f"sg{i}", [128, 512], F32) for i in range(2)]
                        mx = [sb(e2, f"mx{i}", [128, 512], F32) for i in range(2)]
                        pk = 0
                        for br in range(2):
                            for j in range(4):
                                P.dma("sp", yT1.t[:, j * 4:(j + 1) * 4, :],
                                      ycat[16 * br + 4 * j: 16 * br + 4 * j + 4, :, :].rearrange("k p t -> p k t"), writes=[yT1])
                            for cb in range(4):
                                hw = {}
                                for hh in range(2):
                                    c0 = cb * 512 + hh * 256
                                    hw[("g", hh)] = hload(kcp(w_gate[:, br * 2048 + c0: br * 2048 + c0 + 256]))
                                    hw[("o", hh)] = hload(kcp((w_og if br == 0 else w_of)[:, c0:c0 + 256]))
                                for cc_ in range(4):
                                    ch = cb * 4 + cc_
                                    wgt, wgo, wg = hw[("g", cc_ // 2)]
                                    wot, woo, wo = hw[("o", cc_ // 2)]
                                    gsl = slice(wgo + (cc_ % 2) * 128, wgo + (cc_ % 2 + 1) * 128)
                                    osl = slice(woo + (cc_ % 2) * 128, woo + (cc_ % 2 + 1) * 128)
                                    for hf in range(2):
                                        tsl = slice(hf * 512, (hf + 1) * 512)
                                        pgt = psd[pk % 6]
                                        ppt = psd[(pk + 1) % 6]
                                        pk += 2
                                        for kc in range(16):
                                            P.mm(pgt.t[:, :], lhsT=wgt[:, kc, gsl], rhs=uTo.t[:, kc, tsl],
                                                 start=(kc == 0), stop=(kc == 15), reads=[wg, uTo], writes=[pgt])
                                        for kc in range(16):
                                            P.mm(ppt.t[:, :], lhsT=wot[:, kc, osl], rhs=yT[br].t[:, kc, tsl],
                                                 start=(kc == 0), stop=(kc == 15), reads=[wo, yT[br]], writes=[ppt])
                                        s_ = sg[hf]
                                        P.op("act", lambda e: e.activation(out=s_.t[:], in_=pgt.t[:, :], func=AF.Exp, scale=-1.0), reads=[pgt], writes=[s_])
                                        P.op("dve", lambda e: e.tensor_scalar(out=s_.t[:], in0=s_.t[:], scalar1=1.0, scalar2=None, op0=ALU.add), reads=[s_], writes=[s_])
                                        P.op("dve", lambda e: e.reciprocal(out=s_.t[:], in_=s_.t[:]), reads=[s_], writes=[s_])
                                        if br == 0:
                                            P.op("dve", lambda e: e.tensor_tensor(out=mixT.t[:, ch, tsl], in0=ppt.t[:, :], in1=s_.t[:], op=ALU.mult),
                                                 reads=[ppt, s_], writes=[mixT])
                                        else:
                                            m_ = mx[hf]
                                            P.op("dve", lambda e: e.tensor_tensor(out=m_.t[:], in0=ppt.t[:, :], in1=s_.t[:], op=ALU.mult),
                                                 reads=[ppt, s_], writes=[m_])
                                            P.op("pool", lambda e: e.tensor_tensor(out=mixT.t[:, ch, tsl], in0=mixT.t[:, ch, tsl], in1=m_.t[:], op=ALU.add),
                                                 reads=[mixT, m_], writes=[mixT])
                        P.barrier()
                    with ExitStack() as e3:
                        gv1 = sb(e3, "gv1", [128, 2048], F32)
                        P.dma("sp", gv1.t[:], gvec[:, 2048:4096], writes=[gv1])
                        h2t = sb(e3, "h2t", [128, 8, 2048], F32)
                        for ti in range(8):
                            P.dma("sp", h2t.t[:, ti, :], xs[(OWN0 + ti) * 128:(OWN0 + ti + 1) * 128, :], writes=[h2t])
                        pk = 0
                        for cb in range(4):
                            wo = wload(kcp(w_out[:, cb * 512:(cb + 1) * 512]))
                            for ti in range(8):
                                pp = psd[pk % 6]
                                pk += 1
                                for kc in range(16):
                                    P.mm(pp.t[:, :], lhsT=mixT.t[:, kc, ti * 128:(ti + 1) * 128], rhs=wo.t[:, kc, :],
                                         start=(kc == 0), stop=(kc == 15), reads=[mixT, wo], writes=[pp])
                                P.op("dve", lambda e: e.tensor_tensor(out=h2t.t[:, ti, cb * 512:(cb + 1) * 512], in0=pp.t[:, :],
                                                                      in1=h2t.t[:, ti, cb * 512:(cb + 1) * 512], op=ALU.add), reads=[pp, h2t], writes=[h2t])
                        sq2 = sb(e3, "sq2", [128, 2048], BF16)
                        u2 = [sb(e3, f"u2{i}", [128, 2048], BF16) for i in range(2)]
                        ss2 = [sb(e3, f"ss2{i}", [128, 4], F32) for i in range(2)]
                        def d3_stage1(ti):
                            ss = ss2[ti % 2]
                            P.dma("sp", h2d[ti, :, :], h2t.t[:, ti, :], reads=[h2t])
                            P.op("act", lambda e: e.activation(out=sq2.t[:], in_=h2t.t[:, ti, :], func=AF.Square, accum_out=ss.t[:, 0:1]),
                                 reads=[h2t], writes=[sq2, ss])
                            P.op("act", lambda e: e.activation(out=ss.t[:, 1:2], in_=ss.t[:, 0:1], func=AF.Ln, bias=EPS, scale=1.0 / 2048), reads=[ss], writes=[ss])
                            P.op("act", lambda e: e.activation(out=ss.t[:, 2:3], in_=ss.t[:, 1:2], func=AF.Exp, scale=-0.5), reads=[ss], writes=[ss])

                        def d3_stage2(ti):
                            ss = ss2[ti % 2]
                            uu = u2[ti % 2]
                            P.op("dve", lambda e: e.scalar_tensor_tensor(out=uu.t[:], in0=h2t.t[:, ti, :], scalar=ss.t[:, 2:3], in1=gv1.t[:],
                                                                         op0=ALU.mult, op1=ALU.mult), reads=[h2t, ss, gv1], writes=[uu])
                            for half in range(2):
                                pb_ = tb_ = tpx[half]
                                for k8 in range(8):
                                    kc = half * 8 + k8
                                    P.tr(tb_.t[:, k8 * 128:(k8 + 1) * 128], uu.t[:, kc * 128:(kc + 1) * 128], identb, reads=[uu, cbf], writes=[pb_])
                                P.op("act" if half == 0 else "dve",
                                     (lambda e: e.activation(out=u2T.t[:, 0:8, ti * 128:(ti + 1) * 128],
                                                             in_=tb_.t[:, :].rearrange("p (k c) -> p k c", k=8), func=AF.Copy)) if half == 0 else
                                     (lambda e: e.tensor_copy(out=u2T.t[:, 8:16, ti * 128:(ti + 1) * 128],
                                                              in_=tb_.t[:, :].rearrange("p (k c) -> p k c", k=8))),
                                     reads=[pb_], writes=[u2T])

                        d3_stage1(0)
                        for ti in range(8):
                            if ti + 1 < 8:
                                d3_stage1(ti + 1)
                            d3_stage2(ti)
                        P.barrier()
                with ExitStack() as e4:
                    gv2 = sb(e4, "gv2", [128, 2048], F32)
                    P.dma("sp", gv2.t[:], gvec[:, 4096:6144], writes=[gv2])
                    h3a = [sb(e4, f"h3a{i}", [128, 2048], F32) for i in range(8)]
                    for ti in range(8):
                        P.dma("sp", h3a[ti].t[:], h2d[ti, :, :], writes=[h3a[ti]])
                    ab = [sb(e4, f"ab{i}", [128, 4, 1024], BF16) for i in range(2)]
                    rl_ = [sb(e4, f"rlu{i}", [128, 512], F32) for i in range(2)]
                    sq3 = sb(e4, "sq3", [128, 2048], BF16)
                    ss3 = [sb(e4, f"ss3{i}", [128, 4], F32) for i in range(2)]
                    pk = 0
                    pd = 0
                    for fbk in range(16):
                        wu = wload(kcp(w_up[:, fbk * 512:(fbk + 1) * 512]))
                        wdt = wload(w_down[fbk * 512:(fbk + 1) * 512, :].rearrange("(fc p) c -> p fc c", p=128))
                        wd = wdt.t[:].rearrange("p a b -> p (a b)").rearrange("p (f c) -> p f c", f=4)
                        a_ = ab[fbk % 2]
                        for cc_ in range(4):
                            for hf in range(2):
                                pp = psd[4 + pk % 2]
                                r_ = rl_[pk % 2]
                                pk += 1
                                for kc in range(16):
                                    P.mm(pp.t[:, :], lhsT=wu.t[:, kc, cc_ * 128:(cc_ + 1) * 128], rhs=u2T.t[:, kc, hf * 512:(hf + 1) * 512],
                                         start=(kc == 0), stop=(kc == 15), reads=[wu, u2T], writes=[pp])
                                P.op("act", lambda e: e.activation(out=r_.t[:], in_=pp.t[:, :], func=AF.Relu), reads=[pp], writes=[r_])
                                P.op("pool", lambda e: e.tensor_tensor(out=a_.t[:, cc_, hf * 512:(hf + 1) * 512], in0=r_.t[:], in1=r_.t[:], op=ALU.mult),
                                     reads=[r_], writes=[a_])
                        for ti in range(8):
                            for cb in range(4):
                                pp = psd[pd % 4]
                                pd += 1
                                for fc in range(4):
                                    P.mm(pp.t[:, :], lhsT=a_.t[:, fc, ti * 128:(ti + 1) * 128], rhs=wd[:, fc, cb * 512:(cb + 1) * 512],
                                         start=(fc == 0), stop=(fc == 3), reads=[a_, wdt], writes=[pp])
                                P.op("dve", lambda e: e.tensor_tensor(out=h3a[ti].t[:, cb * 512:(cb + 1) * 512], in0=pp.t[:, :],
                                                                      in1=h3a[ti].t[:, cb * 512:(cb + 1) * 512], op=ALU.add),
                                     reads=[pp, h3a[ti]], writes=[h3a[ti]])
                    for ti in range(8):
                        hb_ = h3a[ti]
                        ss = ss3[ti % 2]
                        P.op("act", lambda e: e.activation(out=sq3.t[:], in_=hb_.t[:], func=AF.Square, accum_out=ss.t[:, 0:1]),
                             reads=[hb_], writes=[sq3, ss])
                        P.op("act", lambda e: e.activation(out=ss.t[:, 1:2], in_=ss.t[:, 0:1], func=AF.Ln, bias=EPS, scale=1.0 / 2048), reads=[ss], writes=[ss])
                        P.op("act", lambda e: e.activation(out=ss.t[:, 2:3], in_=ss.t[:, 1:2], func=AF.Exp, scale=-0.5), reads=[ss], writes=[ss])
                        P.op("dve", lambda e: e.scalar_tensor_tensor(out=hb_.t[:], in0=hb_.t[:], scalar=ss.t[:, 2:3], in1=gv2.t[:],
                                                                     op0=ALU.mult, op1=ALU.mult), reads=[hb_, ss, gv2], writes=[hb_])
                        P.dma("sp", out[ti * 128:(ti + 1) * 128, :], hb_.t[:], reads=[hb_])
                    P.barrier()
        except _Stop:
            esB.close()
        P.barrier()
    return nc, P


_CACHE = {}


def _consts():
    p = np.arange(128)[:, None]
    f = np.arange(128)[None, :]
    ident = (p == f).astype(np.float32)
    ones = np.ones((128, 128), np.float32)
    maskT = np.where(f >= p, 0.0, NEG).astype(np.float32)
    q = np.arange(512)[None, :]
    maskG = [np.where(q - 128 * r - p >= 0, 0.0, NEG).astype(np.float32) for r in range(4)]
    lv = []
    for l in range(7):
        s_ = 1 << l
        j, i = p, f
        m = ((i // (2 * s_)) == (j // (2 * s_))) & ((i % (2 * s_)) >= s_) & ((j % (2 * s_)) < s_)
        lv.append(m.astype(np.float32))
    cbf = np.concatenate([ident, ones] + [maskT] * 4 + maskG + lv, axis=1)
    utri = (p <= f).astype(np.float32)
    slow = (p > f).astype(np.float32)
    cf = np.concatenate([ident, utri, slow, ones], axis=1)
    return np.ascontiguousarray(cbf), np.ascontiguousarray(cf)


def kernel(x, meta_tokens, mix_norm_g, w_in, conv_w, a_log, dt_bias, gdn_norm_g, w_o_gdn,
           fox_q_norm_g, fox_k_norm_g, fox_f_bias, w_o_fox, w_out, mlp_norm_g, w_up, w_down, final_norm_g):
    f32 = np.float32
    x = np.asarray(x, f32)
    w_in0 = np.asarray(w_in, f32)[0]
    if "nc" not in _CACHE:
        _, P1 = build(None)
        need = set(P1.used)
        nc, _ = build(need)
        _CACHE["nc"] = nc
    nc = _CACHE["nc"]
    GQ, GZ, GB, GA, FQ, FF, GTA = 0, 6144, 8192, 8208, 8224, 14368, 14384
    cols = []
    for h in range(NH):
        sl = lambda base: w_in0[:, base + h * 128: base + (h + 1) * 128]
        cols += [sl(GQ), sl(GQ + 2048), sl(GQ + 4096), sl(GZ), sl(FQ), sl(FQ + 2048), sl(FQ + 4096)]
    w_head = np.ascontiguousarray(np.concatenate(cols, axis=1))
    w_small = np.ascontiguousarray(np.concatenate([w_in0[:, GB:GB + 16], w_in0[:, GA:GA + 16], w_in0[:, FF:FF + 16]], axis=1))
    w_gate = np.ascontiguousarray(w_in0[:, GTA:GTA + 4096])
    cw = np.asarray(conv_w, f32)[0]
    cwl = np.zeros((128, NH, 3, 4), f32)
    for h in range(NH):
        for c in range(3):
            cwl[:, h, c, :] = cw[:, c * 2048 + h * 128: c * 2048 + (h + 1) * 128].T
    convw = np.ascontiguousarray(cwl.reshape(128, NH * 12))
    hp = np.ascontiguousarray(np.tile(np.concatenate([np.asarray(a_log, f32)[0], np.asarray(dt_bias, f32)[0],
                                                      np.asarray(fox_f_bias, f32)[0]])[None, :], (128, 1)))
    gcol = np.ascontiguousarray(np.stack([np.asarray(gdn_norm_g, f32)[0], np.asarray(fox_q_norm_g, f32)[0],
                                          np.asarray(fox_k_norm_g, f32)[0]], axis=1))
    gvec = np.ascontiguousarray(np.tile(np.concatenate([np.asarray(mix_norm_g, f32)[0], np.asarray(mlp_norm_g, f32)[0],
                                                        np.asarray(final_norm_g, f32)])[None, :], (128, 1)))
    cbf, cf = _consts()
    shared = {"w_head": w_head, "w_small": w_small, "w_gate": w_gate, "convw": convw, "hp": hp, "gcol": gcol,
              "gvec": gvec, "w_og": np.ascontiguousarray(np.asarray(w_o_gdn, f32)[0]),
              "w_of": np.ascontiguousarray(np.asarray(w_o_fox, f32)[0]), "w_out": np.ascontiguousarray(np.asarray(w_out, f32)[0]),
              "w_up": np.ascontiguousarray(np.asarray(w_up, f32)[0]), "w_down": np.ascontiguousarray(np.asarray(w_down, f32)[0]),
              "cbf": cbf, "cf": cf}
    meta = np.asarray(meta_tokens, f32)
    in_maps = []
    for c in range(8):
        b, tq = c // 4, c % 4
        nreal = 16 + (tq + 1) * 1024
        xs = np.zeros((NT * 128, 2048), f32)
        xs[NT * 128 - nreal: NT * 128 - nreal + 16] = meta
        xs[NT * 128 - nreal + 16:] = x[b, :(tq + 1) * 1024]
        vmk = np.zeros((NT * 128,), f32)
        vmk[NT * 128 - nreal:] = 1.0
        m = dict(shared)
        m["xs"] = xs
        m["vmask"] = np.ascontiguousarray(vmk.reshape(NT, 128).T)
        in_maps.append(m)
    if _CACHE.get("dbg_cores"):
        cs_ = _CACHE["dbg_cores"]
        return run_bass_kernel_spmd(nc, [in_maps[c] for c in cs_], core_ids=list(range(len(cs_))), trace=bool(_CACHE.get("trace")))
    res = run_bass_kernel_spmd(nc, in_maps, core_ids=list(range(8)))
    _CACHE["res"] = res
    outp = np.zeros((2, 4096, 2048), f32)
    for c in range(8):
        b, tq = c // 4, c % 4
        outp[b, tq * 1024:(tq + 1) * 1024] = np.asarray(res.results[c]["out"], f32)
    return outp
```
